# Optimizing a Trainium2 kernel written in Bass

```python
import jax, jax.numpy as jnp
from jax import lax
import numpy as np

D_MODEL = 1024
BATCH = 16
SEQ = 256
DEPTH = 2
DEC_BATCH = 4
DEC_SEQ = 4096
PAST_LEN = 512

GRID_W = 64
N_EVEN = (DEPTH + 1) // 2
N_ODD = DEPTH // 2
EPS = 1e-6
POOL_WIDTH = D_MODEL // 2
POOL_GROUPS = 4
POOL_GROUP_DIM = POOL_WIDTH // POOL_GROUPS
POOL_WINDOWS = (2, 4, 8, 16)
CONV_WIDTH = D_MODEL // 2
CONV_K = 3
EVEN_IN = POOL_WIDTH + 3 * CONV_WIDTH
EVEN_OUT = POOL_WIDTH + CONV_WIDTH
HEAD_DIM = 128
N_HEADS = D_MODEL // HEAD_DIM
N_KV_HEADS = N_HEADS // 4
QKV_WIDTH = (N_HEADS + 2 * N_KV_HEADS) * HEAD_DIM
Q_BLOCK = 128
ROPE_THETA = 10000.0
AXIS_DIM = HEAD_DIM // 2
D_FF = ((8 * D_MODEL // 3 + 255) // 256) * 256

kernel_name = "hybrid_pool_conv_gqa_diffusion_step"


def rms_norm(x, g):
    xf = x.astype(jnp.float32)
    y = xf * lax.rsqrt(jnp.mean(xf * xf, axis=-1, keepdims=True) + EPS)
    return (y * g.astype(jnp.float32)).astype(x.dtype)


def ada_params(cvec, w_ada, b_ada):
    m = jax.nn.silu(cvec) @ w_ada + b_ada
    return jnp.split(m[:, None, :], 6, axis=-1)


def modulate(x, g, shift, scale):
    return rms_norm(x, g) * (1 + scale) + shift


def window_mean(x, w):
    L = x.shape[1]
    cs = jnp.pad(jnp.cumsum(x.astype(jnp.float32), axis=1), ((0, 0), (1, 0), (0, 0)))
    t = jnp.arange(L)
    lo = jnp.clip(t - w // 2, 0, L)
    hi = jnp.clip(t + w // 2, 0, L)
    cnt = (hi - lo).astype(jnp.float32)[None, :, None]
    return ((cs[:, hi] - cs[:, lo]) / cnt).astype(x.dtype)


def pool_mixer(u, w_grp, scale):
    B, L, _ = u.shape
    ug = u.reshape(B, L, POOL_GROUPS, POOL_GROUP_DIM)
    pooled = jnp.stack([window_mean(ug[:, :, i], w) for i, w in enumerate(POOL_WINDOWS)], axis=2) - ug
    y = jnp.einsum('blgc,gcd->blgd', pooled, w_grp).reshape(B, L, POOL_WIDTH)
    return y * scale


def short_conv_mixer(bg, cg, v, conv_w, conv_b):
    z = cg * v
    L = z.shape[1]
    zp = jnp.pad(z, ((0, 0), (CONV_K // 2, CONV_K // 2), (0, 0)))
    conv = sum(zp[:, k:k + L] * conv_w[k] for k in range(CONV_K)) + conv_b
    return bg * conv


def even_mixer(h, w_in, pool_w, pool_scale, conv_w, conv_b, w_out):
    p = h @ w_in
    u, bg, cg, v = jnp.split(p, [POOL_WIDTH, POOL_WIDTH + CONV_WIDTH, POOL_WIDTH + 2 * CONV_WIDTH], axis=-1)
    ya = pool_mixer(u, pool_w, pool_scale)
    yb = short_conv_mixer(bg, cg, v, conv_w, conv_b)
    return jnp.concatenate([ya, yb], axis=-1) @ w_out


def rope_2d_tables(L):
    rows = L // GRID_W
    r = jnp.repeat(jnp.arange(rows), GRID_W).astype(jnp.float32)
    col = jnp.tile(jnp.arange(GRID_W), rows).astype(jnp.float32)
    inv = ROPE_THETA ** (-jnp.arange(0, AXIS_DIM, 2, dtype=jnp.float32) / AXIS_DIM)
    ar = r[:, None] * inv
    ac = col[:, None] * inv
    ang = jnp.concatenate([ar, ar, ac, ac], axis=-1)
    return jnp.cos(ang), jnp.sin(ang)


def _rot_half(a):
    a1, a2 = jnp.split(a, 2, axis=-1)
    return jnp.concatenate([-a2, a1], axis=-1)


def apply_rope_2d(x, cos, sin):
    xr, xc = jnp.split(x, 2, axis=-1)
    xrot = jnp.concatenate([_rot_half(xr), _rot_half(xc)], axis=-1)
    return (x * cos[None, :, None, :] + xrot * sin[None, :, None, :]).astype(x.dtype)


def qkv_proj(h, w_qkv, q_gain, k_gain):
    B, L, _ = h.shape
    p = h @ w_qkv
    q, k, v = jnp.split(p, [N_HEADS * HEAD_DIM, (N_HEADS + N_KV_HEADS) * HEAD_DIM], axis=-1)
    q = rms_norm(q.reshape(B, L, N_HEADS, HEAD_DIM), q_gain)
    k = rms_norm(k.reshape(B, L, N_KV_HEADS, HEAD_DIM), k_gain)
    v = v.reshape(B, L, N_KV_HEADS, HEAD_DIM)
    return q, k, v


def blocked_attention(q, k, v):
    B, Lq, _, _ = q.shape
    nb = Lq // Q_BLOCK
    G = N_HEADS // N_KV_HEADS
    qb = q.reshape(B, nb, Q_BLOCK, N_KV_HEADS, G, HEAD_DIM).transpose(1, 0, 2, 3, 4, 5)
    sm_scale = HEAD_DIM ** -0.5

    def one_block(qi):
        s = jnp.einsum('bqkgd,bskd->bkgqs', qi, k, preferred_element_type=jnp.float32) * sm_scale
        p = jax.nn.softmax(s, axis=-1).astype(v.dtype)
        return jnp.einsum('bkgqs,bskd->bqkgd', p, v)

    o = lax.map(one_block, qb)
    return o.transpose(1, 0, 2, 3, 4, 5).reshape(B, Lq, N_HEADS * HEAD_DIM)


def swiglu(h, w_in, w_out):
    a, b = jnp.split(h @ w_in, 2, axis=-1)
    return (jax.nn.silu(a) * b) @ w_out


def setup_inputs(seed: int = 0) -> dict:
    key = jax.random.key(seed)
    ks = jax.random.split(key, 24)
    nrm = lambda k, shape, s: jax.random.normal(k, shape, jnp.float32) * s
    return {
        "x_prompt": nrm(ks[0], (BATCH, SEQ, D_MODEL), 1.0),
        "x_sample": nrm(ks[1], (DEC_BATCH, DEC_SEQ, D_MODEL), 1.0),
        "cache_k": nrm(ks[2], (DEC_BATCH, N_ODD, PAST_LEN, N_KV_HEADS, HEAD_DIM), 1.0),
        "cache_v": nrm(ks[3], (DEC_BATCH, N_ODD, PAST_LEN, N_KV_HEADS, HEAD_DIM), 1.0),
        "c": nrm(ks[4], (DEC_BATCH, D_MODEL), 1.0),
        "c_ctx": nrm(ks[5], (D_MODEL,), 1.0),
        "w_ada": nrm(ks[6], (DEPTH, D_MODEL, 6 * D_MODEL), 0.5 * D_MODEL ** -0.5),
        "b_ada": nrm(ks[7], (DEPTH, 6 * D_MODEL), 0.02),
        "norm_g": 1.0 + nrm(ks[8], (DEPTH, 2, D_MODEL), 0.02),
        "w_in_even": nrm(ks[9], (N_EVEN, D_MODEL, EVEN_IN), D_MODEL ** -0.5),
        "pool_w": nrm(ks[10], (N_EVEN, POOL_GROUPS, POOL_GROUP_DIM, POOL_GROUP_DIM), POOL_GROUP_DIM ** -0.5),
        "pool_scale": 1.0 + nrm(ks[11], (N_EVEN, POOL_WIDTH), 0.02),
        "conv_w": nrm(ks[12], (N_EVEN, CONV_K, CONV_WIDTH), CONV_K ** -0.5),
        "conv_b": nrm(ks[13], (N_EVEN, CONV_WIDTH), 0.02),
        "w_out_even": nrm(ks[14], (N_EVEN, EVEN_OUT, D_MODEL), EVEN_OUT ** -0.5),
        "w_qkv": nrm(ks[15], (N_ODD, D_MODEL, QKV_WIDTH), D_MODEL ** -0.5),
        "q_gain": 1.0 + nrm(ks[16], (N_ODD, HEAD_DIM), 0.02),
        "k_gain": 1.0 + nrm(ks[17], (N_ODD, HEAD_DIM), 0.02),
        "w_o": nrm(ks[18], (N_ODD, N_HEADS * HEAD_DIM, D_MODEL), (N_HEADS * HEAD_DIM) ** -0.5),
        "w_ffn_in": nrm(ks[19], (DEPTH, D_MODEL, 2 * D_FF), D_MODEL ** -0.5),
        "w_ffn_out": nrm(ks[20], (DEPTH, D_FF, D_MODEL), D_FF ** -0.5),
        "final_g": 1.0 + nrm(ks[21], (D_MODEL,), 0.02),
    }


def reference(x_prompt, x_sample, cache_k, cache_v, c, c_ctx, w_ada, b_ada, norm_g,
              w_in_even, pool_w, pool_scale, conv_w, conv_b, w_out_even,
              w_qkv, q_gain, k_gain, w_o, w_ffn_in, w_ffn_out, final_g):
    x = x_prompt
    ctx_cond = c_ctx[None, :]
    new_k, new_v = [], []
    for l in range(DEPTH):
        i = l // 2
        sh1, sc1, g1, sh2, sc2, g2 = ada_params(ctx_cond, w_ada[l], b_ada[l])
        h = modulate(x, norm_g[l, 0], sh1, sc1)
        if l % 2 == 0:
            y = even_mixer(h, w_in_even[i], pool_w[i], pool_scale[i], conv_w[i], conv_b[i], w_out_even[i])
        else:
            q, k, v = qkv_proj(h, w_qkv[i], q_gain[i], k_gain[i])
            y = blocked_attention(q, k, v) @ w_o[i]
            new_k.append(k)
            new_v.append(v)
        x = x + g1 * y
        h = modulate(x, norm_g[l, 1], sh2, sc2)
        x = x + g2 * swiglu(h, w_ffn_in[l], w_ffn_out[l])
    y_prompt = rms_norm(x, final_g)
    new_cache_k = jnp.stack(new_k, axis=1)
    new_cache_v = jnp.stack(new_v, axis=1)

    x = x_sample
    cos, sin = rope_2d_tables(x_sample.shape[1])
    for l in range(DEPTH):
        i = l // 2
        sh1, sc1, g1, sh2, sc2, g2 = ada_params(c, w_ada[l], b_ada[l])
        h = modulate(x, norm_g[l, 0], sh1, sc1)
        if l % 2 == 0:
            y = even_mixer(h, w_in_even[i], pool_w[i], pool_scale[i], conv_w[i], conv_b[i], w_out_even[i])
        else:
            q, k, v = qkv_proj(h, w_qkv[i], q_gain[i], k_gain[i])
            q = apply_rope_2d(q, cos, sin)
            k = apply_rope_2d(k, cos, sin)
            k_all = jnp.concatenate([cache_k[:, i].astype(k.dtype), k], axis=1)
            v_all = jnp.concatenate([cache_v[:, i].astype(v.dtype), v], axis=1)
            y = blocked_attention(q, k_all, v_all) @ w_o[i]
        x = x + g1 * y
        h = modulate(x, norm_g[l, 1], sh2, sc2)
        x = x + g2 * swiglu(h, w_ffn_in[l], w_ffn_out[l])
    y_sample = rms_norm(x, final_g)

    return (y_prompt, y_sample, new_cache_k, new_cache_v)
```

```python
import contextlib
import numpy as np
import concourse.bass as bass
import concourse.mybir as mybir
from concourse.bass_utils import run_bass_kernel_spmd

F32 = mybir.dt.float32
BF16 = mybir.dt.bfloat16
ALU = mybir.AluOpType
AF = mybir.ActivationFunctionType

ENGS = ["pe", "act", "dve", "pool", "sp"]
D = 1024
DFF = 2816
NT = 2560
EPS = 1e-6


class Op:
    __slots__ = ("eng", "fn", "deps", "signal", "sig_idx", "dma_key", "dma_ord", "pos", "inc")

    def __init__(self, eng, fn, dma_key, inc):
        self.eng = eng
        self.fn = fn
        self.deps = set()
        self.signal = False
        self.sig_idx = 0
        self.dma_key = dma_key
        self.dma_ord = 0
        self.pos = 0
        self.inc = inc


class Sched:
    def __init__(self, nc):
        self.nc = nc
        self.ops = {e: [] for e in ENGS}
        self.last_w = {}
        self.readers = {}
        self.dma_count = {}
        self.dma_last = {}

    def add(self, eng, fn, reads=(), writes=(), excl=(), dma_key=None, inc=16, extra=()):
        op = Op(eng, fn, dma_key, inc)
        writes = list(writes) + list(excl)
        deps = set(extra)
        for k in reads:
            w = self.last_w.get(k)
            if w is not None:
                deps.add(w)
        for k in writes:
            w = self.last_w.get(k)
            if w is not None:
                deps.add(w)
            for r in self.readers.get(k, ()):
                deps.add(r)
        op.deps = deps
        for k in writes:
            self.last_w[k] = op
            self.readers[k] = []
        for k in reads:
            if self.last_w.get(k) is not op:
                self.readers.setdefault(k, []).append(op)
        op.pos = len(self.ops[eng])
        self.ops[eng].append(op)
        if dma_key is not None:
            c = self.dma_count.get(dma_key, 0) + 1
            self.dma_count[dma_key] = c
            op.dma_ord = c
            self.dma_last[dma_key] = op
        return op

    def barrier(self, skip_dma=None):
        lasts = [self.ops[e][-1] for e in ENGS if self.ops[e]]
        lasts = [o for o in lasts if not (o.dma_key is not None and skip_dma is not None and skip_dma(o.dma_key))]
        lasts += [o for k, o in self.dma_last.items() if not (skip_dma is not None and skip_dma(k))]
        for e in ENGS:
            self.add(e, None, extra=lasts)

    def emit(self, final_wait_eng="sp"):
        nc = self.nc
        need = {}
        for e in ENGS:
            for op in self.ops[e]:
                best = {}
                dmabest = {}
                for d in op.deps:
                    if d is op:
                        continue
                    if d.dma_key is not None:
                        b = dmabest.get(d.dma_key)
                        if b is None or d.dma_ord > b.dma_ord:
                            dmabest[d.dma_key] = d
                        continue
                    if d.eng == op.eng:
                        if d.eng == "pe":
                            continue
                        if op.pos - d.pos > 2:
                            continue
                    b = best.get(d.eng)
                    if b is None or d.pos > b.pos:
                        best[d.eng] = d
                lst = list(best.values()) + list(dmabest.values())
                for d in lst:
                    d.signal = True
                need[id(op)] = lst
        for e in ENGS:
            k = 0
            for op in self.ops[e]:
                if op.dma_key is None and op.signal:
                    assert op.fn is not None
                    k += 1
                    op.sig_idx = k
        stack = contextlib.ExitStack()
        eng_sem = {e: stack.enter_context(nc.semaphore("s_" + e)) for e in ENGS}
        dma_sem = {}
        dma_inc = {}
        for e in ENGS:
            for op in self.ops[e]:
                if op.dma_key is not None and op.dma_key not in dma_sem:
                    dma_sem[op.dma_key] = stack.enter_context(nc.semaphore("d%d" % len(dma_sem)))
                    dma_inc[op.dma_key] = op.inc

        def emit_engine(eng_name, eng):
            waited = {}
            for op in self.ops[eng_name]:
                for d in need[id(op)]:
                    if d.dma_key is not None:
                        sem = dma_sem[d.dma_key]
                        val = d.dma_ord * dma_inc[d.dma_key]
                        key = ("d", d.dma_key)
                    else:
                        sem = eng_sem[d.eng]
                        val = d.sig_idx
                        key = ("e", d.eng)
                    if waited.get(key, 0) >= val:
                        continue
                    waited[key] = val
                    eng.wait_ge(sem, val)
                if op.fn is None:
                    continue
                ins = op.fn(eng)
                if op.dma_key is not None:
                    ins.then_inc(dma_sem[op.dma_key], dma_inc[op.dma_key])
                elif op.signal:
                    ins.then_inc(eng_sem[eng_name], 1)
            if eng_name == final_wait_eng:
                for k, sem in dma_sem.items():
                    eng.wait_ge(sem, self.dma_count[k] * dma_inc[k])

        with stack:
            with nc.Block() as block:

                @block.tensor
                def _(eng):
                    emit_engine("pe", eng)

                @block.scalar
                def _(eng):
                    emit_engine("act", eng)

                @block.vector
                def _(eng):
                    emit_engine("dve", eng)

                @block.gpsimd
                def _(eng):
                    emit_engine("pool", eng)

                @block.sync
                def _(eng):
                    emit_engine("sp", eng)


P_CVEC = 0
P_BADA = 16
P_NG = 112
P_PSC = 144
P_CW = 148
P_CB = 160
P_QG = 164
P_KG = 165
P_FG = 166
P_CV5 = 176
NP = 216


def build_program():
    nc = bass.Bass("TRN2", target_bir_lowering=False)
    S = Sched(nc)

    def din(name, shape, dt=F32):
        return nc.dram_tensor(name, list(shape), dt, kind="ExternalInput")

    def dout(name, shape, dt=F32):
        return nc.dram_tensor(name, list(shape), dt, kind="ExternalOutput")

    xt = din("xt", [NT, D])
    xh = din("xh", [64, D])
    params = din("params", [128, NP])
    consts = din("consts", [128, 258])
    mh_d = din("mh", [128, 64])
    rce_d = din("rce", [128, 640])
    cs_d = din("cs", [128, 4, 2, 512])
    ck_d = din("ck", [512, 256])
    cv_d = din("cv", [512, 256])
    wada_sh = din("wada_sh", [D, 6144])
    w_in_even = din("w_in_even", [D, 2048])
    pool_w = din("pool_w", [4, 128, 128])
    w_out_even = din("w_out_even", [D, D])
    w_qkv = din("w_qkv", [D, 1536])
    w_o = din("w_o", [D, D])
    w_ffn_in = din("w_ffn_in", [2, D, 2 * DFF])
    w_ffn_out = din("w_ffn_out", [2, DFF, D])
    y_d = dout("y", [NT, D])
    nk_d = dout("nk", [512, 256])
    nv_d = dout("nv", [512, 256])

    def dscr(name, shape, dt=BF16):
        return nc.dram_tensor(name, list(shape), dt)

    sc_win = dscr("sc_win", [D, 2048])
    sc_wout = dscr("sc_wout", [D, D])
    sc_pool = dscr("sc_pool", [512, 128])
    sc_qkv = dscr("sc_qkv", [D, 1536])
    sc_wo = dscr("sc_wo", [D, D])
    sc_fin = [dscr("sc_fin%d" % l, [D, 2 * DFF]) for l in range(2)]
    sc_fout = [dscr("sc_fout%d" % l, [DFF, D]) for l in range(2)]
    ag_in = dscr("ag_in", [2, 6144], F32)
    ag_out = dscr("ag_out", [4, 6144], F32)
    kvx_in = dscr("kvx_in", [128, 8192])
    kvx_out = dscr("kvx_out", [256, 8192])

    off = [16512]
    OFF = {}

    def sb(name, shape, dt, at=None):
        n = int(np.prod(shape[1:])) * (4 if dt == F32 else 2)
        if at is None:
            o = off[0]
            off[0] = (o + n + 31) // 32 * 32
        else:
            o = at
        OFF[name] = o
        return nc.alloc_sbuf_tensor_at(name, list(shape), dt, offset=o)

    X = sb("X", [128, 8, 2048], F32)
    XH = sb("XH", [128, 8, 64], F32)
    PRM = sb("PRM", [128, NP], F32)
    ADA = sb("ADA", [128, 2, 48, 2], F32)
    AM = sb("AM", [128, 2, 2, 8, 2], F32)
    IDENT = sb("IDENT", [128, 128], F32)
    ROTF = sb("ROTF", [128, 128], F32)
    ROTB = sb("ROTB", [128, 128], BF16)
    ONESD = sb("ONESD", [128, 128], BF16)
    ONESH = sb("ONESH", [128, 128], BF16)
    ONES1 = sb("ONES1", [128, 128], BF16)
    EPST = sb("EPST", [128, 1], F32)
    RCE = sb("RCE", [128, 5, 4, 2, 16], F32)
    MH = sb("MH", [128, 4, 16], F32)
    ST = sb("ST", [128, 8, 2], F32)
    work0 = off[0]
    Hb = sb("Hb", [128, 8, 528], BF16)
    SQR = sb("SQR", [128, 3, 528], BF16)
    RSTD = sb("RSTD", [128, 528], F32)
    TMPM = sb("TMPM", [128, 2, 528], F32)
    NSLOT = 5
    WR = [sb("WR%d" % i, [128, 2048], BF16) for i in range(NSLOT)]
    G = sb("G", [128, 12, 512], BF16)
    SA = sb("SA", [128, 2, 512], F32)
    QO = sb("QO", [128, 8, 512], BF16)
    RS = sb("RS", [128, 512], F32)
    KN = sb("KN", [128, 512], F32)
    U1 = sb("U1", [128, 512], F32)
    U2 = sb("U2", [128, 512], BF16)
    CS = sb("CS", [128, 2, 512], F32)
    PB = sb("PB", [128, 4, 512], BF16)
    RL = sb("RL", [128, 512], F32)
    PSM = sb("PSM", [128, 4, 512], BF16)
    PSQ = sb("PSQ", [128, 3, 512], BF16)
    YST = sb("YST", [128, 1024], F32)
    KST = sb("KST", [128, 2, 512], BF16)
    VST = sb("VST", [128, 4, 256], BF16)
    PKT = sb("PKT", [128, 2, 512], BF16)
    PV = sb("PV", [128, 4, 256], BF16)
    m0 = off[0]
    UP = sb("UP", [128, 2, 544], F32)
    PT = sb("PT", [128, 2, 544], F32)
    PD = sb("PD", [128, 2, 512], BF16)
    ZP = sb("ZP", [128, 2, 516], F32)
    VSB = sb("VSB", [128, 512], F32)
    ACC = sb("ACC", [128, 512], F32)
    HB = sb("HB", [128, 192], F32)
    m1 = off[0]
    assert m1 - m0 <= 36864 - 16384, (m1 - m0)
    XP = sb("XP", [128, 8, 512], F32, at=m0 + 36864 - 16384)
    sb_end = m0 + 36864
    assert sb_end <= 229344, sb_end
    YCAT = sb("YCAT", [128, 8, 512], BF16, at=OFF["G"])
    KT = sb("KT", [128, 2, 4608], BF16, at=m0)
    V = sb("V", [128, 36, 256], BF16, at=m0 + 18432)
    NKST = sb("NKST", [128, 4, 256], F32, at=OFF["SA"])
    NVST = sb("NVST", [128, 4, 256], F32, at=OFF["PB"])
    so = [work0]

    def sbs(name, shape, dt):
        n = int(np.prod(shape[1:])) * (4 if dt == F32 else 2)
        o = so[0]
        so[0] = (o + n + 31) // 32 * 32
        assert so[0] <= m1, (name, so[0], m1)
        return nc.alloc_sbuf_tensor_at(name, list(shape), dt, offset=o)

    XST = [sbs("XST%d" % i, [128, 1024], F32) for i in range(2)]
    AST = [sbs("AST%d" % i, [128, 8, 256], F32) for i in range(3)]
    ATK = sbs("ATK", [128, 6144], F32)
    AGS = sbs("AGS", [128, 2, 6144], F32)

    PS = nc.alloc_psum_tensor("PS", [128, 8, 512], F32)

    bank_state = {"next": 0, "pinned": set()}

    def nb():
        while True:
            b = bank_state["next"]
            bank_state["next"] = (b + 1) % 8
            if b not in bank_state["pinned"]:
                return b

    def pk(b):
        return ("ps", b)

    cnt = {"ev": 0, "cast": 0}

    def prm(col, n=1):
        return PRM[:, col:col + n]

    def mm(out_ap, pairs, reads, bank, extra_w=()):
        n = len(pairs)
        chunked = [k for k in reads if isinstance(k, tuple) and k[0] in ("H", "HH", "G", "QO") and isinstance(k[1], int)]
        if len(chunked) == n and n > 1:
            others = [k for k in reads if k not in chunked]
            for i, (l, r) in enumerate(pairs):
                S.add("pe", lambda e, l=l, r=r, i=i: e.matmul(out_ap, lhsT=l, rhs=r, start=(i == 0), stop=(i == n - 1)),
                      reads=(others if i in (0, n - 1) else []) + [chunked[i]], excl=[pk(bank)] + list(extra_w))
            return

        def fn(e):
            ins = None
            for i, (l, r) in enumerate(pairs):
                ins = e.matmul(out_ap, lhsT=l, rhs=r, start=(i == 0), stop=(i == n - 1))
            return ins
        S.add("pe", fn, reads=reads, excl=[pk(bank)] + list(extra_w))

    wstate = {"i": 0}

    def wload(sc, rkeys, k0, nk, c0, ncols):
        i = wstate["i"] % NSLOT
        wstate["i"] += 1
        dst = WR[i][:, 0:nk * ncols].rearrange("p (k c) -> p k c", k=nk)
        src = sc[k0:k0 + nk * 128, c0:c0 + ncols].rearrange("(k p) c -> p k c", p=128)
        rk = [k for (k, r0, r1) in rkeys if r0 < k0 + nk * 128 and r1 > k0]
        wop = S.add("sp", lambda e: e.dma_start(out=dst, in_=src), reads=rk, writes=[("w", i)], dma_key=("w", i))
        conv_tick(wop)
        return dst, ("w", i)

    def convert(name, src_fn, dst, nrows, rows_per, defer=None):
        keys = []
        for r0 in range(0, nrows, rows_per):
            r1 = min(nrows, r0 + rows_per)
            key = ("cv", name, r0)
            job = (lambda r0=r0, r1=r1, key=key: S.add("pool", lambda e: e.dma_start(out=dst[r0:r1, :], in_=src_fn(r0, r1)),
                                                       reads=list(gate["reads"]), writes=[key, "cvchain"], dma_key="cv", extra=list(gate["extra"])))
            if defer is None:
                job()
            else:
                defer.append(job)
            keys.append((key, r0, r1))
        return keys

    jobs_l0b = []
    jobs_l1 = []
    gate = {"reads": [], "extra": []}
    K_win = convert("win", lambda a, b: w_in_even[a:b, :], sc_win, D, 128, jobs_l0b)
    K_pool = convert("pool", lambda a, b: pool_w.ap().rearrange("g p j -> (g p) j")[a:b, :], sc_pool, 512, 512, jobs_l0b)
    K_wout = convert("wout", lambda a, b: w_out_even[a:b, :], sc_wout, D, 256, jobs_l0b)
    K_fin = [None, None]
    K_fout = [None, None]
    K_fin[0] = convert("fin0", lambda a, b: w_ffn_in[0, a:b, :], sc_fin[0], D, 64, jobs_l0b)
    K_fout[0] = convert("fout0", lambda a, b: w_ffn_out[0, a:b, :], sc_fout[0], DFF, 176, jobs_l0b)
    K_qkv = convert("qkv", lambda a, b: w_qkv[a:b, :], sc_qkv, D, 256, jobs_l0b)
    KL1 = {}
    KL1["wo"] = convert("wo", lambda a, b: w_o[a:b, :], sc_wo, D, 256, jobs_l1)
    K_fin[1] = convert("fin1", lambda a, b: w_ffn_in[1, a:b, :], sc_fin[1], D, 64, jobs_l1)
    K_fout[1] = convert("fout1", lambda a, b: w_ffn_out[1, a:b, :], sc_fout[1], DFF, 176, jobs_l1)
    tick = {"on": False, "n": 0}

    def conv_tick(wkey=None):
        if tick["on"] and jobs_l1:
            tick["n"] += 1
            if tick["n"] % 3 == 0:
                gate["extra"] = [wkey] if wkey is not None else []
                jobs_l1.pop(0)()
                gate["extra"] = []

    S.add("sp", lambda e: e.dma_start(out=PRM[:, :], in_=params[:, :]), writes=["PRM"], dma_key="ld_prm")
    S.add("sp", lambda e: e.dma_start(out=IDENT[:, :], in_=consts[:, 0:128]), writes=["IDENT"], dma_key="ld_id")
    S.add("sp", lambda e: e.dma_start(out=ROTF[:, :], in_=consts[:, 128:256]), writes=["ROTF"], dma_key="ld_rot")
    S.add("sp", lambda e: e.dma_start(out=MH[:, :, :], in_=mh_d.ap().rearrange("p (a b) -> p a b", a=4)), writes=["MH"], dma_key="ld_mh")
    S.add("sp", lambda e: e.dma_start(out=RCE.ap().rearrange("p a b c d -> p (a b c d)"), in_=rce_d[:, :]),
          writes=["RCE"], dma_key="ld_rce")
    S.add("pool", lambda e: e.memset(ONESD[:, :], 1.0 / 1024.0), writes=["ONESD"])
    S.add("pool", lambda e: e.memset(ONESH[:, :], 1.0 / 128.0), writes=["ONESH"])
    S.add("pool", lambda e: e.memset(ONES1[:, :], 1.0), writes=["ONES1"])
    S.add("pool", lambda e: e.memset(EPST[:, :], EPS), writes=["EPST"])
    S.add("dve", lambda e: e.tensor_copy(out=ROTB[:, :], in_=ROTF[:, :]), reads=["ROTF"], writes=["ROTB"])
    S.add("act", lambda e: e.activation(out=ST[:, :, :], in_=PRM[:, P_CVEC:P_CVEC + 16].rearrange("p (k v) -> p k v", v=2), func=AF.Silu),
          reads=["PRM"], writes=["ST"])

    for cb in range(24):
        ast = AST[cb % 3]
        ak = ("AST", cb % 3)
        S.add("sp", lambda e, ast=ast, cb=cb: e.dma_start(
            out=ast[:, :, :], in_=wada_sh[:, cb * 256:(cb + 1) * 256].rearrange("(k p) c -> p k c", p=128)),
            writes=[ak], dma_key=ak)
        b = nb()
        mm(PS[0:2, b, 0:256], [(ST[:, kc, :], ast[:, kc, :]) for kc in range(8)], [ak, "ST"], b)
        S.add("dve", lambda e, b=b, cb=cb: e.tensor_copy(out=ATK[0:2, cb * 256:(cb + 1) * 256], in_=PS[0:2, b, 0:256]), writes=[("ATK", cb)], excl=[pk(b)])
    S.add("sp", lambda e: e.dma_start(out=ag_in[:, :], in_=ATK[0:2, :]), reads=[("ATK", cb) for cb in range(24)], writes=["ag_in"], dma_key="st_ag")
    S.add("pool", lambda e: e.collective_compute("AllGather", ALU.bypass, replica_groups=[[0, 1], [2, 3], [4, 5], [6, 7]],
                                                 ins=[ag_in.ap().opt()], outs=[ag_out.ap().opt()]),
          reads=["ag_in"], writes=["ag_out"], dma_key="cc_ada", inc=1)
    def evac(out_ap, in_ap, reads, writes, excl):
        cnt["ev"] += 1
        if cnt["ev"] % 2 == 0:
            S.add("act", lambda e: e.copy(out=out_ap, in_=in_ap), reads=reads, writes=writes, excl=excl)
        else:
            S.add("dve", lambda e: e.tensor_copy(out=out_ap, in_=in_ap), reads=reads, writes=writes, excl=excl)

    for tb in range(21):
        st = XST[tb % 2]
        stk = ("XST", tb % 2)
        if tb < 20:
            src = xt[tb * 128:(tb + 1) * 128, :]
            npart = 128
        else:
            src = xh[0:64, :]
            npart = 64
        S.add("sp", lambda e, st=st, src=src, npart=npart: e.dma_start(out=st[0:npart, :], in_=src), writes=[stk], dma_key=stk)
        for half in range(2):
            b = nb()
            for ci in range(4):
                c = half * 4 + ci
                S.add("pe", lambda e, b=b, ci=ci, c=c, st=st, npart=npart: e.transpose(
                    PS[:, b, ci * 128:ci * 128 + npart], st[0:npart, c * 128:(c + 1) * 128], IDENT[0:npart, 0:npart]),
                    reads=[stk, "IDENT"], excl=[pk(b)])
            if tb < 16:
                dst = X[:, half * 4:half * 4 + 4, tb * 128:(tb + 1) * 128]
                wk = [("X", tb // 4, half * 4 + ci) for ci in range(4)]
            elif tb < 20:
                dst = XP[:, half * 4:half * 4 + 4, (tb - 16) * 128:(tb - 15) * 128]
                wk = [("X", 4, half * 4 + ci) for ci in range(4)]
            else:
                dst = XH[:, half * 4:half * 4 + 4, 0:64]
                wk = ["XH"]
            src_ps = PS[:, b, 0:512].rearrange("p (c t) -> p c t", c=4)[:, :, 0:npart]
            evac(dst, src_ps, [], wk, [pk(b)])

    S.add("sp", lambda e: e.dma_start(out=AGS[0:2, :, :], in_=ag_out.ap().rearrange("(r v) j -> v r j", v=2)), reads=["ag_out"], writes=["AGS"], dma_key="ld_ags")
    for job in jobs_l0b:
        job()
    for l in range(2):
        for i0 in range(0, 48, 4):
            b2 = nb()
            for ii in range(4):
                j0 = (i0 + ii) * 128
                S.add("pe", lambda e, b2=b2, ii=ii, l=l, j0=j0: e.matmul(PS[:, b2, ii * 2:ii * 2 + 2], lhsT=AGS[0:2, l, j0:j0 + 128], rhs=IDENT[0:2, 0:2],
                                                                       start=True, stop=True),
                      reads=["AGS", "IDENT"], excl=[pk(b2)])
            S.add("act", lambda e, b2=b2, l=l, i0=i0: e.copy(out=ADA[:, l, i0:i0 + 4, :], in_=PS[:, b2, 0:8].rearrange("p (i v) -> p i v", v=2)),
                  writes=[("ADA", l)], excl=[pk(b2)])
    for l in range(2):
        for v in range(2):
            S.add("dve", lambda e, l=l, v=v: e.tensor_tensor(out=ADA[:, l, :, v], in0=ADA[:, l, :, v],
                                                            in1=PRM[:, P_BADA + l * 48:P_BADA + (l + 1) * 48], op=ALU.add),
                  reads=["PRM"], writes=[("ADA", l)])
            for j in range(2):
                S.add("dve", lambda e, l=l, v=v, j=j: e.scalar_tensor_tensor(
                    out=AM[:, l, j, :, v], in0=ADA[:, l, 8 + 24 * j:16 + 24 * j, v], scalar=1.0,
                    in1=PRM[:, P_NG + (l * 2 + j) * 8:P_NG + (l * 2 + j) * 8 + 8], op0=ALU.add, op1=ALU.mult),
                    reads=["PRM", ("ADA", l)], writes=["AM"])

    S.barrier(skip_dma=lambda k: k == "cv")

    def xs(ti, c, c0, c1):
        if ti < 4:
            return X[:, c, ti * 512 + c0:ti * 512 + c1]
        return XP[:, c, c0:c1]

    def stats_begin():
        b = nb()
        bank_state["pinned"].add(b)
        return {"b": b, "pend": [], "n": 0}

    def stats_chunk(st, src_ap, src_key, c):
        q = SQR[:, c % 3, 0:512]
        qk = ("SQR", c % 3)
        S.add("act", lambda e: e.activation(out=q, in_=src_ap, func=AF.Square), reads=[src_key], writes=[qk])
        st["pend"].append((q, qk))

    def stats_flush(st, keep=0):
        while len(st["pend"]) > keep:
            q, qk = st["pend"].pop(0)
            n_ = st["n"]
            b = st["b"]
            S.add("pe", lambda e, q=q, n_=n_, b=b: e.matmul(PS[:, b, :], lhsT=ONESD[:, :], rhs=q, start=(n_ == 0), stop=(n_ == 7)),
                  reads=[qk], excl=[pk(b)])
            st["n"] += 1

    def norm_mod(src, srck, ncols, acol, bcol, dcol0, D_scale_tile, hk, pre=None):
        if pre is not None:
            stats_flush(pre, 0)
            b = pre["b"]
            bank_state["pinned"].discard(b)
        else:
            b = nb()
        for c in range(8 if pre is None else 0):
            q = SQR[:, c % 3, 0:ncols]
            qk = ("SQR", c % 3)
            S.add("act", lambda e, c=c, q=q: e.activation(out=q, in_=src(c), func=AF.Square), reads=[srck(c)], writes=[qk])
            S.add("pe", lambda e, c=c, q=q, b=b: e.matmul(PS[:, b, 0:ncols], lhsT=D_scale_tile[:, :], rhs=q, start=(c == 0), stop=(c == 7)),
                  reads=[qk], excl=[pk(b)])
        rs = RSTD[:, 0:ncols]
        S.add("act", lambda e: e.activation(out=rs, in_=PS[:, b, 0:ncols], func=AF.Ln, bias=EPST[:, 0:1]), writes=["RSTD"], excl=[pk(b)])
        S.add("act", lambda e: e.activation(out=rs, in_=rs, func=AF.Exp, scale=-0.5), writes=["RSTD"])
        for c in range(8):
            t = TMPM[:, c % 2, 0:ncols]
            tk = ("TMPM", c % 2)
            S.add("dve", lambda e, c=c, t=t: e.scalar_tensor_tensor(out=t, in0=src(c), scalar=acol(c), in1=rs, op0=ALU.mult, op1=ALU.mult),
                  reads=[srck(c), "RSTD", "AM"], writes=[tk])
            S.add("act", lambda e, c=c, t=t: e.activation(out=Hb[:, c, dcol0:dcol0 + ncols], in_=t, func=AF.Identity, bias=bcol(c)),
                  reads=[tk, ("ADA", 0), ("ADA", 1)], writes=[(hk, c)])

    def ffn(ti, l, v):
        g2 = lambda m: ADA[:, l, 40 + m, v:v + 1]
        hr = [("H", kc) for kc in range(8)]
        stf = None
        for (f0, nf) in ((0, 12), (12, 10)):
            if f0 > 0:
                stf = stats_begin()
            for i in range(f0 // 2, (f0 + nf) // 2):
                wa, wak = wload(sc_fin[l], K_fin[l], 0, 8, i * 256, 256)
                wb, wbk = wload(sc_fin[l], K_fin[l], 0, 8, DFF + i * 256, 256)
                for f2 in range(2):
                    fl = 2 * i + f2 - f0
                    ba = nb()
                    mm(PS[:, ba, :], [(wa[:, kc, f2 * 128:(f2 + 1) * 128], Hb[:, kc, 0:512]) for kc in range(8)], [wak] + hr, ba)
                    bb = nb()
                    mm(PS[:, bb, :], [(wb[:, kc, f2 * 128:(f2 + 1) * 128], Hb[:, kc, 0:512]) for kc in range(8)], [wbk] + hr, bb)
                    sa = SA[:, fl % 2, :]
                    sak = ("SA", fl % 2)
                    S.add("act", lambda e, sa=sa, ba=ba: e.activation(out=sa, in_=PS[:, ba, :], func=AF.Silu), writes=[sak], excl=[pk(ba)])
                    S.add("dve", lambda e, sa=sa, bb=bb, fl=fl: e.tensor_tensor(out=G[:, fl, :], in0=PS[:, bb, :], in1=sa, op=ALU.mult),
                          reads=[sak], writes=[("G", fl)], excl=[pk(bb)])
            h1 = nf // 2
            for mp in range(4):
                w1, w1k = wload(sc_fout[l], K_fout[l], f0 * 128, h1, mp * 256, 256)
                w2, w2k = wload(sc_fout[l], K_fout[l], (f0 + h1) * 128, nf - h1, mp * 256, 256)
                for m2 in range(2):
                    m = mp * 2 + m2
                    bo = nb()
                    pairs = [(w1[:, kc, m2 * 128:(m2 + 1) * 128], G[:, kc, :]) for kc in range(h1)]
                    pairs += [(w2[:, kc, m2 * 128:(m2 + 1) * 128], G[:, h1 + kc, :]) for kc in range(nf - h1)]
                    mm(PS[:, bo, :], pairs, [w1k, w2k] + [("G", kc) for kc in range(nf)], bo)
                    S.add("dve", lambda e, m=m, bo=bo: e.scalar_tensor_tensor(out=xs(ti, m, 0, 512), in0=PS[:, bo, :], scalar=g2(m),
                                                                              in1=xs(ti, m, 0, 512), op0=ALU.mult, op1=ALU.add),
                          reads=[("ADA", l)], writes=[("X", ti, m)], excl=[pk(bo)])
                    if f0 > 0:
                        stats_chunk(stf, xs(ti, m, 0, 512), ("X", ti, m), m)
                        stats_flush(stf, keep=2)
        return stf

    def layer0(ti):
        sample = ti < 4
        v = 1 if sample else 0
        segs = [(0, 512)] if sample else [(0, 256), (256, 256)]
        l = 0
        xk = lambda c: ("X", ti, c)
        norm_mod(lambda c: xs(ti, c, 0, 512), xk, 512, lambda c: AM[:, l, 0, c, v:v + 1], lambda c: ADA[:, l, 0 + c, v:v + 1], 0, ONESD, "H")
        if sample:
            norm_mod(lambda c: XH[:, c, ti * 16:(ti + 1) * 16], lambda c: "XH", 16, lambda c: AM[:, l, 0, c, v:v + 1],
                     lambda c: ADA[:, l, 0 + c, v:v + 1], 512, ONESD, "HH")
        hreads = [("H", kc) for kc in range(8)]
        hhreads = [("HH", kc) for kc in range(8)]
        bh = None
        if sample:
            bh = nb()
            bank_state["pinned"].add(bh)
        halo_slot = {}
        for i_, ch in enumerate([0, 1, 2, 3, 8, 9, 10, 11, 12, 13, 14, 15]):
            halo_slot[ch] = i_

        wunits = {}

        def win_halo(ch):
            w_, wk = wunits[ch // 2]
            c2 = ch % 2
            hs = halo_slot[ch]
            mm(PS[:, bh, hs * 16:(hs + 1) * 16], [(w_[:, kc, c2 * 128:(c2 + 1) * 128], Hb[:, kc, 512:528]) for kc in range(8)],
               [wk] + hhreads, bh)

        def win_chunk(ch, halo=True):
            u = ch // 2
            if u not in wunits:
                wunits[u] = wload(sc_win, K_win, 0, 8, u * 256, 256)
            w_, wk = wunits[u]
            c2 = ch % 2
            b = nb()
            mm(PS[:, b, :], [(w_[:, kc, c2 * 128:(c2 + 1) * 128], Hb[:, kc, 0:512]) for kc in range(8)], [wk] + hreads, b)
            if halo and sample and ch in halo_slot:
                win_halo(ch)
            return b

        ubanks = [win_chunk(g, halo=False) for g in range(4)]
        if sample:
            for g in range(4):
                win_halo(g)
        for bb_ in ubanks:
            bank_state["pinned"].add(bb_)
        if sample:
            S.add("act", lambda e: e.copy(out=HB[:, 0:64], in_=PS[:, bh, 0:64]), writes=["HBu"], excl=[pk(bh)])
        def pool_section():
            pw_, pwk = wload(sc_pool, K_pool, 0, 4, 0, 128)
            for g in range(4):
                up = UP[:, g % 2, :]
                upk = ("UP", g % 2)
                bu = ubanks[g]
                nsteps = g + 1
                for si, (c0, L) in enumerate(segs):
                    base = si * (L + 16)
                    S.add("act", lambda e, up=up, bu=bu, base=base, c0=c0, L=L: e.copy(out=up[:, base + 8:base + 8 + L], in_=PS[:, bu, c0:c0 + L]),
                          writes=[upk], excl=[pk(bu)])
                    if sample:
                        S.add("dve", lambda e, up=up, g=g: e.tensor_tensor(out=up[:, 0:8], in0=HB[:, g * 16:g * 16 + 8], in1=MH[:, ti, 0:8], op=ALU.mult),
                              reads=["HBu", "MH"], writes=[upk])
                        S.add("dve", lambda e, up=up, g=g: e.tensor_tensor(out=up[:, 520:528], in0=HB[:, g * 16 + 8:g * 16 + 16], in1=MH[:, ti, 8:16], op=ALU.mult),
                              reads=["HBu", "MH"], writes=[upk])
                    else:
                        S.add("pool", lambda e, up=up, base=base: e.memset(up[:, base:base + 8], 0.0), writes=[upk])
                        S.add("pool", lambda e, up=up, base=base, L=L: e.memset(up[:, base + 8 + L:base + 16 + L], 0.0), writes=[upk])
                bank_state["pinned"].discard(bu)
                for si, (c0, L) in enumerate(segs):
                    base = si * (L + 16)
                    P_ = L + 16
                    cur = up
                    curk = upk
                    curoff = base
                    ln = P_
                    for s_ in range(nsteps):
                        sh = 1 << s_
                        dst = PT[:, s_ % 2, :]
                        dk = ("PT", s_ % 2)
                        nl = ln - sh
                        S.add("dve", lambda e, cur=cur, curoff=curoff, dst=dst, nl=nl, sh=sh: e.tensor_tensor(
                            out=dst[:, 0:nl], in0=cur[:, curoff:curoff + nl], in1=cur[:, curoff + sh:curoff + sh + nl], op=ALU.add),
                            reads=[curk], writes=[dk])
                        cur, curk, curoff, ln = dst, dk, 0, nl
                    w = 2 << g
                    st0 = 8 - w // 2
                    pd = PD[:, g % 2, :]
                    pdk = ("PD", g % 2)
                    S.add("dve", lambda e, cur=cur, st0=st0, L=L, c0=c0, pd=pd, up=up, base=base, w=w: e.scalar_tensor_tensor(
                        out=pd[:, c0:c0 + L], in0=cur[:, st0:st0 + L], scalar=1.0 / w, in1=up[:, base + 8:base + 8 + L], op0=ALU.mult, op1=ALU.subtract),
                        reads=[curk, upk], writes=[pdk])
                    edges = [(0, 0), (L - 8, 8)]
                    if sample:
                        edges = ([(0, 0)] if ti == 0 else []) + ([(L - 8, 8)] if ti == 3 else [])
                    for (e0, r0) in edges:
                        tmp = TMPM[:, 0, 0:8]
                        S.add("dve", lambda e, cur=cur, st0=st0, e0=e0, r0=r0, g=g, si=si, tmp=tmp: e.tensor_tensor(
                            out=tmp, in0=cur[:, st0 + e0:st0 + e0 + 8], in1=RCE[:, ti, g, si, r0:r0 + 8], op=ALU.mult),
                            reads=[curk, "RCE"], writes=[("TMPM", 0)])
                        S.add("dve", lambda e, tmp=tmp, pd=pd, c0=c0, e0=e0, up=up, base=base: e.tensor_tensor(
                            out=pd[:, c0 + e0:c0 + e0 + 8], in0=tmp, in1=up[:, base + 8 + e0:base + 16 + e0], op=ALU.subtract),
                            reads=[("TMPM", 0), upk], writes=[pdk])
                bp = nb()
                mm(PS[:, bp, :], [(pw_[:, g, :], PD[:, g % 2, :])], [pwk, ("PD", g % 2)], bp)
                S.add("act", lambda e, g=g, bp=bp: e.activation(out=YCAT[:, g, :], in_=PS[:, bp, :], func=AF.Identity, scale=prm(P_PSC + g)),
                      reads=["PRM"], writes=[("G", g)], excl=[pk(bp)])
        for j in range(4):
            if j % 2 == 0:
                wunits.clear()
            if j == 0:
                bcg, bv, bbg = win_chunk(8), win_chunk(12), win_chunk(4)
                for bb_ in (bcg, bv, bbg):
                    bank_state["pinned"].add(bb_)
                pool_section()
                for bb_ in (bcg, bv, bbg):
                    bank_state["pinned"].discard(bb_)
            else:
                bcg, bv, bbg = win_chunk(8 + j), win_chunk(12 + j), win_chunk(4 + j)
            zp = ZP[:, j % 2, :]
            zk = ("ZP", j % 2)
            S.add("act", lambda e, bv=bv: e.copy(out=VSB[:, :], in_=PS[:, bv, :]), writes=["VSB"], excl=[pk(bv)])
            for si, (c0, L) in enumerate(segs):
                base = si * (L + 2)
                S.add("dve", lambda e, zp=zp, base=base, L=L, c0=c0, bcg=bcg: e.tensor_tensor(
                    out=zp[:, base + 1:base + 1 + L], in0=PS[:, bcg, c0:c0 + L], in1=VSB[:, c0:c0 + L], op=ALU.mult),
                    reads=["VSB"], writes=[zk], excl=[pk(bcg)])
                if not sample:
                    S.add("pool", lambda e, zp=zp, base=base: e.memset(zp[:, base:base + 1], 0.0), writes=[zk])
                    S.add("pool", lambda e, zp=zp, base=base, L=L: e.memset(zp[:, base + 1 + L:base + 2 + L], 0.0), writes=[zk])
            if sample:
                pass
            if sample:
                hs_c, hs_v = halo_slot[8 + j], halo_slot[12 + j]
                tz = TMPM[:, 1, 0:16]
                S.add("act", lambda e, hs_c=hs_c: e.copy(out=TMPM[:, 0, 16:32], in_=PS[:, bh, hs_c * 16:hs_c * 16 + 16]),
                      writes=[("TMPM", 0)], excl=[pk(bh)])
                S.add("dve", lambda e, hs_v=hs_v, tz=tz: e.tensor_tensor(out=tz, in0=PS[:, bh, hs_v * 16:hs_v * 16 + 16], in1=TMPM[:, 0, 16:32], op=ALU.mult),
                      reads=[("TMPM", 0)], writes=[("TMPM", 1)], excl=[pk(bh)])
                S.add("dve", lambda e, zp=zp, tz=tz: e.tensor_tensor(out=zp[:, 0:1], in0=tz[:, 7:8], in1=MH[:, ti, 7:8], op=ALU.mult),
                      reads=[("TMPM", 1), "MH"], writes=[zk])
                S.add("dve", lambda e, zp=zp, tz=tz: e.tensor_tensor(out=zp[:, 513:514], in0=tz[:, 8:9], in1=MH[:, ti, 8:9], op=ALU.mult),
                      reads=[("TMPM", 1), "MH"], writes=[zk])
            for si, (c0, L) in enumerate(segs):
                base = si * (L + 2)
                acc = ACC[:, c0:c0 + L]
                S.add("act", lambda e, zp=zp, base=base, L=L, acc=acc, j=j: e.activation(
                    out=acc, in_=zp[:, base:base + L], func=AF.Identity, scale=prm(P_CW + 0 * 4 + j), bias=prm(P_CB + j)),
                    reads=[zk, "PRM"], writes=["ACC"])
                for k in (1, 2):
                    S.add("dve", lambda e, zp=zp, base=base, L=L, acc=acc, j=j, k=k: e.scalar_tensor_tensor(
                        out=acc, in0=zp[:, base + k:base + k + L], scalar=prm(P_CW + k * 4 + j), in1=acc, op0=ALU.mult, op1=ALU.add),
                        reads=[zk, "PRM"], writes=["ACC"])
            S.add("dve", lambda e, j=j, bbg=bbg: e.tensor_tensor(out=YCAT[:, 4 + j, :], in0=PS[:, bbg, :], in1=ACC[:, :], op=ALU.mult),
                  reads=["ACC"], writes=[("G", 4 + j)], excl=[pk(bbg)])
        if sample:
            bank_state["pinned"].discard(bh)
        st2 = stats_begin()
        for m in range(8):
            if m % 2 == 0:
                w_, wk = wload(sc_wout, K_wout, 0, 8, (m // 2) * 256, 256)
            b = nb()
            mm(PS[:, b, :], [(w_[:, kc, (m % 2) * 128:(m % 2 + 1) * 128], YCAT[:, kc, :]) for kc in range(8)], [wk] + [("G", kc) for kc in range(8)], b)
            S.add("dve", lambda e, m=m, b=b: e.scalar_tensor_tensor(out=xs(ti, m, 0, 512), in0=PS[:, b, :], scalar=ADA[:, l, 16 + m, v:v + 1],
                                                                    in1=xs(ti, m, 0, 512), op0=ALU.mult, op1=ALU.add),
                  reads=[("ADA", l)], writes=[("X", ti, m)], excl=[pk(b)])
            stats_chunk(st2, xs(ti, m, 0, 512), ("X", ti, m), m)
            stats_flush(st2, keep=2)
        norm_mod(lambda c: xs(ti, c, 0, 512), xk, 512, lambda c: AM[:, l, 1, c, v:v + 1], lambda c: ADA[:, l, 24 + c, v:v + 1], 0, ONESD, "H", pre=st2)
        return ffn(ti, l, v)

    RS2 = sb("RS2", [128, 512], F32, at=OFF["G"])
    KN2 = sb("KN2", [128, 512], F32, at=OFF["G"] + 2048)
    U12 = sb("U12", [128, 512], F32, at=OFF["G"] + 4096)
    U22 = sb("U22", [128, 512], BF16, at=OFF["G"] + 6144)
    HSET = [
        dict(sq=SQR[:, 0, 0:512], sqk=("SQR", 0), rs=RS, rsk="RS", kn=KN, knk="KN", u1=U1, u1k="U1", u2=U2, u2k="U2", extra=[]),
        dict(sq=SQR[:, 1, 0:512], sqk=("SQR", 1), rs=RS2, rsk="RS2", kn=KN2, knk="KN2", u1=U12, u1k="U12", u2=U22, u2k="U22",
             extra=[("G", i) for i in range(7)]),
    ]

    def head_norm_stages(mm_fn, gain_col, out_ap, out_key, rope, par):
        hs = HSET[par]
        st = {}

        def A():
            bq = nb()
            st["bq"] = bq
            bank_state["pinned"].add(bq)
            mm_fn(bq)
            S.add("act", lambda e: e.activation(out=hs["sq"], in_=PS[:, bq, :], func=AF.Square), writes=[hs["sqk"]], excl=[pk(bq)])

        def B():
            bq = st["bq"]
            b2 = nb()
            mm(PS[:, b2, :], [(ONESH[:, :], hs["sq"])], [hs["sqk"], "ONESH"], b2)
            rs = hs["rs"]
            xk = hs["extra"]
            S.add("act", lambda e: e.activation(out=rs[:, :], in_=PS[:, b2, :], func=AF.Ln, bias=EPST[:, 0:1]), writes=[hs["rsk"]] + xk, excl=[pk(b2)])
            S.add("act", lambda e: e.activation(out=rs[:, :], in_=rs[:, :], func=AF.Exp, scale=-0.5), writes=[hs["rsk"]])
            bank_state["pinned"].discard(bq)
            if rope is None:
                S.add("dve", lambda e: e.scalar_tensor_tensor(out=out_ap, in0=PS[:, bq, :], scalar=gain_col, in1=rs[:, :], op0=ALU.mult, op1=ALU.mult),
                      reads=[hs["rsk"], "PRM"], writes=[out_key], excl=[pk(bq)])
                return
            kn, u1, u2 = hs["kn"], hs["u1"], hs["u2"]
            S.add("dve", lambda e: e.scalar_tensor_tensor(out=kn[:, :], in0=PS[:, bq, :], scalar=gain_col, in1=rs[:, :], op0=ALU.mult, op1=ALU.mult),
                  reads=[hs["rsk"], "PRM"], writes=[hs["knk"]] + xk, excl=[pk(bq)])
            S.add("pool", lambda e: e.tensor_tensor(out=u1[:, :], in0=kn[:, :], in1=CS[:, 0, :], op=ALU.mult), reads=[hs["knk"], "CS"], writes=[hs["u1k"]] + xk)
            S.add("dve", lambda e: e.tensor_tensor(out=u2[:, :], in0=kn[:, :], in1=CS[:, 1, :], op=ALU.mult), reads=[hs["knk"], "CS"], writes=[hs["u2k"]] + xk)

        def C():
            if rope is None:
                return
            u1, u2 = hs["u1"], hs["u2"]
            b3 = nb()
            mm(PS[:, b3, :], [(ROTB[:, :], u2[:, :])], [hs["u2k"], "ROTB"], b3)
            S.add("dve", lambda e: e.tensor_tensor(out=out_ap, in0=PS[:, b3, :], in1=u1[:, :], op=ALU.add), reads=[hs["u1k"]], writes=[out_key], excl=[pk(b3)])

        return A, B, C

    def run_head_pipeline(stages):
        n = len(stages)
        stages[0][0]()
        if n > 1:
            stages[1][0]()
        stages[0][1]()
        for h in range(n):
            if h + 2 < n:
                stages[h + 2][0]()
            if h + 1 < n:
                stages[h + 1][1]()
            stages[h][2]()

    def l1_norm1(ti, v, pre=None):
        norm_mod(lambda c: xs(ti, c, 0, 512), lambda c: ("X", ti, c), 512, lambda c: AM[:, 1, 0, c, v:v + 1],
                 lambda c: ADA[:, 1, 0 + c, v:v + 1], 0, ONESD, "H", pre=pre)

    def load_cs(ti):
        S.add("pool", lambda e: e.dma_start(out=CS[:, :, :], in_=cs_d[:, ti, :, :]), writes=["CS"], dma_key="ld_cs")

    def kv_stage(ti, pre=None):
        sample = ti < 4
        v = 1 if sample else 0
        hreads = [("H", kc) for kc in range(8)]
        l1_norm1(ti, v, pre)
        if sample:
            load_cs(ti)
        wk_, wkk = wload(sc_qkv, K_qkv, 0, 8, 1024, 256)
        if sample:
            stages = []
            for kvh in range(2):
                def mmf(bq, kvh=kvh):
                    mm(PS[:, bq, :], [(wk_[:, kc, kvh * 128:(kvh + 1) * 128], Hb[:, kc, 0:512]) for kc in range(8)], [wkk] + hreads, bq)
                stages.append(head_norm_stages(mmf, prm(P_KG), KST[:, kvh, :], ("KST", kvh), True, kvh % 2))
            run_head_pipeline(stages)
        for kvh in range(2):
            if sample:
                break
            b = nb()
            mm(PS[:, b, :], [(wk_[:, kc, kvh * 128:(kvh + 1) * 128], Hb[:, kc, 0:512]) for kc in range(8)], [wkk] + hreads, b)
            if sample:
                pass
            else:
                S.add("act", lambda e, b=b: e.activation(out=SQR[:, 0, 0:512], in_=PS[:, b, :], func=AF.Square), writes=[("SQR", 0)], excl=[pk(b)])
                b2 = nb()
                mm(PS[:, b2, :], [(ONESH[:, :], SQR[:, 0, 0:512])], [("SQR", 0), "ONESH"], b2)
                S.add("act", lambda e, b2=b2: e.activation(out=RS[:, :], in_=PS[:, b2, :], func=AF.Ln, bias=EPST[:, 0:1]), writes=["RS"], excl=[pk(b2)])
                S.add("act", lambda e: e.activation(out=RS[:, :], in_=RS[:, :], func=AF.Exp, scale=-0.5), writes=["RS"])
                S.add("dve", lambda e, b=b: e.scalar_tensor_tensor(out=KN[:, :], in0=PS[:, b, :], scalar=prm(P_KG), in1=RS[:, :], op0=ALU.mult, op1=ALU.mult),
                      reads=["RS", "PRM"], writes=["KN"], excl=[pk(b)])
                S.add("act", lambda e, kvh=kvh: e.copy(out=PKT[:, kvh, :], in_=KN[:, :]), reads=["KN"], writes=[("PKT", kvh)])
                b3 = nb()
                for tt in range(4):
                    S.add("pe", lambda e, tt=tt, b3=b3: e.transpose(PS[:, b3, tt * 128:(tt + 1) * 128], KN[:, tt * 128:(tt + 1) * 128], IDENT[:, :]),
                          reads=["KN", "IDENT"], excl=[pk(b3)])
                S.add("act", lambda e, kvh=kvh, b3=b3: e.copy(out=NKST[:, :, kvh * 128:(kvh + 1) * 128],
                                                              in_=PS[:, b3, :].rearrange("p (t d) -> p t d", t=4)),
                      writes=[("SA", 0), ("SA", 1), "NKST"], excl=[pk(b3)])
        if sample:
            S.add("pool", lambda e: e.dma_start(out=kvx_in[:, 0:4096].rearrange("p (k j t) -> p k j t", k=2, j=4)[:, :, ti, :], in_=KST[:, :, :]),
                  reads=[("KST", 0), ("KST", 1)], writes=["kvx_in"], dma_key="st_k")
        else:
            S.add("pool", lambda e: e.dma_start(out=nk_d.ap().rearrange("(t p) c -> p t c", p=128), in_=NKST[:, :, :]),
                  reads=["NKST", ("SA", 0), ("SA", 1)], dma_key="st_nk")
        wv_, wvk = wload(sc_qkv, K_qkv, 0, 8, 1280, 256)
        for tp in range(2):
            b = nb()
            for t2 in range(2):
                tt = tp * 2 + t2
                mm(PS[:, b, t2 * 256:(t2 + 1) * 256], [(Hb[:, kc, tt * 128:(tt + 1) * 128], wv_[:, kc, :]) for kc in range(8)],
                   [wvk] + hreads, b)
            src = PS[:, b, :].rearrange("p (t c) -> p t c", t=2)
            if sample:
                S.add("act", lambda e, tp=tp, src=src: e.copy(out=VST[:, tp * 2:tp * 2 + 2, :], in_=src), writes=[("VST", tp)], excl=[pk(b)])
            else:
                S.add("act", lambda e, tp=tp, src=src: e.copy(out=NVST[:, tp * 2:tp * 2 + 2, :], in_=src),
                      writes=[("PB", 0), ("PB", 1), ("PB", 2), ("PB", 3), ("NVST", tp)], excl=[pk(b)])
                S.add("dve", lambda e, tp=tp: e.tensor_copy(out=PV[:, tp * 2:tp * 2 + 2, :], in_=NVST[:, tp * 2:tp * 2 + 2, :]),
                      reads=[("NVST", tp), ("PB", 0), ("PB", 1), ("PB", 2), ("PB", 3)], writes=[("PV", tp)])
        if sample:
            S.add("pool", lambda e: e.dma_start(out=kvx_in[:, 4096 + ti * 1024:4096 + (ti + 1) * 1024].rearrange("p (t c) -> p t c", t=4), in_=VST[:, :, :]),
                  reads=[("VST", 0), ("VST", 1)], writes=["kvx_in"], dma_key="st_v")
        else:
            S.add("pool", lambda e: e.dma_start(out=nv_d.ap().rearrange("(t p) c -> p t c", p=128), in_=NVST[:, :, :]),
                  reads=[("NVST", 0), ("NVST", 1), ("PB", 0), ("PB", 1), ("PB", 2), ("PB", 3)], dma_key="st_nv")

    SM_SCALE = float(128 ** -0.5)

    hoisted = set()

    def attention(ti, hoist_next=None):
        sample = ti < 4
        v = 1 if sample else 0
        hreads = [("H", kc) for kc in range(8)]
        if ti not in hoisted:
            l1_norm1(ti, v)
        if sample:
            load_cs(ti)
        stages = []
        wq_cache = {}

        def get_wq(hp):
            if hp not in wq_cache:
                wq_cache[hp] = wload(sc_qkv, K_qkv, 0, 8, hp * 256, 256)
            return wq_cache[hp]
        for h in range(8):
            def mmf(bq, h=h):
                wq_, wqk = get_wq(h // 2)
                h2 = h % 2
                mm(PS[:, bq, :], [(wq_[:, kc, h2 * 128:(h2 + 1) * 128], Hb[:, kc, 0:512]) for kc in range(8)], [wqk] + hreads, bq)
            stages.append(head_norm_stages(mmf, prm(P_QG), QO[:, h, :], ("QO", h), True if sample else None, h % 2))
        run_head_pipeline(stages)
        segs_ = []
        for h in range(8):
            kvh = h // 4
            bo, bl = (0, 1) if h % 2 == 0 else (2, 3)
            if sample:
                segs_.append(dict(h=h, q0=0, qn=512, bo=bo, bl=bl, last=True,
                                  chunks=[(KT[:, kvh, sc * 128:(sc + 1) * 128], V[:, sc, kvh * 128:(kvh + 1) * 128], "KT", "V") for sc in range(36)]))
            else:
                for s_ in range(2):
                    segs_.append(dict(h=h, q0=s_ * 256, qn=256, bo=bo, bl=bl, last=(s_ == 1),
                                      chunks=[(PKT[:, kvh, s_ * 256 + sc * 128:s_ * 256 + (sc + 1) * 128], PV[:, s_ * 2 + sc, kvh * 128:(kvh + 1) * 128],
                                               ("PKT", kvh), ("PV", s_)) for sc in range(2)]))
        items = [(sg, i) for sg in segs_ for i in range(len(sg["chunks"]))]
        NI = len(items)
        sbank = [None] * NI
        AHEAD = 3

        def issue_s(g):
            sg, i = items[g]
            kt_ap, _, kk, _ = sg["chunks"][i]
            b = 4 + g % 4
            sbank[g] = b
            mm(PS[:, b, 0:sg["qn"]], [(kt_ap, QO[:, sg["h"], sg["q0"]:sg["q0"] + sg["qn"]])], [kk, ("QO", sg["h"])], b)

        def make_finalize(h, bo, bl):
            def finalize():
                S.add("dve", lambda e: e.reciprocal(out=RL[:, :], in_=PS[:, bl, :]), writes=[("RL", kk) for kk in range(8)], excl=[pk(bl)])
                S.add("dve", lambda e: e.tensor_tensor(out=QO[:, h, :], in0=PS[:, bo, :], in1=RL[:, :], op=ALU.mult),
                      reads=[("RL", kk) for kk in range(8)], writes=[("QO", h)], excl=[pk(bo)])
            return finalize

        NPAIR = NI // 2

        def issue_s_pair(p):
            for t in range(2):
                g = 2 * p + t
                sg, i = items[g]
                kt_ap, _, kk, _ = sg["chunks"][i]
                b = 4 + (p % 2) * 2 + t
                mm(PS[:, b, 0:sg["qn"]], [(kt_ap, QO[:, sg["h"], sg["q0"]:sg["q0"] + sg["qn"]])], [kk, ("QO", sg["h"])], b)

        for p in range(min(2, NPAIR)):
            issue_s_pair(p)
        pend = []
        fin_pending = []
        for p in range(NPAIR):
            sg, i0 = items[2 * p]
            n = len(sg["chunks"])
            q0, qn, bo, bl = sg["q0"], sg["qn"], sg["bo"], sg["bl"]
            while fin_pending and fin_pending[0][0] <= p:
                fin_pending.pop(0)[1]()
            b0 = 4 + (p % 2) * 2
            k0 = (p % 2) * 2
            pbp = PB[:, k0:k0 + 2, 0:qn]
            S.add("act", lambda e, b0=b0, pbp=pbp, qn=qn: e.activation(out=pbp, in_=PS[:, b0:b0 + 2, 0:qn], func=AF.Exp, scale=SM_SCALE),
                  writes=[("PB", k0), ("PB", k0 + 1)], excl=[pk(b0), pk(b0 + 1)])
            if p + 2 < NPAIR:
                issue_s_pair(p + 2)
            for t in range(2):
                i = i0 + t
                _, v_ap, _, vk = sg["chunks"][i]
                pb = PB[:, k0 + t, 0:qn]
                S.add("pe", lambda e, v_ap=v_ap, pb=pb, i=i, bo=bo, q0=q0, qn=qn, n=n: e.matmul(PS[:, bo, q0:q0 + qn], lhsT=v_ap, rhs=pb, start=(i == 0), stop=(i == n - 1)),
                      reads=[vk, ("PB", k0 + t)], excl=[pk(bo)])
            pj = p % 4
            psm = PSM[:, pj, 0:qn]
            S.add("dve", lambda e, psm=psm, k0=k0, qn=qn: e.tensor_tensor(out=psm, in0=PB[:, k0, 0:qn], in1=PB[:, k0 + 1, 0:qn], op=ALU.add),
                  reads=[("PB", k0), ("PB", k0 + 1)], writes=[("PSM", pj)])
            ip = i0 // 2
            npairs = n // 2
            if npairs % 2 == 1 or npairs < 2:
                pend.append((psm, ("PSM", pj), p, ip == 0, ip == npairs - 1, bl, q0, qn))
            elif ip % 2 == 1:
                qj = (p // 2) % 3
                psq = PSQ[:, qj, 0:qn]
                psm0 = PSM[:, (p - 1) % 4, 0:qn]
                S.add("dve", lambda e, psq=psq, psm0=psm0, psm=psm: e.tensor_tensor(out=psq, in0=psm0, in1=psm, op=ALU.add),
                      reads=[("PSM", (p - 1) % 4), ("PSM", pj)], writes=[("PSQ", qj)])
                pend.append((psq, ("PSQ", qj), p, ip == 1, ip == npairs - 1, bl, q0, qn))
            while pend and (pend[0][2] <= p - 1 or p == NPAIR - 1):
                src_, key_, p_, first_, last_, bl_, q0_, qn_ = pend.pop(0)
                S.add("pe", lambda e, src_=src_, first_=first_, last_=last_, bl_=bl_, q0_=q0_, qn_=qn_: e.matmul(
                    PS[:, bl_, q0_:q0_ + qn_], lhsT=ONES1[:, :], rhs=src_, start=first_, stop=last_),
                      reads=["ONES1", key_], excl=[pk(bl_)])
            if i0 + 1 == n - 1 and sg["last"]:
                if sample:
                    hh = sg["h"]
                    for k in range(8):
                        def piece(k=k, bl=bl):
                            S.add("dve", lambda e: e.reciprocal(out=RL[:, k * 64:(k + 1) * 64], in_=PS[:, bl, k * 64:(k + 1) * 64]),
                                  writes=[("RL", k)], excl=[pk(bl)])
                        fin_pending.append((p + 4 + k, piece))
                    for k2 in range(2):
                        def mulp(k2=k2, bo=bo, hh=hh):
                            S.add("dve", lambda e: e.tensor_tensor(out=QO[:, hh, k2 * 256:(k2 + 1) * 256], in0=PS[:, bo, k2 * 256:(k2 + 1) * 256],
                                                                   in1=RL[:, k2 * 256:(k2 + 1) * 256], op=ALU.mult),
                                  reads=[("RL", kk) for kk in range(8)], writes=[("QO", hh)], excl=[pk(bo)])
                        fin_pending.append((p + 12 + k2, mulp))
                else:
                    fin_pending.append((p + 2, make_finalize(sg["h"], bo, bl)))
        while fin_pending:
            fin_pending.pop(0)[1]()
        sto = stats_begin()
        for m in range(8):
            if m % 2 == 0:
                w_, wk = wload(sc_wo, KL1["wo"], 0, 8, (m // 2) * 256, 256)
            b = nb()
            mm(PS[:, b, :], [(w_[:, kc, (m % 2) * 128:(m % 2 + 1) * 128], QO[:, kc, :]) for kc in range(8)], [wk] + [("QO", kc) for kc in range(8)], b)
            S.add("dve", lambda e, m=m, b=b: e.scalar_tensor_tensor(out=xs(ti, m, 0, 512), in0=PS[:, b, :], scalar=ADA[:, 1, 16 + m, v:v + 1],
                                                                    in1=xs(ti, m, 0, 512), op0=ALU.mult, op1=ALU.add),
                  reads=[("ADA", 1)], writes=[("X", ti, m)], excl=[pk(b)])
            stats_chunk(sto, xs(ti, m, 0, 512), ("X", ti, m), m)
            stats_flush(sto, keep=2)
        norm_mod(lambda c: xs(ti, c, 0, 512), lambda c: ("X", ti, c), 512, lambda c: AM[:, 1, 1, c, v:v + 1],
                 lambda c: ADA[:, 1, 24 + c, v:v + 1], 0, ONESD, "H", pre=sto)
        stf_ = ffn(ti, 1, v)
        stats_flush(stf_, 0)
        b = stf_["b"]
        bank_state["pinned"].discard(b)
        S.add("act", lambda e, b=b: e.activation(out=RSTD[:, 0:512], in_=PS[:, b, :], func=AF.Ln, bias=EPST[:, 0:1]), writes=["RSTD"], excl=[pk(b)])
        S.add("act", lambda e: e.activation(out=RSTD[:, 0:512], in_=RSTD[:, 0:512], func=AF.Exp, scale=-0.5), writes=["RSTD"])
        for c in range(8):
            S.add("dve", lambda e, c=c: e.scalar_tensor_tensor(out=xs(ti, c, 0, 512), in0=xs(ti, c, 0, 512), scalar=prm(P_FG + c), in1=RSTD[:, 0:512],
                                                               op0=ALU.mult, op1=ALU.mult),
                  reads=["RSTD", "PRM"], writes=[("X", ti, c)])
        nt = hoist_next
        sth = stats_begin() if nt is not None else None
        for tt in range(4):
            if nt is not None:
                for c in (2 * tt, 2 * tt + 1):
                    stats_chunk(sth, xs(nt, c, 0, 512), ("X", nt, c), c)
            late = []
            for half in range(2):
                b = nb()
                for ci in range(4):
                    c = half * 4 + ci
                    S.add("pe", lambda e, b=b, ci=ci, c=c, tt=tt: e.transpose(PS[:, b, ci * 128:(ci + 1) * 128], xs(ti, c, tt * 128, (tt + 1) * 128), IDENT[:, :]),
                          reads=[("X", ti, c), "IDENT"], excl=[pk(b)])
                ev = (lambda half=half, b=b: evac(YST[:, half * 512:(half + 1) * 512], PS[:, b, :], [], [("YST", half)], [pk(b)]))
                if nt is not None and tt == 3:
                    late.append(ev)
                else:
                    ev()
            if nt is not None:
                stats_flush(sth, 0)
                if tt == 3:
                    l1_norm1(nt, 1, sth)
                    hoisted.add(nt)
                    for ev in late:
                        ev()
            r0 = ti * 512 + tt * 128
            S.add("pool", lambda e, r0=r0: e.dma_start(out=y_d[r0:r0 + 128, :], in_=YST[:, :]), reads=[("YST", 0), ("YST", 1)], dma_key="st_y")

    for ti in range(4):
        if ti == 1:
            tick["on"] = True
        stx = layer0(ti)
        kv_stage(ti, stx)
    tick["on"] = False
    while jobs_l1:
        jobs_l1.pop(0)()
    S.add("pool", lambda e: e.collective_compute("AllGather", ALU.bypass, replica_groups=[[0, 1], [2, 3], [4, 5], [6, 7]],
                                                 ins=[kvx_in.ap().opt()], outs=[kvx_out.ap().opt()]),
          reads=["kvx_in"], writes=["kvx_out"], dma_key="cc", inc=1)
    stx = layer0(4)
    kv_stage(4, stx)
    attention(4, hoist_next=0)
    S.barrier()
    for r in range(2):
        for kvh in range(2):
            S.add("sp", lambda e, r=r, kvh=kvh: e.dma_start(out=KT[:, kvh, 512 + r * 2048:512 + (r + 1) * 2048],
                                                          in_=kvx_out[r * 128:(r + 1) * 128, kvh * 2048:(kvh + 1) * 2048]),
                  reads=["kvx_out"], writes=["KT"], dma_key=("ld_kt", r, kvh))
        S.add("sp", lambda e, r=r: e.dma_start(out=V[:, 4 + r * 16:4 + (r + 1) * 16, :],
                                               in_=kvx_out[r * 128:(r + 1) * 128, 4096:8192].rearrange("p (t c) -> p t c", t=16)),
              reads=["kvx_out"], writes=["V"], dma_key=("ld_v", r))
    S.add("pool", lambda e: e.dma_start(out=V[:, 0:4, :], in_=cv_d.ap().rearrange("(t p) c -> p t c", p=128)), writes=["V"], dma_key="ld_cv")
    CK2 = YST
    S.add("sp", lambda e: e.dma_start(out=CK2[:, :].rearrange("p (t c) -> p t c", t=4), in_=ck_d.ap().rearrange("(t p) c -> p t c", p=128)),
          writes=[("YST", 0), ("YST", 1)], dma_key="ld_ck2")
    for kvh in range(2):
        b = nb()
        for tt in range(4):
            S.add("pe", lambda e, b=b, tt=tt, kvh=kvh: e.transpose(PS[:, b, tt * 128:(tt + 1) * 128],
                                                                 CK2[:, tt * 256 + kvh * 128:tt * 256 + (kvh + 1) * 128], IDENT[:, :]),
                  reads=[("YST", 0), ("YST", 1), "IDENT"], excl=[pk(b)])
        S.add("act", lambda e, b=b, kvh=kvh: e.copy(out=KT[:, kvh, 0:512], in_=PS[:, b, :]), writes=["KT"], excl=[pk(b)])
    for ti in range(4):
        attention(ti, hoist_next=(ti + 1 if ti < 3 else None))
    S.emit()
    return nc


_CACHE = {}


def _rope_tables(L, grid_w=64, theta=10000.0, axis_dim=64):
    rows = L // grid_w
    r = np.repeat(np.arange(rows), grid_w).astype(np.float32)
    col = np.tile(np.arange(grid_w), rows).astype(np.float32)
    inv = (np.float32(theta) ** (-np.arange(0, axis_dim, 2, dtype=np.float32) / np.float32(axis_dim))).astype(np.float32)
    ar = r[:, None] * inv
    ac = col[:, None] * inv
    ang = np.concatenate([ar, ar, ac, ac], axis=-1)
    return np.cos(ang).astype(np.float32), np.sin(ang).astype(np.float32)


def kernel(x_prompt, x_sample, cache_k, cache_v, c, c_ctx, w_ada, b_ada, norm_g,
           w_in_even, pool_w, pool_scale, conv_w, conv_b, w_out_even,
           w_qkv, q_gain, k_gain, w_o, w_ffn_in, w_ffn_out, final_g):
    f = lambda a: np.ascontiguousarray(np.asarray(a, dtype=np.float32))
    x_prompt, x_sample, cache_k, cache_v, c, c_ctx = map(f, (x_prompt, x_sample, cache_k, cache_v, c, c_ctx))
    w_ada, b_ada, norm_g, w_in_even, pool_w, pool_scale = map(f, (w_ada, b_ada, norm_g, w_in_even, pool_w, pool_scale))
    conv_w, conv_b, w_out_even, w_qkv, q_gain, k_gain = map(f, (conv_w, conv_b, w_out_even, w_qkv, q_gain, k_gain))
    w_o, w_ffn_in, w_ffn_out, final_g = map(f, (w_o, w_ffn_in, w_ffn_out, final_g))
    if "nc" not in _CACHE:
        _CACHE["nc"] = build_program()
    nc = _CACHE["nc"]

    ident = np.eye(128, dtype=np.float32)
    R = np.zeros((128, 128), np.float32)
    for d in range(128):
        if (d % 64) < 32:
            R[d, d + 32] = -1.0
        else:
            R[d, d - 32] = 1.0
    consts0 = np.concatenate([ident, np.ascontiguousarray(R.T)], axis=1)
    wada_flat = [w_ada[0], w_ada[1]]
    cos, sin = _rope_tables(4096)

    def fm(vec):
        return np.ascontiguousarray(vec.reshape(-1, 128).T)

    in_maps = []
    for core in range(8):
        b = core // 2
        s0 = (core % 2) * 2048
        xt = np.concatenate([x_sample[b, s0:s0 + 2048], x_prompt[2 * core:2 * core + 2].reshape(512, D)], axis=0)
        xh = np.zeros((64, D), np.float32)
        mh = np.zeros((4, 16), np.float32)
        for j in range(4):
            a = s0 + 512 * j
            if a - 8 >= 0:
                xh[j * 16:j * 16 + 8] = x_sample[b, a - 8:a]
                mh[j, 0:8] = 1.0
            if a + 520 <= 4096:
                xh[j * 16 + 8:j * 16 + 16] = x_sample[b, a + 512:a + 520]
                mh[j, 8:16] = 1.0
        rce = np.zeros((5, 4, 2, 16), np.float32)
        for ti in range(5):
            for g in range(4):
                w = 2 << g
                for si in range(2):
                    if ti < 4:
                        L, t0, seglen = 4096, s0 + 512 * ti, 512
                    else:
                        L, t0, seglen = 256, 0, 256
                    cols = np.concatenate([np.arange(0, 8), np.arange(seglen - 8, seglen)]) + t0
                    lo = np.clip(cols - w // 2, 0, L)
                    hi = np.clip(cols + w // 2, 0, L)
                    rce[ti, g, si] = (np.float32(1.0) / (hi - lo).astype(np.float32))
        cvec = np.stack([c_ctx, c[b]], axis=0)
        cvec5 = np.concatenate([c_ctx[None, :], c], axis=0)
        sel = np.zeros((128, 2), np.float32)
        sel[0, 0] = 1.0
        sel[1 + b, 1] = 1.0
        consts = np.ascontiguousarray(np.concatenate([consts0, sel], axis=1))
        wada_sh = wada_flat[core % 2]
        prm = np.zeros((128, NP), np.float32)
        prm[:, P_CVEC:P_CVEC + 16] = cvec.reshape(2, 8, 128).transpose(2, 1, 0).reshape(128, 16)
        prm[:, P_BADA:P_BADA + 96] = b_ada.reshape(2, 48, 128).transpose(2, 0, 1).reshape(128, 96)
        prm[:, P_NG:P_NG + 32] = norm_g.reshape(2, 2, 8, 128).transpose(3, 0, 1, 2).reshape(128, 32)
        prm[:, P_PSC:P_PSC + 4] = fm(pool_scale[0])
        prm[:, P_CW:P_CW + 12] = conv_w[0].reshape(3, 4, 128).transpose(2, 0, 1).reshape(128, 12)
        prm[:, P_CB:P_CB + 4] = fm(conv_b[0])
        prm[:, P_QG] = q_gain[0]
        prm[:, P_KG] = k_gain[0]
        prm[:, P_FG:P_FG + 8] = fm(final_g)
        prm[:, P_CV5:P_CV5 + 40] = cvec5.reshape(5, 8, 128).transpose(2, 1, 0).reshape(128, 40)
        cs = np.zeros((128, 4, 2, 512), np.float32)
        for j in range(4):
            cs[:, j, 0, :] = cos[s0 + 512 * j:s0 + 512 * (j + 1)].T
            cs[:, j, 1, :] = sin[s0 + 512 * j:s0 + 512 * (j + 1)].T
        in_maps.append({
            "xt": np.ascontiguousarray(xt), "xh": xh, "params": prm, "consts": consts,
            "mh": np.ascontiguousarray(np.broadcast_to(mh.reshape(1, 64), (128, 64))),
            "rce": np.ascontiguousarray(np.broadcast_to(rce.reshape(1, 640), (128, 640))),
            "cs": cs,
            "ck": np.ascontiguousarray(cache_k[b, 0].reshape(512, 256)),
            "cv": np.ascontiguousarray(cache_v[b, 0].reshape(512, 256)),
            "wada_sh": wada_sh, "w_in_even": w_in_even[0], "pool_w": pool_w[0], "w_out_even": w_out_even[0],
            "w_qkv": w_qkv[0], "w_o": w_o[0], "w_ffn_in": w_ffn_in, "w_ffn_out": w_ffn_out,
        })
    res = run_bass_kernel_spmd(nc, in_maps, core_ids=list(range(8)))
    y_prompt = np.zeros((16, 256, D), np.float32)
    y_sample = np.zeros((4, 4096, D), np.float32)
    nk = np.zeros((16, 1, 256, 2, 128), np.float32)
    nv = np.zeros((16, 1, 256, 2, 128), np.float32)
    for core in range(8):
        r = res.results[core]
        b = core // 2
        s0 = (core % 2) * 2048
        y = np.asarray(r["y"], dtype=np.float32)
        y_sample[b, s0:s0 + 2048] = y[0:2048]
        y_prompt[2 * core:2 * core + 2] = y[2048:2560].reshape(2, 256, D)
        nk[2 * core:2 * core + 2, 0] = np.asarray(r["nk"], dtype=np.float32).reshape(2, 256, 2, 128)
        nv[2 * core:2 * core + 2, 0] = np.asarray(r["nv"], dtype=np.float32).reshape(2, 256, 2, 128)
    return (y_prompt, y_sample, nk, nv)
```

```python
import contextlib
import numpy as np
import concourse.bass as bass
import concourse.mybir as mybir
from concourse.bass_utils import run_bass_kernel_spmd

F32 = mybir.dt.float32
BF16 = mybir.dt.bfloat16
ALU = mybir.AluOpType
AF = mybir.ActivationFunctionType

ENGS = ["pe", "act", "dve", "pool", "sp"]
D = 1024
DFF = 2816
NT = 2560
EPS = 1e-6


class Op:
    __slots__ = ("eng", "fn", "deps", "signal", "sig_idx", "dma_key", "dma_ord", "pos", "inc")

    def __init__(self, eng, fn, dma_key, inc):
        self.eng = eng
        self.fn = fn
        self.deps = set()
        self.signal = False
        self.sig_idx = 0
        self.dma_key = dma_key
        self.dma_ord = 0
        self.pos = 0
        self.inc = inc


class Sched:
    def __init__(self, nc):
        self.nc = nc
        self.ops = {e: [] for e in ENGS}
        self.last_w = {}
        self.readers = {}
        self.dma_count = {}
        self.dma_last = {}

    def add(self, eng, fn, reads=(), writes=(), excl=(), dma_key=None, inc=16, extra=()):
        op = Op(eng, fn, dma_key, inc)
        writes = list(writes) + list(excl)
        deps = set(extra)
        for k in reads:
            w = self.last_w.get(k)
            if w is not None:
                deps.add(w)
        for k in writes:
            w = self.last_w.get(k)
            if w is not None:
                deps.add(w)
            for r in self.readers.get(k, ()):
                deps.add(r)
        op.deps = deps
        for k in writes:
            self.last_w[k] = op
            self.readers[k] = []
        for k in reads:
            if self.last_w.get(k) is not op:
                self.readers.setdefault(k, []).append(op)
        op.pos = len(self.ops[eng])
        self.ops[eng].append(op)
        if dma_key is not None:
            c = self.dma_count.get(dma_key, 0) + 1
            self.dma_count[dma_key] = c
            op.dma_ord = c
            self.dma_last[dma_key] = op
        return op

    def barrier(self, skip_dma=None):
        lasts = [self.ops[e][-1] for e in ENGS if self.ops[e]]
        lasts = [o for o in lasts if not (o.dma_key is not None and skip_dma is not None and skip_dma(o.dma_key))]
        lasts += [o for k, o in self.dma_last.items() if not (skip_dma is not None and skip_dma(k))]
        for e in ENGS:
            self.add(e, None, extra=lasts)

    def emit(self, final_wait_eng="sp"):
        nc = self.nc
        need = {}
        for e in ENGS:
            for op in self.ops[e]:
                best = {}
                dmabest = {}
                for d in op.deps:
                    if d is op:
                        continue
                    if d.dma_key is not None:
                        b = dmabest.get(d.dma_key)
                        if b is None or d.dma_ord > b.dma_ord:
                            dmabest[d.dma_key] = d
                        continue
                    if d.eng == op.eng:
                        if d.eng == "pe":
                            continue
                        if op.pos - d.pos > 2:
                            continue
                    b = best.get(d.eng)
                    if b is None or d.pos > b.pos:
                        best[d.eng] = d
                lst = list(best.values()) + list(dmabest.values())
                for d in lst:
                    d.signal = True
                need[id(op)] = lst
        for e in ENGS:
            k = 0
            for op in self.ops[e]:
                if op.dma_key is None and op.signal:
                    assert op.fn is not None
                    k += 1
                    op.sig_idx = k
        stack = contextlib.ExitStack()
        eng_sem = {e: stack.enter_context(nc.semaphore("s_" + e)) for e in ENGS}
        dma_sem = {}
        dma_inc = {}
        for e in ENGS:
            for op in self.ops[e]:
                if op.dma_key is not None and op.dma_key not in dma_sem:
                    dma_sem[op.dma_key] = stack.enter_context(nc.semaphore("d%d" % len(dma_sem)))
                    dma_inc[op.dma_key] = op.inc

        def emit_engine(eng_name, eng):
            waited = {}
            for op in self.ops[eng_name]:
                for d in need[id(op)]:
                    if d.dma_key is not None:
                        sem = dma_sem[d.dma_key]
                        val = d.dma_ord * dma_inc[d.dma_key]
                        key = ("d", d.dma_key)
                    else:
                        sem = eng_sem[d.eng]
                        val = d.sig_idx
                        key = ("e", d.eng)
                    if waited.get(key, 0) >= val:
                        continue
                    waited[key] = val
                    eng.wait_ge(sem, val)
                if op.fn is None:
                    continue
                ins = op.fn(eng)
                if op.dma_key is not None:
                    ins.then_inc(dma_sem[op.dma_key], dma_inc[op.dma_key])
                elif op.signal:
                    ins.then_inc(eng_sem[eng_name], 1)
            if eng_name == final_wait_eng:
                for k, sem in dma_sem.items():
                    eng.wait_ge(sem, self.dma_count[k] * dma_inc[k])

        with stack:
            with nc.Block() as block:

                @block.tensor
                def _(eng):
                    emit_engine("pe", eng)

                @block.scalar
                def _(eng):
                    emit_engine("act", eng)

                @block.vector
                def _(eng):
                    emit_engine("dve", eng)

                @block.gpsimd
                def _(eng):
                    emit_engine("pool", eng)

                @block.sync
                def _(eng):
                    emit_engine("sp", eng)


P_CVEC = 0
P_BADA = 16
P_NG = 112
P_PSC = 144
P_CW = 148
P_CB = 160
P_QG = 164
P_KG = 165
P_FG = 166
P_CV5 = 176
NP = 216


def build_program():
    nc = bass.Bass("TRN2", target_bir_lowering=False)
    S = Sched(nc)

    def din(name, shape, dt=F32):
        return nc.dram_tensor(name, list(shape), dt, kind="ExternalInput")

    def dout(name, shape, dt=F32):
        return nc.dram_tensor(name, list(shape), dt, kind="ExternalOutput")

    xt = din("xt", [NT, D])
    xh = din("xh", [64, D])
    params = din("params", [128, NP])
    consts = din("consts", [128, 258])
    mh_d = din("mh", [128, 64])
    rce_d = din("rce", [128, 640])
    cs_d = din("cs", [128, 4, 2, 512])
    ck_d = din("ck", [512, 256])
    cv_d = din("cv", [512, 256])
    wada_sh = din("wada_sh", [D, 6144])
    w_in_even = din("w_in_even", [D, 2048])
    pool_w = din("pool_w", [4, 128, 128])
    w_out_even = din("w_out_even", [D, D])
    w_qkv = din("w_qkv", [D, 1536])
    w_o = din("w_o", [D, D])
    w_ffn_in = din("w_ffn_in", [2, D, 2 * DFF])
    w_ffn_out = din("w_ffn_out", [2, DFF, D])
    y_d = dout("y", [NT, D])
    nk_d = dout("nk", [512, 256])
    nv_d = dout("nv", [512, 256])

    def dscr(name, shape, dt=BF16):
        return nc.dram_tensor(name, list(shape), dt)

    sc_win = dscr("sc_win", [D, 2048])
    sc_wout = dscr("sc_wout", [D, D])
    sc_pool = dscr("sc_pool", [512, 128])
    sc_qkv = dscr("sc_qkv", [D, 1536])
    sc_wo = dscr("sc_wo", [D, D])
    sc_fin = [dscr("sc_fin%d" % l, [D, 2 * DFF]) for l in range(2)]
    sc_fout = [dscr("sc_fout%d" % l, [DFF, D]) for l in range(2)]
    ag_in = dscr("ag_in", [2, 6144], F32)
    ag_out = dscr("ag_out", [4, 6144], F32)
    kvx_in = dscr("kvx_in", [128, 8192])
    kvx_out = dscr("kvx_out", [256, 8192])

    off = [16512]
    OFF = {}

    def sb(name, shape, dt, at=None):
        n = int(np.prod(shape[1:])) * (4 if dt == F32 else 2)
        if at is None:
            o = off[0]
            off[0] = (o + n + 31) // 32 * 32
        else:
            o = at
        OFF[name] = o
        return nc.alloc_sbuf_tensor_at(name, list(shape), dt, offset=o)

    X = sb("X", [128, 8, 2048], F32)
    XH = sb("XH", [128, 8, 64], F32)
    PRM = sb("PRM", [128, NP], F32)
    ADA = sb("ADA", [128, 2, 48, 2], F32)
    AM = sb("AM", [128, 2, 2, 8, 2], F32)
    IDENT = sb("IDENT", [128, 128], F32)
    ROTF = sb("ROTF", [128, 128], F32)
    ROTB = sb("ROTB", [128, 128], BF16)
    ONESD = sb("ONESD", [128, 128], BF16)
    ONESH = sb("ONESH", [128, 128], BF16)
    ONES1 = sb("ONES1", [128, 128], BF16)
    EPST = sb("EPST", [128, 1], F32)
    RCE = sb("RCE", [128, 5, 4, 2, 16], F32)
    MH = sb("MH", [128, 4, 16], F32)
    ST = sb("ST", [128, 8, 2], F32)
    work0 = off[0]
    Hb = sb("Hb", [128, 8, 528], BF16)
    SQR = sb("SQR", [128, 3, 528], BF16)
    RSTD = sb("RSTD", [128, 528], F32)
    TMPM = sb("TMPM", [128, 2, 528], F32)
    NSLOT = 5
    WR = [sb("WR%d" % i, [128, 2048], BF16) for i in range(NSLOT)]
    G = sb("G", [128, 12, 512], BF16)
    SA = sb("SA", [128, 2, 512], F32)
    QO = sb("QO", [128, 8, 512], BF16)
    RS = sb("RS", [128, 512], F32)
    KN = sb("KN", [128, 512], F32)
    U1 = sb("U1", [128, 512], F32)
    U2 = sb("U2", [128, 512], BF16)
    CS = sb("CS", [128, 2, 512], F32)
    PB = sb("PB", [128, 4, 512], BF16)
    RL = sb("RL", [128, 512], F32)
    PSM = sb("PSM", [128, 4, 512], BF16)
    PSQ = sb("PSQ", [128, 3, 512], BF16)
    YST = sb("YST", [128, 1024], F32)
    KST = sb("KST", [128, 2, 512], BF16)
    VST = sb("VST", [128, 4, 256], BF16)
    PKT = sb("PKT", [128, 2, 512], BF16)
    PV = sb("PV", [128, 4, 256], BF16)
    m0 = off[0]
    UP = sb("UP", [128, 2, 544], F32)
    PT = sb("PT", [128, 2, 544], F32)
    PD = sb("PD", [128, 2, 512], BF16)
    ZP = sb("ZP", [128, 2, 516], F32)
    VSB = sb("VSB", [128, 512], F32)
    ACC = sb("ACC", [128, 512], F32)
    HB = sb("HB", [128, 192], F32)
    m1 = off[0]
    assert m1 - m0 <= 36864 - 16384, (m1 - m0)
    XP = sb("XP", [128, 8, 512], F32, at=m0 + 36864 - 16384)
    sb_end = m0 + 36864
    assert sb_end <= 229344, sb_end
    YCAT = sb("YCAT", [128, 8, 512], BF16, at=OFF["G"])
    KT = sb("KT", [128, 2, 4608], BF16, at=m0)
    V = sb("V", [128, 36, 256], BF16, at=m0 + 18432)
    NKST = sb("NKST", [128, 4, 256], F32, at=OFF["SA"])
    NVST = sb("NVST", [128, 4, 256], F32, at=OFF["PB"])
    so = [work0]

    def sbs(name, shape, dt):
        n = int(np.prod(shape[1:])) * (4 if dt == F32 else 2)
        o = so[0]
        so[0] = (o + n + 31) // 32 * 32
        assert so[0] <= m1, (name, so[0], m1)
        return nc.alloc_sbuf_tensor_at(name, list(shape), dt, offset=o)

    XST = [sbs("XST%d" % i, [128, 1024], F32) for i in range(2)]
    AST = [sbs("AST%d" % i, [128, 8, 256], F32) for i in range(4)]
    ATK = sbs("ATK", [128, 6144], F32)
    AGS = sbs("AGS", [128, 2, 6144], F32)

    PS = nc.alloc_psum_tensor("PS", [128, 8, 512], F32)

    bank_state = {"next": 0, "pinned": set()}

    def nb():
        while True:
            b = bank_state["next"]
            bank_state["next"] = (b + 1) % 8
            if b not in bank_state["pinned"]:
                return b

    def pk(b):
        return ("ps", b)

    cnt = {"ev": 0, "cast": 0}

    def prm(col, n=1):
        return PRM[:, col:col + n]

    def mm(out_ap, pairs, reads, bank, extra_w=()):
        n = len(pairs)
        chunked = [k for k in reads if isinstance(k, tuple) and k[0] in ("H", "HH", "G", "QO") and isinstance(k[1], int)]
        if len(chunked) == n and n > 1:
            others = [k for k in reads if k not in chunked]
            for i, (l, r) in enumerate(pairs):
                S.add("pe", lambda e, l=l, r=r, i=i: e.matmul(out_ap, lhsT=l, rhs=r, start=(i == 0), stop=(i == n - 1)),
                      reads=(others if i in (0, n - 1) else []) + [chunked[i]], excl=[pk(bank)] + list(extra_w))
            return

        def fn(e):
            ins = None
            for i, (l, r) in enumerate(pairs):
                ins = e.matmul(out_ap, lhsT=l, rhs=r, start=(i == 0), stop=(i == n - 1))
            return ins
        S.add("pe", fn, reads=reads, excl=[pk(bank)] + list(extra_w))

    wstate = {"i": 0}

    def wload(sc, rkeys, k0, nk, c0, ncols):
        i = wstate["i"] % NSLOT
        wstate["i"] += 1
        dst = WR[i][:, 0:nk * ncols].rearrange("p (k c) -> p k c", k=nk)
        src = sc[k0:k0 + nk * 128, c0:c0 + ncols].rearrange("(k p) c -> p k c", p=128)
        rk = [k for (k, r0, r1) in rkeys if r0 < k0 + nk * 128 and r1 > k0]
        wop = S.add("sp", lambda e: e.dma_start(out=dst, in_=src), reads=rk, writes=[("w", i)], dma_key=("w", i))
        conv_tick(wop)
        return dst, ("w", i)

    def convert(name, src_fn, dst, nrows, rows_per, defer=None):
        keys = []
        for r0 in range(0, nrows, rows_per):
            r1 = min(nrows, r0 + rows_per)
            key = ("cv", name, r0)
            job = (lambda r0=r0, r1=r1, key=key: S.add("pool", lambda e: e.dma_start(out=dst[r0:r1, :], in_=src_fn(r0, r1)),
                                                       reads=list(gate["reads"]), writes=[key, "cvchain"], dma_key="cv", extra=list(gate["extra"])))
            if defer is None:
                job()
            else:
                defer.append(job)
            keys.append((key, r0, r1))
        return keys

    jobs_l0b = []
    jobs_l1 = []
    gate = {"reads": [], "extra": []}
    K_win = convert("win", lambda a, b: w_in_even[a:b, :], sc_win, D, 128, jobs_l0b)
    K_pool = convert("pool", lambda a, b: pool_w.ap().rearrange("g p j -> (g p) j")[a:b, :], sc_pool, 512, 512, jobs_l0b)
    K_wout = convert("wout", lambda a, b: w_out_even[a:b, :], sc_wout, D, 256, jobs_l0b)
    K_fin = [None, None]
    K_fout = [None, None]
    K_fin[0] = convert("fin0", lambda a, b: w_ffn_in[0, a:b, :], sc_fin[0], D, 64, jobs_l0b)
    K_fout[0] = convert("fout0", lambda a, b: w_ffn_out[0, a:b, :], sc_fout[0], DFF, 176, jobs_l0b)
    K_qkv = convert("qkv", lambda a, b: w_qkv[a:b, :], sc_qkv, D, 256, jobs_l0b)
    KL1 = {}
    KL1["wo"] = convert("wo", lambda a, b: w_o[a:b, :], sc_wo, D, 256, jobs_l1)
    K_fin[1] = convert("fin1", lambda a, b: w_ffn_in[1, a:b, :], sc_fin[1], D, 64, jobs_l1)
    K_fout[1] = convert("fout1", lambda a, b: w_ffn_out[1, a:b, :], sc_fout[1], DFF, 176, jobs_l1)
    tick = {"on": False, "n": 0}

    def conv_tick(wkey=None):
        if tick["on"] and jobs_l1:
            tick["n"] += 1
            if tick["n"] % 3 == 0:
                gate["extra"] = [wkey] if wkey is not None else []
                jobs_l1.pop(0)()
                gate["extra"] = []

    S.add("sp", lambda e: e.dma_start(out=PRM[:, :], in_=params[:, :]), writes=["PRM"], dma_key="ld_prm")
    S.add("pool", lambda e: e.memset(ONESD[:, :], 1.0 / 1024.0), writes=["ONESD"])
    S.add("pool", lambda e: e.memset(ONESH[:, :], 1.0 / 128.0), writes=["ONESH"])
    S.add("pool", lambda e: e.memset(ONES1[:, :], 1.0), writes=["ONES1"])
    S.add("pool", lambda e: e.memset(EPST[:, :], EPS), writes=["EPST"])
    S.add("act", lambda e: e.activation(out=ST[:, :, :], in_=PRM[:, P_CVEC:P_CVEC + 16].rearrange("p (k v) -> p k v", v=2), func=AF.Silu),
          reads=["PRM"], writes=["ST"])

    for cb in range(24):
        ast = AST[cb % 4]
        ak = ("AST", cb % 4)
        S.add("sp", lambda e, ast=ast, cb=cb: e.dma_start(
            out=ast[:, :, :], in_=wada_sh[:, cb * 256:(cb + 1) * 256].rearrange("(k p) c -> p k c", p=128)),
            writes=[ak], dma_key=ak)
        b = nb()
        mm(PS[0:2, b, 0:256], [(ST[:, kc, :], ast[:, kc, :]) for kc in range(8)], [ak, "ST"], b)
        S.add("dve", lambda e, b=b, cb=cb: e.tensor_copy(out=ATK[0:2, cb * 256:(cb + 1) * 256], in_=PS[0:2, b, 0:256]), writes=[("ATK", cb)], excl=[pk(b)])
    S.add("sp", lambda e: e.dma_start(out=IDENT[:, :], in_=consts[:, 0:128]), writes=["IDENT"], dma_key="ld_id")
    S.add("sp", lambda e: e.dma_start(out=ROTF[:, :], in_=consts[:, 128:256]), writes=["ROTF"], dma_key="ld_rot")
    S.add("sp", lambda e: e.dma_start(out=MH[:, :, :], in_=mh_d.ap().rearrange("p (a b) -> p a b", a=4)), writes=["MH"], dma_key="ld_mh")
    S.add("sp", lambda e: e.dma_start(out=RCE.ap().rearrange("p a b c d -> p (a b c d)"), in_=rce_d[:, :]),
          writes=["RCE"], dma_key="ld_rce")
    S.add("dve", lambda e: e.tensor_copy(out=ROTB[:, :], in_=ROTF[:, :]), reads=["ROTF"], writes=["ROTB"])
    S.add("sp", lambda e: e.dma_start(out=ag_in[:, :], in_=ATK[0:2, :]), reads=[("ATK", cb) for cb in range(24)], writes=["ag_in"], dma_key="st_ag")
    S.add("pool", lambda e: e.collective_compute("AllGather", ALU.bypass, replica_groups=[[0, 1], [2, 3], [4, 5], [6, 7]],
                                                 ins=[ag_in.ap().opt()], outs=[ag_out.ap().opt()]),
          reads=["ag_in"], writes=["ag_out"], dma_key="cc_ada", inc=1)
    def evac(out_ap, in_ap, reads, writes, excl):
        cnt["ev"] += 1
        if cnt["ev"] % 2 == 0:
            S.add("act", lambda e: e.copy(out=out_ap, in_=in_ap), reads=reads, writes=writes, excl=excl)
        else:
            S.add("dve", lambda e: e.tensor_copy(out=out_ap, in_=in_ap), reads=reads, writes=writes, excl=excl)

    for tb in range(21):
        st = XST[tb % 2]
        stk = ("XST", tb % 2)
        if tb < 20:
            src = xt[tb * 128:(tb + 1) * 128, :]
            npart = 128
        else:
            src = xh[0:64, :]
            npart = 64
        S.add("sp", lambda e, st=st, src=src, npart=npart: e.dma_start(out=st[0:npart, :], in_=src), writes=[stk], dma_key=stk)
        for half in range(2):
            b = nb()
            for ci in range(4):
                c = half * 4 + ci
                S.add("pe", lambda e, b=b, ci=ci, c=c, st=st, npart=npart: e.transpose(
                    PS[:, b, ci * 128:ci * 128 + npart], st[0:npart, c * 128:(c + 1) * 128], IDENT[0:npart, 0:npart]),
                    reads=[stk, "IDENT"], excl=[pk(b)])
            if tb < 16:
                dst = X[:, half * 4:half * 4 + 4, tb * 128:(tb + 1) * 128]
                wk = [("X", tb // 4, half * 4 + ci) for ci in range(4)]
            elif tb < 20:
                dst = XP[:, half * 4:half * 4 + 4, (tb - 16) * 128:(tb - 15) * 128]
                wk = [("X", 4, half * 4 + ci) for ci in range(4)]
            else:
                dst = XH[:, half * 4:half * 4 + 4, 0:64]
                wk = ["XH"]
            src_ps = PS[:, b, 0:512].rearrange("p (c t) -> p c t", c=4)[:, :, 0:npart]
            evac(dst, src_ps, [], wk, [pk(b)])

    S.add("sp", lambda e: e.dma_start(out=AGS[0:2, :, :], in_=ag_out.ap().rearrange("(r v) j -> v r j", v=2)), reads=["ag_out"], writes=["AGS"], dma_key="ld_ags")
    for job in jobs_l0b:
        job()
    for l in range(2):
        for i0 in range(0, 48, 4):
            b2 = nb()
            for ii in range(4):
                j0 = (i0 + ii) * 128
                S.add("pe", lambda e, b2=b2, ii=ii, l=l, j0=j0: e.matmul(PS[:, b2, ii * 2:ii * 2 + 2], lhsT=AGS[0:2, l, j0:j0 + 128], rhs=IDENT[0:2, 0:2],
                                                                       start=True, stop=True),
                      reads=["AGS", "IDENT"], excl=[pk(b2)])
            S.add("act", lambda e, b2=b2, l=l, i0=i0: e.copy(out=ADA[:, l, i0:i0 + 4, :], in_=PS[:, b2, 0:8].rearrange("p (i v) -> p i v", v=2)),
                  writes=[("ADA", l)], excl=[pk(b2)])
    for l in range(2):
        for v in range(2):
            S.add("dve", lambda e, l=l, v=v: e.tensor_tensor(out=ADA[:, l, :, v], in0=ADA[:, l, :, v],
                                                            in1=PRM[:, P_BADA + l * 48:P_BADA + (l + 1) * 48], op=ALU.add),
                  reads=["PRM"], writes=[("ADA", l)])
            for j in range(2):
                S.add("dve", lambda e, l=l, v=v, j=j: e.scalar_tensor_tensor(
                    out=AM[:, l, j, :, v], in0=ADA[:, l, 8 + 24 * j:16 + 24 * j, v], scalar=1.0,
                    in1=PRM[:, P_NG + (l * 2 + j) * 8:P_NG + (l * 2 + j) * 8 + 8], op0=ALU.add, op1=ALU.mult),
                    reads=["PRM", ("ADA", l)], writes=["AM"])

    S.barrier(skip_dma=lambda k: k == "cv")

    def xs(ti, c, c0, c1):
        if ti < 4:
            return X[:, c, ti * 512 + c0:ti * 512 + c1]
        return XP[:, c, c0:c1]

    def stats_begin():
        b = nb()
        bank_state["pinned"].add(b)
        return {"b": b, "pend": [], "n": 0}

    def stats_chunk(st, src_ap, src_key, c):
        q = SQR[:, c % 3, 0:512]
        qk = ("SQR", c % 3)
        S.add("act", lambda e: e.activation(out=q, in_=src_ap, func=AF.Square), reads=[src_key], writes=[qk])
        st["pend"].append((q, qk))

    def stats_flush(st, keep=0):
        while len(st["pend"]) > keep:
            q, qk = st["pend"].pop(0)
            n_ = st["n"]
            b = st["b"]
            S.add("pe", lambda e, q=q, n_=n_, b=b: e.matmul(PS[:, b, :], lhsT=ONESD[:, :], rhs=q, start=(n_ == 0), stop=(n_ == 7)),
                  reads=[qk], excl=[pk(b)])
            st["n"] += 1

    def norm_mod(src, srck, ncols, acol, bcol, dcol0, D_scale_tile, hk, pre=None):
        if pre is not None:
            stats_flush(pre, 0)
            b = pre["b"]
            bank_state["pinned"].discard(b)
        else:
            b = nb()
        for c in range(8 if pre is None else 0):
            q = SQR[:, c % 3, 0:ncols]
            qk = ("SQR", c % 3)
            S.add("act", lambda e, c=c, q=q: e.activation(out=q, in_=src(c), func=AF.Square), reads=[srck(c)], writes=[qk])
            S.add("pe", lambda e, c=c, q=q, b=b: e.matmul(PS[:, b, 0:ncols], lhsT=D_scale_tile[:, :], rhs=q, start=(c == 0), stop=(c == 7)),
                  reads=[qk], excl=[pk(b)])
        rs = RSTD[:, 0:ncols]
        S.add("act", lambda e: e.activation(out=rs, in_=PS[:, b, 0:ncols], func=AF.Ln, bias=EPST[:, 0:1]), writes=["RSTD"], excl=[pk(b)])
        S.add("act", lambda e: e.activation(out=rs, in_=rs, func=AF.Exp, scale=-0.5), writes=["RSTD"])
        for c in range(8):
            t = TMPM[:, c % 2, 0:ncols]
            tk = ("TMPM", c % 2)
            S.add("dve", lambda e, c=c, t=t: e.scalar_tensor_tensor(out=t, in0=src(c), scalar=acol(c), in1=rs, op0=ALU.mult, op1=ALU.mult),
                  reads=[srck(c), "RSTD", "AM"], writes=[tk])
            S.add("act", lambda e, c=c, t=t: e.activation(out=Hb[:, c, dcol0:dcol0 + ncols], in_=t, func=AF.Identity, bias=bcol(c)),
                  reads=[tk, ("ADA", 0), ("ADA", 1)], writes=[(hk, c)])

    def ffn(ti, l, v):
        g2 = lambda m: ADA[:, l, 40 + m, v:v + 1]
        hr = [("H", kc) for kc in range(8)]
        stf = None
        for (f0, nf) in ((0, 12), (12, 10)):
            if f0 > 0:
                stf = stats_begin()
            for i in range(f0 // 2, (f0 + nf) // 2):
                wa, wak = wload(sc_fin[l], K_fin[l], 0, 8, i * 256, 256)
                wb, wbk = wload(sc_fin[l], K_fin[l], 0, 8, DFF + i * 256, 256)
                for f2 in range(2):
                    fl = 2 * i + f2 - f0
                    ba = nb()
                    mm(PS[:, ba, :], [(wa[:, kc, f2 * 128:(f2 + 1) * 128], Hb[:, kc, 0:512]) for kc in range(8)], [wak] + hr, ba)
                    bb = nb()
                    mm(PS[:, bb, :], [(wb[:, kc, f2 * 128:(f2 + 1) * 128], Hb[:, kc, 0:512]) for kc in range(8)], [wbk] + hr, bb)
                    sa = SA[:, fl % 2, :]
                    sak = ("SA", fl % 2)
                    S.add("act", lambda e, sa=sa, ba=ba: e.activation(out=sa, in_=PS[:, ba, :], func=AF.Silu), writes=[sak], excl=[pk(ba)])
                    S.add("dve", lambda e, sa=sa, bb=bb, fl=fl: e.tensor_tensor(out=G[:, fl, :], in0=PS[:, bb, :], in1=sa, op=ALU.mult),
                          reads=[sak], writes=[("G", fl)], excl=[pk(bb)])
            h1 = nf // 2
            for mp in range(4):
                w1, w1k = wload(sc_fout[l], K_fout[l], f0 * 128, h1, mp * 256, 256)
                w2, w2k = wload(sc_fout[l], K_fout[l], (f0 + h1) * 128, nf - h1, mp * 256, 256)
                for m2 in range(2):
                    m = mp * 2 + m2
                    bo = nb()
                    pairs = [(w1[:, kc, m2 * 128:(m2 + 1) * 128], G[:, kc, :]) for kc in range(h1)]
                    pairs += [(w2[:, kc, m2 * 128:(m2 + 1) * 128], G[:, h1 + kc, :]) for kc in range(nf - h1)]
                    mm(PS[:, bo, :], pairs, [w1k, w2k] + [("G", kc) for kc in range(nf)], bo)
                    S.add("dve", lambda e, m=m, bo=bo: e.scalar_tensor_tensor(out=xs(ti, m, 0, 512), in0=PS[:, bo, :], scalar=g2(m),
                                                                              in1=xs(ti, m, 0, 512), op0=ALU.mult, op1=ALU.add),
                          reads=[("ADA", l)], writes=[("X", ti, m)], excl=[pk(bo)])
                    if f0 > 0:
                        stats_chunk(stf, xs(ti, m, 0, 512), ("X", ti, m), m)
                        stats_flush(stf, keep=2)
        return stf

    def layer0(ti):
        sample = ti < 4
        v = 1 if sample else 0
        segs = [(0, 512)] if sample else [(0, 256), (256, 256)]
        l = 0
        xk = lambda c: ("X", ti, c)
        norm_mod(lambda c: xs(ti, c, 0, 512), xk, 512, lambda c: AM[:, l, 0, c, v:v + 1], lambda c: ADA[:, l, 0 + c, v:v + 1], 0, ONESD, "H")
        if sample:
            norm_mod(lambda c: XH[:, c, ti * 16:(ti + 1) * 16], lambda c: "XH", 16, lambda c: AM[:, l, 0, c, v:v + 1],
                     lambda c: ADA[:, l, 0 + c, v:v + 1], 512, ONESD, "HH")
        hreads = [("H", kc) for kc in range(8)]
        hhreads = [("HH", kc) for kc in range(8)]
        bh = None
        if sample:
            bh = nb()
            bank_state["pinned"].add(bh)
        halo_slot = {}
        for i_, ch in enumerate([0, 1, 2, 3, 8, 9, 10, 11, 12, 13, 14, 15]):
            halo_slot[ch] = i_

        wunits = {}

        def win_halo(ch):
            w_, wk = wunits[ch // 2]
            c2 = ch % 2
            hs = halo_slot[ch]
            mm(PS[:, bh, hs * 16:(hs + 1) * 16], [(w_[:, kc, c2 * 128:(c2 + 1) * 128], Hb[:, kc, 512:528]) for kc in range(8)],
               [wk] + hhreads, bh)

        def win_chunk(ch, halo=True):
            u = ch // 2
            if u not in wunits:
                wunits[u] = wload(sc_win, K_win, 0, 8, u * 256, 256)
            w_, wk = wunits[u]
            c2 = ch % 2
            b = nb()
            mm(PS[:, b, :], [(w_[:, kc, c2 * 128:(c2 + 1) * 128], Hb[:, kc, 0:512]) for kc in range(8)], [wk] + hreads, b)
            if halo and sample and ch in halo_slot:
                win_halo(ch)
            return b

        ubanks = [win_chunk(g, halo=False) for g in range(4)]
        if sample:
            for g in range(4):
                win_halo(g)
        for bb_ in ubanks:
            bank_state["pinned"].add(bb_)
        if sample:
            S.add("act", lambda e: e.copy(out=HB[:, 0:64], in_=PS[:, bh, 0:64]), writes=["HBu"], excl=[pk(bh)])
        def pool_section():
            pw_, pwk = wload(sc_pool, K_pool, 0, 4, 0, 128)
            for g in range(4):
                up = UP[:, g % 2, :]
                upk = ("UP", g % 2)
                bu = ubanks[g]
                nsteps = g + 1
                for si, (c0, L) in enumerate(segs):
                    base = si * (L + 16)
                    S.add("act", lambda e, up=up, bu=bu, base=base, c0=c0, L=L: e.copy(out=up[:, base + 8:base + 8 + L], in_=PS[:, bu, c0:c0 + L]),
                          writes=[upk], excl=[pk(bu)])
                    if sample:
                        S.add("dve", lambda e, up=up, g=g: e.tensor_tensor(out=up[:, 0:8], in0=HB[:, g * 16:g * 16 + 8], in1=MH[:, ti, 0:8], op=ALU.mult),
                              reads=["HBu", "MH"], writes=[upk])
                        S.add("dve", lambda e, up=up, g=g: e.tensor_tensor(out=up[:, 520:528], in0=HB[:, g * 16 + 8:g * 16 + 16], in1=MH[:, ti, 8:16], op=ALU.mult),
                              reads=["HBu", "MH"], writes=[upk])
                    else:
                        S.add("pool", lambda e, up=up, base=base: e.memset(up[:, base:base + 8], 0.0), writes=[upk])
                        S.add("pool", lambda e, up=up, base=base, L=L: e.memset(up[:, base + 8 + L:base + 16 + L], 0.0), writes=[upk])
                bank_state["pinned"].discard(bu)
                for si, (c0, L) in enumerate(segs):
                    base = si * (L + 16)
                    P_ = L + 16
                    cur = up
                    curk = upk
                    curoff = base
                    ln = P_
                    for s_ in range(nsteps):
                        sh = 1 << s_
                        dst = PT[:, s_ % 2, :]
                        dk = ("PT", s_ % 2)
                        nl = ln - sh
                        S.add("dve", lambda e, cur=cur, curoff=curoff, dst=dst, nl=nl, sh=sh: e.tensor_tensor(
                            out=dst[:, 0:nl], in0=cur[:, curoff:curoff + nl], in1=cur[:, curoff + sh:curoff + sh + nl], op=ALU.add),
                            reads=[curk], writes=[dk])
                        cur, curk, curoff, ln = dst, dk, 0, nl
                    w = 2 << g
                    st0 = 8 - w // 2
                    pd = PD[:, g % 2, :]
                    pdk = ("PD", g % 2)
                    S.add("dve", lambda e, cur=cur, st0=st0, L=L, c0=c0, pd=pd, up=up, base=base, w=w: e.scalar_tensor_tensor(
                        out=pd[:, c0:c0 + L], in0=cur[:, st0:st0 + L], scalar=1.0 / w, in1=up[:, base + 8:base + 8 + L], op0=ALU.mult, op1=ALU.subtract),
                        reads=[curk, upk], writes=[pdk])
                    edges = [(0, 0), (L - 8, 8)]
                    if sample:
                        edges = ([(0, 0)] if ti == 0 else []) + ([(L - 8, 8)] if ti == 3 else [])
                    for (e0, r0) in edges:
                        tmp = TMPM[:, 0, 0:8]
                        S.add("dve", lambda e, cur=cur, st0=st0, e0=e0, r0=r0, g=g, si=si, tmp=tmp: e.tensor_tensor(
                            out=tmp, in0=cur[:, st0 + e0:st0 + e0 + 8], in1=RCE[:, ti, g, si, r0:r0 + 8], op=ALU.mult),
                            reads=[curk, "RCE"], writes=[("TMPM", 0)])
                        S.add("dve", lambda e, tmp=tmp, pd=pd, c0=c0, e0=e0, up=up, base=base: e.tensor_tensor(
                            out=pd[:, c0 + e0:c0 + e0 + 8], in0=tmp, in1=up[:, base + 8 + e0:base + 16 + e0], op=ALU.subtract),
                            reads=[("TMPM", 0), upk], writes=[pdk])
                bp = nb()
                mm(PS[:, bp, :], [(pw_[:, g, :], PD[:, g % 2, :])], [pwk, ("PD", g % 2)], bp)
                S.add("act", lambda e, g=g, bp=bp: e.activation(out=YCAT[:, g, :], in_=PS[:, bp, :], func=AF.Identity, scale=prm(P_PSC + g)),
                      reads=["PRM"], writes=[("G", g)], excl=[pk(bp)])
        for j in range(4):
            if j % 2 == 0:
                wunits.clear()
            if j == 0:
                bcg, bv, bbg = win_chunk(8), win_chunk(12), win_chunk(4)
                for bb_ in (bcg, bv, bbg):
                    bank_state["pinned"].add(bb_)
                pool_section()
                for bb_ in (bcg, bv, bbg):
                    bank_state["pinned"].discard(bb_)
            else:
                bcg, bv, bbg = win_chunk(8 + j), win_chunk(12 + j), win_chunk(4 + j)
            zp = ZP[:, j % 2, :]
            zk = ("ZP", j % 2)
            S.add("act", lambda e, bv=bv: e.copy(out=VSB[:, :], in_=PS[:, bv, :]), writes=["VSB"], excl=[pk(bv)])
            for si, (c0, L) in enumerate(segs):
                base = si * (L + 2)
                S.add("dve", lambda e, zp=zp, base=base, L=L, c0=c0, bcg=bcg: e.tensor_tensor(
                    out=zp[:, base + 1:base + 1 + L], in0=PS[:, bcg, c0:c0 + L], in1=VSB[:, c0:c0 + L], op=ALU.mult),
                    reads=["VSB"], writes=[zk], excl=[pk(bcg)])
                if not sample:
                    S.add("pool", lambda e, zp=zp, base=base: e.memset(zp[:, base:base + 1], 0.0), writes=[zk])
                    S.add("pool", lambda e, zp=zp, base=base, L=L: e.memset(zp[:, base + 1 + L:base + 2 + L], 0.0), writes=[zk])
            if sample:
                pass
            if sample:
                hs_c, hs_v = halo_slot[8 + j], halo_slot[12 + j]
                tz = TMPM[:, 1, 0:16]
                S.add("act", lambda e, hs_c=hs_c: e.copy(out=TMPM[:, 0, 16:32], in_=PS[:, bh, hs_c * 16:hs_c * 16 + 16]),
                      writes=[("TMPM", 0)], excl=[pk(bh)])
                S.add("dve", lambda e, hs_v=hs_v, tz=tz: e.tensor_tensor(out=tz, in0=PS[:, bh, hs_v * 16:hs_v * 16 + 16], in1=TMPM[:, 0, 16:32], op=ALU.mult),
                      reads=[("TMPM", 0)], writes=[("TMPM", 1)], excl=[pk(bh)])
                S.add("dve", lambda e, zp=zp, tz=tz: e.tensor_tensor(out=zp[:, 0:1], in0=tz[:, 7:8], in1=MH[:, ti, 7:8], op=ALU.mult),
                      reads=[("TMPM", 1), "MH"], writes=[zk])
                S.add("dve", lambda e, zp=zp, tz=tz: e.tensor_tensor(out=zp[:, 513:514], in0=tz[:, 8:9], in1=MH[:, ti, 8:9], op=ALU.mult),
                      reads=[("TMPM", 1), "MH"], writes=[zk])
            for si, (c0, L) in enumerate(segs):
                base = si * (L + 2)
                acc = ACC[:, c0:c0 + L]
                S.add("act", lambda e, zp=zp, base=base, L=L, acc=acc, j=j: e.activation(
                    out=acc, in_=zp[:, base:base + L], func=AF.Identity, scale=prm(P_CW + 0 * 4 + j), bias=prm(P_CB + j)),
                    reads=[zk, "PRM"], writes=["ACC"])
                for k in (1, 2):
                    S.add("dve", lambda e, zp=zp, base=base, L=L, acc=acc, j=j, k=k: e.scalar_tensor_tensor(
                        out=acc, in0=zp[:, base + k:base + k + L], scalar=prm(P_CW + k * 4 + j), in1=acc, op0=ALU.mult, op1=ALU.add),
                        reads=[zk, "PRM"], writes=["ACC"])
            S.add("dve", lambda e, j=j, bbg=bbg: e.tensor_tensor(out=YCAT[:, 4 + j, :], in0=PS[:, bbg, :], in1=ACC[:, :], op=ALU.mult),
                  reads=["ACC"], writes=[("G", 4 + j)], excl=[pk(bbg)])
        if sample:
            bank_state["pinned"].discard(bh)
        st2 = stats_begin()
        for m in range(8):
            if m % 2 == 0:
                w_, wk = wload(sc_wout, K_wout, 0, 8, (m // 2) * 256, 256)
            b = nb()
            mm(PS[:, b, :], [(w_[:, kc, (m % 2) * 128:(m % 2 + 1) * 128], YCAT[:, kc, :]) for kc in range(8)], [wk] + [("G", kc) for kc in range(8)], b)
            S.add("dve", lambda e, m=m, b=b: e.scalar_tensor_tensor(out=xs(ti, m, 0, 512), in0=PS[:, b, :], scalar=ADA[:, l, 16 + m, v:v + 1],
                                                                    in1=xs(ti, m, 0, 512), op0=ALU.mult, op1=ALU.add),
                  reads=[("ADA", l)], writes=[("X", ti, m)], excl=[pk(b)])
            stats_chunk(st2, xs(ti, m, 0, 512), ("X", ti, m), m)
            stats_flush(st2, keep=2)
        norm_mod(lambda c: xs(ti, c, 0, 512), xk, 512, lambda c: AM[:, l, 1, c, v:v + 1], lambda c: ADA[:, l, 24 + c, v:v + 1], 0, ONESD, "H", pre=st2)
        return ffn(ti, l, v)

    RS2 = sb("RS2", [128, 512], F32, at=OFF["G"])
    KN2 = sb("KN2", [128, 512], F32, at=OFF["G"] + 2048)
    U12 = sb("U12", [128, 512], F32, at=OFF["G"] + 4096)
    U22 = sb("U22", [128, 512], BF16, at=OFF["G"] + 6144)
    HSET = [
        dict(sq=SQR[:, 0, 0:512], sqk=("SQR", 0), rs=RS, rsk="RS", kn=KN, knk="KN", u1=U1, u1k="U1", u2=U2, u2k="U2", extra=[]),
        dict(sq=SQR[:, 1, 0:512], sqk=("SQR", 1), rs=RS2, rsk="RS2", kn=KN2, knk="KN2", u1=U12, u1k="U12", u2=U22, u2k="U22",
             extra=[("G", i) for i in range(7)]),
    ]

    def head_norm_stages(mm_fn, gain_col, out_ap, out_key, rope, par):
        hs = HSET[par]
        st = {}

        def A():
            bq = nb()
            st["bq"] = bq
            bank_state["pinned"].add(bq)
            mm_fn(bq)
            S.add("act", lambda e: e.activation(out=hs["sq"], in_=PS[:, bq, :], func=AF.Square), writes=[hs["sqk"]], excl=[pk(bq)])

        def B():
            bq = st["bq"]
            b2 = nb()
            mm(PS[:, b2, :], [(ONESH[:, :], hs["sq"])], [hs["sqk"], "ONESH"], b2)
            rs = hs["rs"]
            xk = hs["extra"]
            S.add("act", lambda e: e.activation(out=rs[:, :], in_=PS[:, b2, :], func=AF.Ln, bias=EPST[:, 0:1]), writes=[hs["rsk"]] + xk, excl=[pk(b2)])
            S.add("act", lambda e: e.activation(out=rs[:, :], in_=rs[:, :], func=AF.Exp, scale=-0.5), writes=[hs["rsk"]])
            bank_state["pinned"].discard(bq)
            if rope is None:
                S.add("dve", lambda e: e.scalar_tensor_tensor(out=out_ap, in0=PS[:, bq, :], scalar=gain_col, in1=rs[:, :], op0=ALU.mult, op1=ALU.mult),
                      reads=[hs["rsk"], "PRM"], writes=[out_key], excl=[pk(bq)])
                return
            kn, u1, u2 = hs["kn"], hs["u1"], hs["u2"]
            S.add("dve", lambda e: e.scalar_tensor_tensor(out=kn[:, :], in0=PS[:, bq, :], scalar=gain_col, in1=rs[:, :], op0=ALU.mult, op1=ALU.mult),
                  reads=[hs["rsk"], "PRM"], writes=[hs["knk"]] + xk, excl=[pk(bq)])
            S.add("pool", lambda e: e.tensor_tensor(out=u1[:, :], in0=kn[:, :], in1=CS[:, 0, :], op=ALU.mult), reads=[hs["knk"], "CS"], writes=[hs["u1k"]] + xk)
            S.add("dve", lambda e: e.tensor_tensor(out=u2[:, :], in0=kn[:, :], in1=CS[:, 1, :], op=ALU.mult), reads=[hs["knk"], "CS"], writes=[hs["u2k"]] + xk)

        def C():
            if rope is None:
                return
            u1, u2 = hs["u1"], hs["u2"]
            b3 = nb()
            mm(PS[:, b3, :], [(ROTB[:, :], u2[:, :])], [hs["u2k"], "ROTB"], b3)
            S.add("dve", lambda e: e.tensor_tensor(out=out_ap, in0=PS[:, b3, :], in1=u1[:, :], op=ALU.add), reads=[hs["u1k"]], writes=[out_key], excl=[pk(b3)])

        return A, B, C

    def run_head_pipeline(stages):
        n = len(stages)
        stages[0][0]()
        if n > 1:
            stages[1][0]()
        stages[0][1]()
        for h in range(n):
            if h + 2 < n:
                stages[h + 2][0]()
            if h + 1 < n:
                stages[h + 1][1]()
            stages[h][2]()

    def l1_norm1(ti, v, pre=None):
        norm_mod(lambda c: xs(ti, c, 0, 512), lambda c: ("X", ti, c), 512, lambda c: AM[:, 1, 0, c, v:v + 1],
                 lambda c: ADA[:, 1, 0 + c, v:v + 1], 0, ONESD, "H", pre=pre)

    def load_cs(ti):
        S.add("pool", lambda e: e.dma_start(out=CS[:, :, :], in_=cs_d[:, ti, :, :]), writes=["CS"], dma_key="ld_cs")

    def kv_stage(ti, pre=None):
        sample = ti < 4
        v = 1 if sample else 0
        hreads = [("H", kc) for kc in range(8)]
        l1_norm1(ti, v, pre)
        if sample:
            load_cs(ti)
        wk_, wkk = wload(sc_qkv, K_qkv, 0, 8, 1024, 256)
        if sample:
            stages = []
            for kvh in range(2):
                def mmf(bq, kvh=kvh):
                    mm(PS[:, bq, :], [(wk_[:, kc, kvh * 128:(kvh + 1) * 128], Hb[:, kc, 0:512]) for kc in range(8)], [wkk] + hreads, bq)
                stages.append(head_norm_stages(mmf, prm(P_KG), KST[:, kvh, :], ("KST", kvh), True, kvh % 2))
            run_head_pipeline(stages)
        for kvh in range(2):
            if sample:
                break
            b = nb()
            mm(PS[:, b, :], [(wk_[:, kc, kvh * 128:(kvh + 1) * 128], Hb[:, kc, 0:512]) for kc in range(8)], [wkk] + hreads, b)
            if sample:
                pass
            else:
                S.add("act", lambda e, b=b: e.activation(out=SQR[:, 0, 0:512], in_=PS[:, b, :], func=AF.Square), writes=[("SQR", 0)], excl=[pk(b)])
                b2 = nb()
                mm(PS[:, b2, :], [(ONESH[:, :], SQR[:, 0, 0:512])], [("SQR", 0), "ONESH"], b2)
                S.add("act", lambda e, b2=b2: e.activation(out=RS[:, :], in_=PS[:, b2, :], func=AF.Ln, bias=EPST[:, 0:1]), writes=["RS"], excl=[pk(b2)])
                S.add("act", lambda e: e.activation(out=RS[:, :], in_=RS[:, :], func=AF.Exp, scale=-0.5), writes=["RS"])
                S.add("dve", lambda e, b=b: e.scalar_tensor_tensor(out=KN[:, :], in0=PS[:, b, :], scalar=prm(P_KG), in1=RS[:, :], op0=ALU.mult, op1=ALU.mult),
                      reads=["RS", "PRM"], writes=["KN"], excl=[pk(b)])
                S.add("act", lambda e, kvh=kvh: e.copy(out=PKT[:, kvh, :], in_=KN[:, :]), reads=["KN"], writes=[("PKT", kvh)])
                b3 = nb()
                for tt in range(4):
                    S.add("pe", lambda e, tt=tt, b3=b3: e.transpose(PS[:, b3, tt * 128:(tt + 1) * 128], KN[:, tt * 128:(tt + 1) * 128], IDENT[:, :]),
                          reads=["KN", "IDENT"], excl=[pk(b3)])
                S.add("act", lambda e, kvh=kvh, b3=b3: e.copy(out=NKST[:, :, kvh * 128:(kvh + 1) * 128],
                                                              in_=PS[:, b3, :].rearrange("p (t d) -> p t d", t=4)),
                      writes=[("SA", 0), ("SA", 1), "NKST"], excl=[pk(b3)])
        if sample:
            S.add("pool", lambda e: e.dma_start(out=kvx_in[:, 0:4096].rearrange("p (k j t) -> p k j t", k=2, j=4)[:, :, ti, :], in_=KST[:, :, :]),
                  reads=[("KST", 0), ("KST", 1)], writes=["kvx_in"], dma_key="st_k")
        else:
            S.add("pool", lambda e: e.dma_start(out=nk_d.ap().rearrange("(t p) c -> p t c", p=128), in_=NKST[:, :, :]),
                  reads=["NKST", ("SA", 0), ("SA", 1)], dma_key="st_nk")
        wv_, wvk = wload(sc_qkv, K_qkv, 0, 8, 1280, 256)
        for tp in range(2):
            b = nb()
            for t2 in range(2):
                tt = tp * 2 + t2
                mm(PS[:, b, t2 * 256:(t2 + 1) * 256], [(Hb[:, kc, tt * 128:(tt + 1) * 128], wv_[:, kc, :]) for kc in range(8)],
                   [wvk] + hreads, b)
            src = PS[:, b, :].rearrange("p (t c) -> p t c", t=2)
            if sample:
                S.add("act", lambda e, tp=tp, src=src: e.copy(out=VST[:, tp * 2:tp * 2 + 2, :], in_=src), writes=[("VST", tp)], excl=[pk(b)])
            else:
                S.add("act", lambda e, tp=tp, src=src: e.copy(out=NVST[:, tp * 2:tp * 2 + 2, :], in_=src),
                      writes=[("PB", 0), ("PB", 1), ("PB", 2), ("PB", 3), ("NVST", tp)], excl=[pk(b)])
                S.add("dve", lambda e, tp=tp: e.tensor_copy(out=PV[:, tp * 2:tp * 2 + 2, :], in_=NVST[:, tp * 2:tp * 2 + 2, :]),
                      reads=[("NVST", tp), ("PB", 0), ("PB", 1), ("PB", 2), ("PB", 3)], writes=[("PV", tp)])
        if sample:
            S.add("pool", lambda e: e.dma_start(out=kvx_in[:, 4096 + ti * 1024:4096 + (ti + 1) * 1024].rearrange("p (t c) -> p t c", t=4), in_=VST[:, :, :]),
                  reads=[("VST", 0), ("VST", 1)], writes=["kvx_in"], dma_key="st_v")
        else:
            S.add("pool", lambda e: e.dma_start(out=nv_d.ap().rearrange("(t p) c -> p t c", p=128), in_=NVST[:, :, :]),
                  reads=[("NVST", 0), ("NVST", 1), ("PB", 0), ("PB", 1), ("PB", 2), ("PB", 3)], dma_key="st_nv")

    SM_SCALE = float(128 ** -0.5)

    hoisted = set()

    def attention(ti, hoist_next=None):
        sample = ti < 4
        v = 1 if sample else 0
        hreads = [("H", kc) for kc in range(8)]
        if ti not in hoisted:
            l1_norm1(ti, v)
        if sample:
            load_cs(ti)
        stages = []
        wq_cache = {}

        def get_wq(hp):
            if hp not in wq_cache:
                wq_cache[hp] = wload(sc_qkv, K_qkv, 0, 8, hp * 256, 256)
            return wq_cache[hp]
        for h in range(8):
            def mmf(bq, h=h):
                wq_, wqk = get_wq(h // 2)
                h2 = h % 2
                mm(PS[:, bq, :], [(wq_[:, kc, h2 * 128:(h2 + 1) * 128], Hb[:, kc, 0:512]) for kc in range(8)], [wqk] + hreads, bq)
            stages.append(head_norm_stages(mmf, prm(P_QG), QO[:, h, :], ("QO", h), True if sample else None, h % 2))
        run_head_pipeline(stages)
        segs_ = []
        for h in range(8):
            kvh = h // 4
            bo, bl = (0, 1) if h % 2 == 0 else (2, 3)
            if sample:
                segs_.append(dict(h=h, q0=0, qn=512, bo=bo, bl=bl, last=True,
                                  chunks=[(KT[:, kvh, sc * 128:(sc + 1) * 128], V[:, sc, kvh * 128:(kvh + 1) * 128], "KT", "V") for sc in range(36)]))
            else:
                for s_ in range(2):
                    segs_.append(dict(h=h, q0=s_ * 256, qn=256, bo=bo, bl=bl, last=(s_ == 1),
                                      chunks=[(PKT[:, kvh, s_ * 256 + sc * 128:s_ * 256 + (sc + 1) * 128], PV[:, s_ * 2 + sc, kvh * 128:(kvh + 1) * 128],
                                               ("PKT", kvh), ("PV", s_)) for sc in range(2)]))
        items = [(sg, i) for sg in segs_ for i in range(len(sg["chunks"]))]
        NI = len(items)
        sbank = [None] * NI
        AHEAD = 3

        def issue_s(g):
            sg, i = items[g]
            kt_ap, _, kk, _ = sg["chunks"][i]
            b = 4 + g % 4
            sbank[g] = b
            mm(PS[:, b, 0:sg["qn"]], [(kt_ap, QO[:, sg["h"], sg["q0"]:sg["q0"] + sg["qn"]])], [kk, ("QO", sg["h"])], b)

        def make_finalize(h, bo, bl):
            def finalize():
                S.add("dve", lambda e: e.reciprocal(out=RL[:, :], in_=PS[:, bl, :]), writes=[("RL", kk) for kk in range(8)], excl=[pk(bl)])
                S.add("dve", lambda e: e.tensor_tensor(out=QO[:, h, :], in0=PS[:, bo, :], in1=RL[:, :], op=ALU.mult),
                      reads=[("RL", kk) for kk in range(8)], writes=[("QO", h)], excl=[pk(bo)])
            return finalize

        NPAIR = NI // 2

        def issue_s_pair(p):
            for t in range(2):
                g = 2 * p + t
                sg, i = items[g]
                kt_ap, _, kk, _ = sg["chunks"][i]
                b = 4 + (p % 2) * 2 + t
                mm(PS[:, b, 0:sg["qn"]], [(kt_ap, QO[:, sg["h"], sg["q0"]:sg["q0"] + sg["qn"]])], [kk, ("QO", sg["h"])], b)

        for p in range(min(2, NPAIR)):
            issue_s_pair(p)
        pend = []
        fin_pending = []
        for p in range(NPAIR):
            sg, i0 = items[2 * p]
            n = len(sg["chunks"])
            q0, qn, bo, bl = sg["q0"], sg["qn"], sg["bo"], sg["bl"]
            while fin_pending and fin_pending[0][0] <= p:
                fin_pending.pop(0)[1]()
            b0 = 4 + (p % 2) * 2
            k0 = (p % 2) * 2
            pbp = PB[:, k0:k0 + 2, 0:qn]
            S.add("act", lambda e, b0=b0, pbp=pbp, qn=qn: e.activation(out=pbp, in_=PS[:, b0:b0 + 2, 0:qn], func=AF.Exp, scale=SM_SCALE),
                  writes=[("PB", k0), ("PB", k0 + 1)], excl=[pk(b0), pk(b0 + 1)])
            if p + 2 < NPAIR:
                issue_s_pair(p + 2)
            for t in range(2):
                i = i0 + t
                _, v_ap, _, vk = sg["chunks"][i]
                pb = PB[:, k0 + t, 0:qn]
                S.add("pe", lambda e, v_ap=v_ap, pb=pb, i=i, bo=bo, q0=q0, qn=qn, n=n: e.matmul(PS[:, bo, q0:q0 + qn], lhsT=v_ap, rhs=pb, start=(i == 0), stop=(i == n - 1)),
                      reads=[vk, ("PB", k0 + t)], excl=[pk(bo)])
            pj = p % 4
            psm = PSM[:, pj, 0:qn]
            S.add("dve", lambda e, psm=psm, k0=k0, qn=qn: e.tensor_tensor(out=psm, in0=PB[:, k0, 0:qn], in1=PB[:, k0 + 1, 0:qn], op=ALU.add),
                  reads=[("PB", k0), ("PB", k0 + 1)], writes=[("PSM", pj)])
            ip = i0 // 2
            npairs = n // 2
            if npairs % 2 == 1 or npairs < 2:
                pend.append((psm, ("PSM", pj), p, ip == 0, ip == npairs - 1, bl, q0, qn))
            elif ip % 2 == 1:
                qj = (p // 2) % 3
                psq = PSQ[:, qj, 0:qn]
                psm0 = PSM[:, (p - 1) % 4, 0:qn]
                S.add("dve", lambda e, psq=psq, psm0=psm0, psm=psm: e.tensor_tensor(out=psq, in0=psm0, in1=psm, op=ALU.add),
                      reads=[("PSM", (p - 1) % 4), ("PSM", pj)], writes=[("PSQ", qj)])
                pend.append((psq, ("PSQ", qj), p, ip == 1, ip == npairs - 1, bl, q0, qn))
            while pend and (pend[0][2] <= p - 1 or p == NPAIR - 1):
                src_, key_, p_, first_, last_, bl_, q0_, qn_ = pend.pop(0)
                S.add("pe", lambda e, src_=src_, first_=first_, last_=last_, bl_=bl_, q0_=q0_, qn_=qn_: e.matmul(
                    PS[:, bl_, q0_:q0_ + qn_], lhsT=ONES1[:, :], rhs=src_, start=first_, stop=last_),
                      reads=["ONES1", key_], excl=[pk(bl_)])
            if i0 + 1 == n - 1 and sg["last"]:
                if sample:
                    hh = sg["h"]
                    for k in range(8):
                        def piece(k=k, bl=bl):
                            S.add("dve", lambda e: e.reciprocal(out=RL[:, k * 64:(k + 1) * 64], in_=PS[:, bl, k * 64:(k + 1) * 64]),
                                  writes=[("RL", k)], excl=[pk(bl)])
                        fin_pending.append((p + 4 + k, piece))
                    for k2 in range(2):
                        def mulp(k2=k2, bo=bo, hh=hh):
                            S.add("dve", lambda e: e.tensor_tensor(out=QO[:, hh, k2 * 256:(k2 + 1) * 256], in0=PS[:, bo, k2 * 256:(k2 + 1) * 256],
                                                                   in1=RL[:, k2 * 256:(k2 + 1) * 256], op=ALU.mult),
                                  reads=[("RL", kk) for kk in range(8)], writes=[("QO", hh)], excl=[pk(bo)])
                        fin_pending.append((p + 12 + k2, mulp))
                else:
                    fin_pending.append((p + 2, make_finalize(sg["h"], bo, bl)))
        while fin_pending:
            fin_pending.pop(0)[1]()
        sto = stats_begin()
        for m in range(8):
            if m % 2 == 0:
                w_, wk = wload(sc_wo, KL1["wo"], 0, 8, (m // 2) * 256, 256)
            b = nb()
            mm(PS[:, b, :], [(w_[:, kc, (m % 2) * 128:(m % 2 + 1) * 128], QO[:, kc, :]) for kc in range(8)], [wk] + [("QO", kc) for kc in range(8)], b)
            S.add("dve", lambda e, m=m, b=b: e.scalar_tensor_tensor(out=xs(ti, m, 0, 512), in0=PS[:, b, :], scalar=ADA[:, 1, 16 + m, v:v + 1],
                                                                    in1=xs(ti, m, 0, 512), op0=ALU.mult, op1=ALU.add),
                  reads=[("ADA", 1)], writes=[("X", ti, m)], excl=[pk(b)])
            stats_chunk(sto, xs(ti, m, 0, 512), ("X", ti, m), m)
            stats_flush(sto, keep=2)
        norm_mod(lambda c: xs(ti, c, 0, 512), lambda c: ("X", ti, c), 512, lambda c: AM[:, 1, 1, c, v:v + 1],
                 lambda c: ADA[:, 1, 24 + c, v:v + 1], 0, ONESD, "H", pre=sto)
        stf_ = ffn(ti, 1, v)
        stats_flush(stf_, 0)
        b = stf_["b"]
        bank_state["pinned"].discard(b)
        S.add("act", lambda e, b=b: e.activation(out=RSTD[:, 0:512], in_=PS[:, b, :], func=AF.Ln, bias=EPST[:, 0:1]), writes=["RSTD"], excl=[pk(b)])
        S.add("act", lambda e: e.activation(out=RSTD[:, 0:512], in_=RSTD[:, 0:512], func=AF.Exp, scale=-0.5), writes=["RSTD"])
        for c in range(8):
            S.add("dve", lambda e, c=c: e.scalar_tensor_tensor(out=xs(ti, c, 0, 512), in0=xs(ti, c, 0, 512), scalar=prm(P_FG + c), in1=RSTD[:, 0:512],
                                                               op0=ALU.mult, op1=ALU.mult),
                  reads=["RSTD", "PRM"], writes=[("X", ti, c)])
        nt = hoist_next
        sth = stats_begin() if nt is not None else None
        for tt in range(4):
            if nt is not None:
                for c in (2 * tt, 2 * tt + 1):
                    stats_chunk(sth, xs(nt, c, 0, 512), ("X", nt, c), c)
            late = []
            for half in range(2):
                b = nb()
                for ci in range(4):
                    c = half * 4 + ci
                    S.add("pe", lambda e, b=b, ci=ci, c=c, tt=tt: e.transpose(PS[:, b, ci * 128:(ci + 1) * 128], xs(ti, c, tt * 128, (tt + 1) * 128), IDENT[:, :]),
                          reads=[("X", ti, c), "IDENT"], excl=[pk(b)])
                ev = (lambda half=half, b=b: evac(YST[:, half * 512:(half + 1) * 512], PS[:, b, :], [], [("YST", half)], [pk(b)]))
                if nt is not None and tt == 3:
                    late.append(ev)
                else:
                    ev()
            if nt is not None:
                stats_flush(sth, 0)
                if tt == 3:
                    l1_norm1(nt, 1, sth)
                    hoisted.add(nt)
                    for ev in late:
                        ev()
            r0 = ti * 512 + tt * 128
            S.add("pool", lambda e, r0=r0: e.dma_start(out=y_d[r0:r0 + 128, :], in_=YST[:, :]), reads=[("YST", 0), ("YST", 1)], dma_key="st_y")

    for ti in range(4):
        if ti == 1:
            tick["on"] = True
        stx = layer0(ti)
        kv_stage(ti, stx)
    tick["on"] = False
    while jobs_l1:
        jobs_l1.pop(0)()
    S.add("pool", lambda e: e.collective_compute("AllGather", ALU.bypass, replica_groups=[[0, 1], [2, 3], [4, 5], [6, 7]],
                                                 ins=[kvx_in.ap().opt()], outs=[kvx_out.ap().opt()]),
          reads=["kvx_in"], writes=["kvx_out"], dma_key="cc", inc=1)
    stx = layer0(4)
    kv_stage(4, stx)
    attention(4, hoist_next=0)
    S.barrier()
    for r in range(2):
        for kvh in range(2):
            S.add("sp", lambda e, r=r, kvh=kvh: e.dma_start(out=KT[:, kvh, 512 + r * 2048:512 + (r + 1) * 2048],
                                                          in_=kvx_out[r * 128:(r + 1) * 128, kvh * 2048:(kvh + 1) * 2048]),
                  reads=["kvx_out"], writes=["KT"], dma_key=("ld_kt", r, kvh))
        S.add("sp", lambda e, r=r: e.dma_start(out=V[:, 4 + r * 16:4 + (r + 1) * 16, :],
                                               in_=kvx_out[r * 128:(r + 1) * 128, 4096:8192].rearrange("p (t c) -> p t c", t=16)),
              reads=["kvx_out"], writes=["V"], dma_key=("ld_v", r))
    S.add("pool", lambda e: e.dma_start(out=V[:, 0:4, :], in_=cv_d.ap().rearrange("(t p) c -> p t c", p=128)), writes=["V"], dma_key="ld_cv")
    CK2 = YST
    S.add("sp", lambda e: e.dma_start(out=CK2[:, :].rearrange("p (t c) -> p t c", t=4), in_=ck_d.ap().rearrange("(t p) c -> p t c", p=128)),
          writes=[("YST", 0), ("YST", 1)], dma_key="ld_ck2")
    for kvh in range(2):
        b = nb()
        for tt in range(4):
            S.add("pe", lambda e, b=b, tt=tt, kvh=kvh: e.transpose(PS[:, b, tt * 128:(tt + 1) * 128],
                                                                 CK2[:, tt * 256 + kvh * 128:tt * 256 + (kvh + 1) * 128], IDENT[:, :]),
                  reads=[("YST", 0), ("YST", 1), "IDENT"], excl=[pk(b)])
        S.add("act", lambda e, b=b, kvh=kvh: e.copy(out=KT[:, kvh, 0:512], in_=PS[:, b, :]), writes=["KT"], excl=[pk(b)])
    for ti in range(4):
        attention(ti, hoist_next=(ti + 1 if ti < 3 else None))
    S.emit()
    return nc


_CACHE = {}


def _rope_tables(L, grid_w=64, theta=10000.0, axis_dim=64):
    rows = L // grid_w
    r = np.repeat(np.arange(rows), grid_w).astype(np.float32)
    col = np.tile(np.arange(grid_w), rows).astype(np.float32)
    inv = (np.float32(theta) ** (-np.arange(0, axis_dim, 2, dtype=np.float32) / np.float32(axis_dim))).astype(np.float32)
    ar = r[:, None] * inv
    ac = col[:, None] * inv
    ang = np.concatenate([ar, ar, ac, ac], axis=-1)
    return np.cos(ang).astype(np.float32), np.sin(ang).astype(np.float32)


def kernel(x_prompt, x_sample, cache_k, cache_v, c, c_ctx, w_ada, b_ada, norm_g,
           w_in_even, pool_w, pool_scale, conv_w, conv_b, w_out_even,
           w_qkv, q_gain, k_gain, w_o, w_ffn_in, w_ffn_out, final_g):
    f = lambda a: np.ascontiguousarray(np.asarray(a, dtype=np.float32))
    x_prompt, x_sample, cache_k, cache_v, c, c_ctx = map(f, (x_prompt, x_sample, cache_k, cache_v, c, c_ctx))
    w_ada, b_ada, norm_g, w_in_even, pool_w, pool_scale = map(f, (w_ada, b_ada, norm_g, w_in_even, pool_w, pool_scale))
    conv_w, conv_b, w_out_even, w_qkv, q_gain, k_gain = map(f, (conv_w, conv_b, w_out_even, w_qkv, q_gain, k_gain))
    w_o, w_ffn_in, w_ffn_out, final_g = map(f, (w_o, w_ffn_in, w_ffn_out, final_g))
    if "nc" not in _CACHE:
        _CACHE["nc"] = build_program()
    nc = _CACHE["nc"]

    ident = np.eye(128, dtype=np.float32)
    R = np.zeros((128, 128), np.float32)
    for d in range(128):
        if (d % 64) < 32:
            R[d, d + 32] = -1.0
        else:
            R[d, d - 32] = 1.0
    consts0 = np.concatenate([ident, np.ascontiguousarray(R.T)], axis=1)
    wada_flat = [w_ada[0], w_ada[1]]
    cos, sin = _rope_tables(4096)

    def fm(vec):
        return np.ascontiguousarray(vec.reshape(-1, 128).T)

    in_maps = []
    for core in range(8):
        b = core // 2
        s0 = (core % 2) * 2048
        xt = np.concatenate([x_sample[b, s0:s0 + 2048], x_prompt[2 * core:2 * core + 2].reshape(512, D)], axis=0)
        xh = np.zeros((64, D), np.float32)
        mh = np.zeros((4, 16), np.float32)
        for j in range(4):
            a = s0 + 512 * j
            if a - 8 >= 0:
                xh[j * 16:j * 16 + 8] = x_sample[b, a - 8:a]
                mh[j, 0:8] = 1.0
            if a + 520 <= 4096:
                xh[j * 16 + 8:j * 16 + 16] = x_sample[b, a + 512:a + 520]
                mh[j, 8:16] = 1.0
        rce = np.zeros((5, 4, 2, 16), np.float32)
        for ti in range(5):
            for g in range(4):
                w = 2 << g
                for si in range(2):
                    if ti < 4:
                        L, t0, seglen = 4096, s0 + 512 * ti, 512
                    else:
                        L, t0, seglen = 256, 0, 256
                    cols = np.concatenate([np.arange(0, 8), np.arange(seglen - 8, seglen)]) + t0
                    lo = np.clip(cols - w // 2, 0, L)
                    hi = np.clip(cols + w // 2, 0, L)
                    rce[ti, g, si] = (np.float32(1.0) / (hi - lo).astype(np.float32))
        cvec = np.stack([c_ctx, c[b]], axis=0)
        cvec5 = np.concatenate([c_ctx[None, :], c], axis=0)
        sel = np.zeros((128, 2), np.float32)
        sel[0, 0] = 1.0
        sel[1 + b, 1] = 1.0
        consts = np.ascontiguousarray(np.concatenate([consts0, sel], axis=1))
        wada_sh = wada_flat[core % 2]
        prm = np.zeros((128, NP), np.float32)
        prm[:, P_CVEC:P_CVEC + 16] = cvec.reshape(2, 8, 128).transpose(2, 1, 0).reshape(128, 16)
        prm[:, P_BADA:P_BADA + 96] = b_ada.reshape(2, 48, 128).transpose(2, 0, 1).reshape(128, 96)
        prm[:, P_NG:P_NG + 32] = norm_g.reshape(2, 2, 8, 128).transpose(3, 0, 1, 2).reshape(128, 32)
        prm[:, P_PSC:P_PSC + 4] = fm(pool_scale[0])
        prm[:, P_CW:P_CW + 12] = conv_w[0].reshape(3, 4, 128).transpose(2, 0, 1).reshape(128, 12)
        prm[:, P_CB:P_CB + 4] = fm(conv_b[0])
        prm[:, P_QG] = q_gain[0]
        prm[:, P_KG] = k_gain[0]
        prm[:, P_FG:P_FG + 8] = fm(final_g)
        prm[:, P_CV5:P_CV5 + 40] = cvec5.reshape(5, 8, 128).transpose(2, 1, 0).reshape(128, 40)
        cs = np.zeros((128, 4, 2, 512), np.float32)
        for j in range(4):
            cs[:, j, 0, :] = cos[s0 + 512 * j:s0 + 512 * (j + 1)].T
            cs[:, j, 1, :] = sin[s0 + 512 * j:s0 + 512 * (j + 1)].T
        in_maps.append({
            "xt": np.ascontiguousarray(xt), "xh": xh, "params": prm, "consts": consts,
            "mh": np.ascontiguousarray(np.broadcast_to(mh.reshape(1, 64), (128, 64))),
            "rce": np.ascontiguousarray(np.broadcast_to(rce.reshape(1, 640), (128, 640))),
            "cs": cs,
            "ck": np.ascontiguousarray(cache_k[b, 0].reshape(512, 256)),
            "cv": np.ascontiguousarray(cache_v[b, 0].reshape(512, 256)),
            "wada_sh": wada_sh, "w_in_even": w_in_even[0], "pool_w": pool_w[0], "w_out_even": w_out_even[0],
            "w_qkv": w_qkv[0], "w_o": w_o[0], "w_ffn_in": w_ffn_in, "w_ffn_out": w_ffn_out,
        })
    res = run_bass_kernel_spmd(nc, in_maps, core_ids=list(range(8)))
    y_prompt = np.zeros((16, 256, D), np.float32)
    y_sample = np.zeros((4, 4096, D), np.float32)
    nk = np.zeros((16, 1, 256, 2, 128), np.float32)
    nv = np.zeros((16, 1, 256, 2, 128), np.float32)
    for core in range(8):
        r = res.results[core]
        b = core // 2
        s0 = (core % 2) * 2048
        y = np.asarray(r["y"], dtype=np.float32)
        y_sample[b, s0:s0 + 2048] = y[0:2048]
        y_prompt[2 * core:2 * core + 2] = y[2048:2560].reshape(2, 256, D)
        nk[2 * core:2 * core + 2, 0] = np.asarray(r["nk"], dtype=np.float32).reshape(2, 256, 2, 128)
        nv[2 * core:2 * core + 2, 0] = np.asarray(r["nv"], dtype=np.float32).reshape(2, 256, 2, 128)
    return (y_prompt, y_sample, nk, nv)
```

```python
import contextlib
import numpy as np
import concourse.bass as bass
import concourse.mybir as mybir
from concourse.bass_utils import run_bass_kernel_spmd

F32 = mybir.dt.float32
BF16 = mybir.dt.bfloat16
ALU = mybir.AluOpType
AF = mybir.ActivationFunctionType

ENGS = ["pe", "act", "dve", "pool", "sp"]
D = 1024
DFF = 2816
NT = 2560
EPS = 1e-6


class Op:
    __slots__ = ("eng", "fn", "deps", "signal", "sig_idx", "dma_key", "dma_ord", "pos", "inc")

    def __init__(self, eng, fn, dma_key, inc):
        self.eng = eng
        self.fn = fn
        self.deps = set()
        self.signal = False
        self.sig_idx = 0
        self.dma_key = dma_key
        self.dma_ord = 0
        self.pos = 0
        self.inc = inc


class Sched:
    def __init__(self, nc):
        self.nc = nc
        self.ops = {e: [] for e in ENGS}
        self.last_w = {}
        self.readers = {}
        self.dma_count = {}
        self.dma_last = {}

    def add(self, eng, fn, reads=(), writes=(), excl=(), dma_key=None, inc=16, extra=()):
        op = Op(eng, fn, dma_key, inc)
        writes = list(writes) + list(excl)
        deps = set(extra)
        for k in reads:
            w = self.last_w.get(k)
            if w is not None:
                deps.add(w)
        for k in writes:
            w = self.last_w.get(k)
            if w is not None:
                deps.add(w)
            for r in self.readers.get(k, ()):
                deps.add(r)
        op.deps = deps
        for k in writes:
            self.last_w[k] = op
            self.readers[k] = []
        for k in reads:
            if self.last_w.get(k) is not op:
                self.readers.setdefault(k, []).append(op)
        op.pos = len(self.ops[eng])
        self.ops[eng].append(op)
        if dma_key is not None:
            c = self.dma_count.get(dma_key, 0) + 1
            self.dma_count[dma_key] = c
            op.dma_ord = c
            self.dma_last[dma_key] = op
        return op

    def barrier(self, skip_dma=None):
        lasts = [self.ops[e][-1] for e in ENGS if self.ops[e]]
        lasts = [o for o in lasts if not (o.dma_key is not None and skip_dma is not None and skip_dma(o.dma_key))]
        lasts += [o for k, o in self.dma_last.items() if not (skip_dma is not None and skip_dma(k))]
        for e in ENGS:
            self.add(e, None, extra=lasts)

    def emit(self, final_wait_eng="sp"):
        nc = self.nc
        need = {}
        for e in ENGS:
            for op in self.ops[e]:
                best = {}
                dmabest = {}
                for d in op.deps:
                    if d is op:
                        continue
                    if d.dma_key is not None:
                        b = dmabest.get(d.dma_key)
                        if b is None or d.dma_ord > b.dma_ord:
                            dmabest[d.dma_key] = d
                        continue
                    if d.eng == op.eng:
                        if d.eng == "pe":
                            continue
                        if op.pos - d.pos > 2:
                            continue
                    b = best.get(d.eng)
                    if b is None or d.pos > b.pos:
                        best[d.eng] = d
                lst = list(best.values()) + list(dmabest.values())
                for d in lst:
                    d.signal = True
                need[id(op)] = lst
        for e in ENGS:
            k = 0
            for op in self.ops[e]:
                if op.dma_key is None and op.signal:
                    assert op.fn is not None
                    k += 1
                    op.sig_idx = k
        stack = contextlib.ExitStack()
        eng_sem = {e: stack.enter_context(nc.semaphore("s_" + e)) for e in ENGS}
        dma_sem = {}
        dma_inc = {}
        for e in ENGS:
            for op in self.ops[e]:
                if op.dma_key is not None and op.dma_key not in dma_sem:
                    dma_sem[op.dma_key] = stack.enter_context(nc.semaphore("d%d" % len(dma_sem)))
                    dma_inc[op.dma_key] = op.inc

        def emit_engine(eng_name, eng):
            waited = {}
            for op in self.ops[eng_name]:
                for d in need[id(op)]:
                    if d.dma_key is not None:
                        sem = dma_sem[d.dma_key]
                        val = d.dma_ord * dma_inc[d.dma_key]
                        key = ("d", d.dma_key)
                    else:
                        sem = eng_sem[d.eng]
                        val = d.sig_idx
                        key = ("e", d.eng)
                    if waited.get(key, 0) >= val:
                        continue
                    waited[key] = val
                    eng.wait_ge(sem, val)
                if op.fn is None:
                    continue
                ins = op.fn(eng)
                if op.dma_key is not None:
                    ins.then_inc(dma_sem[op.dma_key], dma_inc[op.dma_key])
                elif op.signal:
                    ins.then_inc(eng_sem[eng_name], 1)
            if eng_name == final_wait_eng:
                for k, sem in dma_sem.items():
                    eng.wait_ge(sem, self.dma_count[k] * dma_inc[k])

        with stack:
            with nc.Block() as block:

                @block.tensor
                def _(eng):
                    emit_engine("pe", eng)

                @block.scalar
                def _(eng):
                    emit_engine("act", eng)

                @block.vector
                def _(eng):
                    emit_engine("dve", eng)

                @block.gpsimd
                def _(eng):
                    emit_engine("pool", eng)

                @block.sync
                def _(eng):
                    emit_engine("sp", eng)


P_CVEC = 0
P_BADA = 16
P_NG = 112
P_PSC = 144
P_CW = 148
P_CB = 160
P_QG = 164
P_KG = 165
P_FG = 166
P_CV5 = 176
NP = 216


def build_program():
    nc = bass.Bass("TRN2", target_bir_lowering=False)
    S = Sched(nc)

    def din(name, shape, dt=F32):
        return nc.dram_tensor(name, list(shape), dt, kind="ExternalInput")

    def dout(name, shape, dt=F32):
        return nc.dram_tensor(name, list(shape), dt, kind="ExternalOutput")

    xt = din("xt", [NT, D])
    xh = din("xh", [64, D])
    params = din("params", [128, NP])
    consts = din("consts", [128, 258])
    mh_d = din("mh", [128, 64])
    rce_d = din("rce", [128, 640])
    cs_d = din("cs", [128, 4, 2, 512])
    ck_d = din("ck", [512, 256])
    cv_d = din("cv", [512, 256])
    wada_sh = din("wada_sh", [D, 6144])
    w_in_even = din("w_in_even", [D, 2048])
    pool_w = din("pool_w", [4, 128, 128])
    w_out_even = din("w_out_even", [D, D])
    w_qkv = din("w_qkv", [D, 1536])
    w_o = din("w_o", [D, D])
    w_ffn_in = din("w_ffn_in", [2, D, 2 * DFF])
    w_ffn_out = din("w_ffn_out", [2, DFF, D])
    y_d = dout("y", [NT, D])
    nk_d = dout("nk", [512, 256])
    nv_d = dout("nv", [512, 256])

    def dscr(name, shape, dt=BF16):
        return nc.dram_tensor(name, list(shape), dt)

    sc_win = dscr("sc_win", [D, 2048])
    sc_wout = dscr("sc_wout", [D, D])
    sc_pool = dscr("sc_pool", [512, 128])
    sc_qkv = dscr("sc_qkv", [D, 1536])
    sc_wo = dscr("sc_wo", [D, D])
    sc_fin = [dscr("sc_fin%d" % l, [D, 2 * DFF]) for l in range(2)]
    sc_fout = [dscr("sc_fout%d" % l, [DFF, D]) for l in range(2)]
    ag_in = dscr("ag_in", [2, 6144], F32)
    ag_out = dscr("ag_out", [4, 6144], F32)
    kvx_in = dscr("kvx_in", [128, 8192])
    kvx_out = dscr("kvx_out", [256, 8192])

    off = [16512]
    OFF = {}

    def sb(name, shape, dt, at=None):
        n = int(np.prod(shape[1:])) * (4 if dt == F32 else 2)
        if at is None:
            o = off[0]
            off[0] = (o + n + 31) // 32 * 32
        else:
            o = at
        OFF[name] = o
        return nc.alloc_sbuf_tensor_at(name, list(shape), dt, offset=o)

    X = sb("X", [128, 8, 2048], F32)
    XH = sb("XH", [128, 8, 64], F32)
    PRM = sb("PRM", [128, NP], F32)
    ADA = sb("ADA", [128, 2, 48, 2], F32)
    AM = sb("AM", [128, 2, 2, 8, 2], F32)
    IDENT = sb("IDENT", [128, 128], F32)
    ROTF = sb("ROTF", [128, 128], F32)
    ROTB = sb("ROTB", [128, 128], BF16)
    ONESD = sb("ONESD", [128, 128], BF16)
    ONESH = sb("ONESH", [128, 128], BF16)
    ONES1 = sb("ONES1", [128, 128], BF16)
    EPST = sb("EPST", [128, 1], F32)
    RCE = sb("RCE", [128, 5, 4, 2, 16], F32)
    MH = sb("MH", [128, 4, 16], F32)
    ST = sb("ST", [128, 8, 2], F32)
    work0 = off[0]
    Hb = sb("Hb", [128, 8, 528], BF16)
    SQR = sb("SQR", [128, 3, 528], BF16)
    RSTD = sb("RSTD", [128, 528], F32)
    TMPM = sb("TMPM", [128, 2, 528], F32)
    NSLOT = 5
    WR = [sb("WR%d" % i, [128, 2048], BF16) for i in range(NSLOT)]
    G = sb("G", [128, 12, 512], BF16)
    SA = sb("SA", [128, 2, 512], F32)
    QO = sb("QO", [128, 8, 512], BF16)
    RS = sb("RS", [128, 512], F32)
    KN = sb("KN", [128, 512], F32)
    U1 = sb("U1", [128, 512], F32)
    U2 = sb("U2", [128, 512], BF16)
    CS = sb("CS", [128, 2, 512], F32)
    PB = sb("PB", [128, 4, 512], BF16)
    RL = sb("RL", [128, 512], F32)
    PSM = sb("PSM", [128, 4, 512], BF16)
    PSQ = sb("PSQ", [128, 3, 512], BF16)
    YST = sb("YST", [128, 1024], F32)
    KST = sb("KST", [128, 2, 512], BF16)
    VST = sb("VST", [128, 4, 256], BF16)
    PKT = sb("PKT", [128, 2, 512], BF16)
    PV = sb("PV", [128, 4, 256], BF16)
    m0 = off[0]
    UP = sb("UP", [128, 2, 544], F32)
    PT = sb("PT", [128, 2, 544], F32)
    PD = sb("PD", [128, 2, 512], BF16)
    ZP = sb("ZP", [128, 2, 516], F32)
    VSB = sb("VSB", [128, 512], F32)
    ACC = sb("ACC", [128, 512], F32)
    HB = sb("HB", [128, 192], F32)
    m1 = off[0]
    assert m1 - m0 <= 36864 - 16384, (m1 - m0)
    XP = sb("XP", [128, 8, 512], F32, at=m0 + 36864 - 16384)
    sb_end = m0 + 36864
    assert sb_end <= 229344, sb_end
    YCAT = sb("YCAT", [128, 8, 512], BF16, at=OFF["G"])
    KT = sb("KT", [128, 2, 4608], BF16, at=m0)
    V = sb("V", [128, 36, 256], BF16, at=m0 + 18432)
    NKST = sb("NKST", [128, 4, 256], F32, at=OFF["SA"])
    NVST = sb("NVST", [128, 4, 256], F32, at=OFF["PB"])
    so = [work0]

    def sbs(name, shape, dt):
        n = int(np.prod(shape[1:])) * (4 if dt == F32 else 2)
        o = so[0]
        so[0] = (o + n + 31) // 32 * 32
        assert so[0] <= m1, (name, so[0], m1)
        return nc.alloc_sbuf_tensor_at(name, list(shape), dt, offset=o)

    XST = [sbs("XST%d" % i, [128, 1024], F32) for i in range(2)]
    AST = [sbs("AST%d" % i, [128, 8, 256], F32) for i in range(4)]
    ATK = sbs("ATK", [128, 6144], F32)
    AGS = sbs("AGS", [128, 2, 6144], F32)

    PS = nc.alloc_psum_tensor("PS", [128, 8, 512], F32)

    bank_state = {"next": 0, "pinned": set()}

    def nb():
        while True:
            b = bank_state["next"]
            bank_state["next"] = (b + 1) % 8
            if b not in bank_state["pinned"]:
                return b

    def pk(b):
        return ("ps", b)

    cnt = {"ev": 0, "cast": 0}

    def prm(col, n=1):
        return PRM[:, col:col + n]

    def mm(out_ap, pairs, reads, bank, extra_w=()):
        n = len(pairs)
        chunked = [k for k in reads if isinstance(k, tuple) and k[0] in ("H", "HH", "G", "QO") and isinstance(k[1], int)]
        if len(chunked) == n and n > 1:
            others = [k for k in reads if k not in chunked]
            for i, (l, r) in enumerate(pairs):
                S.add("pe", lambda e, l=l, r=r, i=i: e.matmul(out_ap, lhsT=l, rhs=r, start=(i == 0), stop=(i == n - 1)),
                      reads=(others if i in (0, n - 1) else []) + [chunked[i]], excl=[pk(bank)] + list(extra_w))
            return

        def fn(e):
            ins = None
            for i, (l, r) in enumerate(pairs):
                ins = e.matmul(out_ap, lhsT=l, rhs=r, start=(i == 0), stop=(i == n - 1))
            return ins
        S.add("pe", fn, reads=reads, excl=[pk(bank)] + list(extra_w))

    wstate = {"i": 0}

    def wload(sc, rkeys, k0, nk, c0, ncols):
        i = wstate["i"] % NSLOT
        wstate["i"] += 1
        dst = WR[i][:, 0:nk * ncols].rearrange("p (k c) -> p k c", k=nk)
        src = sc[k0:k0 + nk * 128, c0:c0 + ncols].rearrange("(k p) c -> p k c", p=128)
        rk = [k for (k, r0, r1) in rkeys if r0 < k0 + nk * 128 and r1 > k0]
        wop = S.add("sp", lambda e: e.dma_start(out=dst, in_=src), reads=rk, writes=[("w", i)], dma_key=("w", i))
        conv_tick(wop)
        return dst, ("w", i)

    def convert(name, src_fn, dst, nrows, rows_per, defer=None):
        keys = []
        for r0 in range(0, nrows, rows_per):
            r1 = min(nrows, r0 + rows_per)
            key = ("cv", name, r0)
            job = (lambda r0=r0, r1=r1, key=key: S.add("pool", lambda e: e.dma_start(out=dst[r0:r1, :], in_=src_fn(r0, r1)),
                                                       reads=list(gate["reads"]), writes=[key, "cvchain"], dma_key="cv", extra=list(gate["extra"])))
            if defer is None:
                job()
            else:
                defer.append(job)
            keys.append((key, r0, r1))
        return keys

    jobs_l0b = []
    jobs_l1 = []
    gate = {"reads": [], "extra": []}
    K_win = convert("win", lambda a, b: w_in_even[a:b, :], sc_win, D, 128, jobs_l0b)
    K_pool = convert("pool", lambda a, b: pool_w.ap().rearrange("g p j -> (g p) j")[a:b, :], sc_pool, 512, 512, jobs_l0b)
    K_wout = convert("wout", lambda a, b: w_out_even[a:b, :], sc_wout, D, 256, jobs_l0b)
    K_fin = [None, None]
    K_fout = [None, None]
    K_fin[0] = convert("fin0", lambda a, b: w_ffn_in[0, a:b, :], sc_fin[0], D, 64, jobs_l0b)
    K_fout[0] = convert("fout0", lambda a, b: w_ffn_out[0, a:b, :], sc_fout[0], DFF, 176, jobs_l0b)
    K_qkv = convert("qkv", lambda a, b: w_qkv[a:b, :], sc_qkv, D, 256, jobs_l0b)
    KL1 = {}
    KL1["wo"] = convert("wo", lambda a, b: w_o[a:b, :], sc_wo, D, 256, jobs_l1)
    K_fin[1] = convert("fin1", lambda a, b: w_ffn_in[1, a:b, :], sc_fin[1], D, 64, jobs_l1)
    K_fout[1] = convert("fout1", lambda a, b: w_ffn_out[1, a:b, :], sc_fout[1], DFF, 176, jobs_l1)
    tick = {"on": False, "n": 0}

    def conv_tick(wkey=None):
        if tick["on"] and jobs_l1:
            tick["n"] += 1
            if tick["n"] % 3 == 0:
                gate["extra"] = [wkey] if wkey is not None else []
                jobs_l1.pop(0)()
                gate["extra"] = []

    S.add("sp", lambda e: e.dma_start(out=PRM[:, :], in_=params[:, :]), writes=["PRM"], dma_key="ld_prm")
    S.add("pool", lambda e: e.memset(ONESD[:, :], 1.0 / 1024.0), writes=["ONESD"])
    S.add("pool", lambda e: e.memset(ONESH[:, :], 1.0 / 128.0), writes=["ONESH"])
    S.add("pool", lambda e: e.memset(ONES1[:, :], 1.0), writes=["ONES1"])
    S.add("pool", lambda e: e.memset(EPST[:, :], EPS), writes=["EPST"])
    S.add("act", lambda e: e.activation(out=ST[:, :, :], in_=PRM[:, P_CVEC:P_CVEC + 16].rearrange("p (k v) -> p k v", v=2), func=AF.Silu),
          reads=["PRM"], writes=["ST"])

    for cb in range(24):
        ast = AST[cb % 4]
        ak = ("AST", cb % 4)
        S.add("sp", lambda e, ast=ast, cb=cb: e.dma_start(
            out=ast[:, :, :], in_=wada_sh[:, cb * 256:(cb + 1) * 256].rearrange("(k p) c -> p k c", p=128)),
            writes=[ak], dma_key=ak)
        b = nb()
        mm(PS[0:2, b, 0:256], [(ST[:, kc, :], ast[:, kc, :]) for kc in range(8)], [ak, "ST"], b)
        S.add("dve", lambda e, b=b, cb=cb: e.tensor_copy(out=ATK[0:2, cb * 256:(cb + 1) * 256], in_=PS[0:2, b, 0:256]), writes=[("ATK", cb)], excl=[pk(b)])
    S.add("sp", lambda e: e.dma_start(out=IDENT[:, :], in_=consts[:, 0:128]), writes=["IDENT"], dma_key="ld_id")
    S.add("sp", lambda e: e.dma_start(out=ROTF[:, :], in_=consts[:, 128:256]), writes=["ROTF"], dma_key="ld_rot")
    S.add("sp", lambda e: e.dma_start(out=MH[:, :, :], in_=mh_d.ap().rearrange("p (a b) -> p a b", a=4)), writes=["MH"], dma_key="ld_mh")
    S.add("sp", lambda e: e.dma_start(out=RCE.ap().rearrange("p a b c d -> p (a b c d)"), in_=rce_d[:, :]),
          writes=["RCE"], dma_key="ld_rce")
    S.add("dve", lambda e: e.tensor_copy(out=ROTB[:, :], in_=ROTF[:, :]), reads=["ROTF"], writes=["ROTB"])
    S.add("sp", lambda e: e.dma_start(out=ag_in[:, :], in_=ATK[0:2, :]), reads=[("ATK", cb) for cb in range(24)], writes=["ag_in"], dma_key="st_ag")
    S.add("pool", lambda e: e.collective_compute("AllGather", ALU.bypass, replica_groups=[[0, 1], [2, 3], [4, 5], [6, 7]],
                                                 ins=[ag_in.ap().opt()], outs=[ag_out.ap().opt()]),
          reads=["ag_in"], writes=["ag_out"], dma_key="cc_ada", inc=1)
    def evac(out_ap, in_ap, reads, writes, excl):
        cnt["ev"] += 1
        if cnt["ev"] % 2 == 0:
            S.add("act", lambda e: e.copy(out=out_ap, in_=in_ap), reads=reads, writes=writes, excl=excl)
        else:
            S.add("dve", lambda e: e.tensor_copy(out=out_ap, in_=in_ap), reads=reads, writes=writes, excl=excl)

    for tb in range(21):
        st = XST[tb % 2]
        stk = ("XST", tb % 2)
        if tb < 20:
            src = xt[tb * 128:(tb + 1) * 128, :]
            npart = 128
        else:
            src = xh[0:64, :]
            npart = 64
        S.add("sp", lambda e, st=st, src=src, npart=npart: e.dma_start(out=st[0:npart, :], in_=src), writes=[stk], dma_key=stk)
        for half in range(2):
            b = nb()
            for ci in range(4):
                c = half * 4 + ci
                S.add("pe", lambda e, b=b, ci=ci, c=c, st=st, npart=npart: e.transpose(
                    PS[:, b, ci * 128:ci * 128 + npart], st[0:npart, c * 128:(c + 1) * 128], IDENT[0:npart, 0:npart]),
                    reads=[stk, "IDENT"], excl=[pk(b)])
            if tb < 16:
                dst = X[:, half * 4:half * 4 + 4, tb * 128:(tb + 1) * 128]
                wk = [("X", tb // 4, half * 4 + ci) for ci in range(4)]
            elif tb < 20:
                dst = XP[:, half * 4:half * 4 + 4, (tb - 16) * 128:(tb - 15) * 128]
                wk = [("X", 4, half * 4 + ci) for ci in range(4)]
            else:
                dst = XH[:, half * 4:half * 4 + 4, 0:64]
                wk = ["XH"]
            src_ps = PS[:, b, 0:512].rearrange("p (c t) -> p c t", c=4)[:, :, 0:npart]
            evac(dst, src_ps, [], wk, [pk(b)])

    S.add("sp", lambda e: e.dma_start(out=AGS[0:2, :, :], in_=ag_out.ap().rearrange("(r v) j -> v r j", v=2)), reads=["ag_out"], writes=["AGS"], dma_key="ld_ags")
    for job in jobs_l0b:
        job()
    for l in range(2):
        for i0 in range(0, 48, 4):
            b2 = nb()
            for ii in range(4):
                j0 = (i0 + ii) * 128
                S.add("pe", lambda e, b2=b2, ii=ii, l=l, j0=j0: e.matmul(PS[:, b2, ii * 2:ii * 2 + 2], lhsT=AGS[0:2, l, j0:j0 + 128], rhs=IDENT[0:2, 0:2],
                                                                       start=True, stop=True),
                      reads=["AGS", "IDENT"], excl=[pk(b2)])
            S.add("act", lambda e, b2=b2, l=l, i0=i0: e.copy(out=ADA[:, l, i0:i0 + 4, :], in_=PS[:, b2, 0:8].rearrange("p (i v) -> p i v", v=2)),
                  writes=[("ADA", l)], excl=[pk(b2)])
    for l in range(2):
        for v in range(2):
            S.add("dve", lambda e, l=l, v=v: e.tensor_tensor(out=ADA[:, l, :, v], in0=ADA[:, l, :, v],
                                                            in1=PRM[:, P_BADA + l * 48:P_BADA + (l + 1) * 48], op=ALU.add),
                  reads=["PRM"], writes=[("ADA", l)])
            for j in range(2):
                S.add("dve", lambda e, l=l, v=v, j=j: e.scalar_tensor_tensor(
                    out=AM[:, l, j, :, v], in0=ADA[:, l, 8 + 24 * j:16 + 24 * j, v], scalar=1.0,
                    in1=PRM[:, P_NG + (l * 2 + j) * 8:P_NG + (l * 2 + j) * 8 + 8], op0=ALU.add, op1=ALU.mult),
                    reads=["PRM", ("ADA", l)], writes=["AM"])

    S.barrier(skip_dma=lambda k: k == "cv")

    def xs(ti, c, c0, c1):
        if ti < 4:
            return X[:, c, ti * 512 + c0:ti * 512 + c1]
        return XP[:, c, c0:c1]

    def stats_begin():
        b = nb()
        bank_state["pinned"].add(b)
        return {"b": b, "pend": [], "n": 0}

    def stats_chunk(st, src_ap, src_key, c):
        q = SQR[:, c % 3, 0:512]
        qk = ("SQR", c % 3)
        S.add("act", lambda e: e.activation(out=q, in_=src_ap, func=AF.Square), reads=[src_key], writes=[qk])
        st["pend"].append((q, qk))

    def stats_flush(st, keep=0):
        while len(st["pend"]) > keep:
            q, qk = st["pend"].pop(0)
            n_ = st["n"]
            b = st["b"]
            S.add("pe", lambda e, q=q, n_=n_, b=b: e.matmul(PS[:, b, :], lhsT=ONESD[:, :], rhs=q, start=(n_ == 0), stop=(n_ == 7)),
                  reads=[qk], excl=[pk(b)])
            st["n"] += 1

    def norm_mod(src, srck, ncols, acol, bcol, dcol0, D_scale_tile, hk, pre=None):
        if pre is not None:
            stats_flush(pre, 0)
            b = pre["b"]
            bank_state["pinned"].discard(b)
        else:
            b = nb()
        for c in range(8 if pre is None else 0):
            q = SQR[:, c % 3, 0:ncols]
            qk = ("SQR", c % 3)
            S.add("act", lambda e, c=c, q=q: e.activation(out=q, in_=src(c), func=AF.Square), reads=[srck(c)], writes=[qk])
            S.add("pe", lambda e, c=c, q=q, b=b: e.matmul(PS[:, b, 0:ncols], lhsT=D_scale_tile[:, :], rhs=q, start=(c == 0), stop=(c == 7)),
                  reads=[qk], excl=[pk(b)])
        rs = RSTD[:, 0:ncols]
        S.add("act", lambda e: e.activation(out=rs, in_=PS[:, b, 0:ncols], func=AF.Ln, bias=EPST[:, 0:1]), writes=["RSTD"], excl=[pk(b)])
        S.add("act", lambda e: e.activation(out=rs, in_=rs, func=AF.Exp, scale=-0.5), writes=["RSTD"])
        for c in range(8):
            t = TMPM[:, c % 2, 0:ncols]
            tk = ("TMPM", c % 2)
            S.add("dve", lambda e, c=c, t=t: e.scalar_tensor_tensor(out=t, in0=src(c), scalar=acol(c), in1=rs, op0=ALU.mult, op1=ALU.mult),
                  reads=[srck(c), "RSTD", "AM"], writes=[tk])
            S.add("act", lambda e, c=c, t=t: e.activation(out=Hb[:, c, dcol0:dcol0 + ncols], in_=t, func=AF.Identity, bias=bcol(c)),
                  reads=[tk, ("ADA", 0), ("ADA", 1)], writes=[(hk, c)])

    def ffn(ti, l, v):
        g2 = lambda m: ADA[:, l, 40 + m, v:v + 1]
        hr = [("H", kc) for kc in range(8)]
        stf = None
        for (f0, nf) in ((0, 12), (12, 10)):
            if f0 > 0:
                stf = stats_begin()
            for i in range(f0 // 2, (f0 + nf) // 2):
                wa, wak = wload(sc_fin[l], K_fin[l], 0, 8, i * 256, 256)
                wb, wbk = wload(sc_fin[l], K_fin[l], 0, 8, DFF + i * 256, 256)
                for f2 in range(2):
                    fl = 2 * i + f2 - f0
                    ba = nb()
                    mm(PS[:, ba, :], [(wa[:, kc, f2 * 128:(f2 + 1) * 128], Hb[:, kc, 0:512]) for kc in range(8)], [wak] + hr, ba)
                    bb = nb()
                    mm(PS[:, bb, :], [(wb[:, kc, f2 * 128:(f2 + 1) * 128], Hb[:, kc, 0:512]) for kc in range(8)], [wbk] + hr, bb)
                    sa = SA[:, fl % 2, :]
                    sak = ("SA", fl % 2)
                    S.add("act", lambda e, sa=sa, ba=ba: e.activation(out=sa, in_=PS[:, ba, :], func=AF.Silu), writes=[sak], excl=[pk(ba)])
                    S.add("dve", lambda e, sa=sa, bb=bb, fl=fl: e.tensor_tensor(out=G[:, fl, :], in0=PS[:, bb, :], in1=sa, op=ALU.mult),
                          reads=[sak], writes=[("G", fl)], excl=[pk(bb)])
            h1 = nf // 2
            for mp in range(4):
                w1, w1k = wload(sc_fout[l], K_fout[l], f0 * 128, h1, mp * 256, 256)
                w2, w2k = wload(sc_fout[l], K_fout[l], (f0 + h1) * 128, nf - h1, mp * 256, 256)
                for m2 in range(2):
                    m = mp * 2 + m2
                    bo = nb()
                    pairs = [(w1[:, kc, m2 * 128:(m2 + 1) * 128], G[:, kc, :]) for kc in range(h1)]
                    pairs += [(w2[:, kc, m2 * 128:(m2 + 1) * 128], G[:, h1 + kc, :]) for kc in range(nf - h1)]
                    mm(PS[:, bo, :], pairs, [w1k, w2k] + [("G", kc) for kc in range(nf)], bo)
                    S.add("dve", lambda e, m=m, bo=bo: e.scalar_tensor_tensor(out=xs(ti, m, 0, 512), in0=PS[:, bo, :], scalar=g2(m),
                                                                              in1=xs(ti, m, 0, 512), op0=ALU.mult, op1=ALU.add),
                          reads=[("ADA", l)], writes=[("X", ti, m)], excl=[pk(bo)])
                    if f0 > 0:
                        stats_chunk(stf, xs(ti, m, 0, 512), ("X", ti, m), m)
                        stats_flush(stf, keep=2)
        return stf

    def layer0(ti):
        sample = ti < 4
        v = 1 if sample else 0
        segs = [(0, 512)] if sample else [(0, 256), (256, 256)]
        l = 0
        xk = lambda c: ("X", ti, c)
        norm_mod(lambda c: xs(ti, c, 0, 512), xk, 512, lambda c: AM[:, l, 0, c, v:v + 1], lambda c: ADA[:, l, 0 + c, v:v + 1], 0, ONESD, "H")
        if sample:
            norm_mod(lambda c: XH[:, c, ti * 16:(ti + 1) * 16], lambda c: "XH", 16, lambda c: AM[:, l, 0, c, v:v + 1],
                     lambda c: ADA[:, l, 0 + c, v:v + 1], 512, ONESD, "HH")
        hreads = [("H", kc) for kc in range(8)]
        hhreads = [("HH", kc) for kc in range(8)]
        bh = None
        if sample:
            bh = nb()
            bank_state["pinned"].add(bh)
        halo_slot = {}
        for i_, ch in enumerate([0, 1, 2, 3, 8, 9, 10, 11, 12, 13, 14, 15]):
            halo_slot[ch] = i_

        wunits = {}

        def win_halo(ch):
            w_, wk = wunits[ch // 2]
            c2 = ch % 2
            hs = halo_slot[ch]
            mm(PS[:, bh, hs * 16:(hs + 1) * 16], [(w_[:, kc, c2 * 128:(c2 + 1) * 128], Hb[:, kc, 512:528]) for kc in range(8)],
               [wk] + hhreads, bh)

        def win_chunk(ch, halo=True):
            u = ch // 2
            if u not in wunits:
                wunits[u] = wload(sc_win, K_win, 0, 8, u * 256, 256)
            w_, wk = wunits[u]
            c2 = ch % 2
            b = nb()
            mm(PS[:, b, :], [(w_[:, kc, c2 * 128:(c2 + 1) * 128], Hb[:, kc, 0:512]) for kc in range(8)], [wk] + hreads, b)
            if halo and sample and ch in halo_slot:
                win_halo(ch)
            return b

        ubanks = [win_chunk(g, halo=False) for g in range(4)]
        if sample:
            for g in range(4):
                win_halo(g)
        for bb_ in ubanks:
            bank_state["pinned"].add(bb_)
        if sample:
            S.add("act", lambda e: e.copy(out=HB[:, 0:64], in_=PS[:, bh, 0:64]), writes=["HBu"], excl=[pk(bh)])
        def pool_section():
            pw_, pwk = wload(sc_pool, K_pool, 0, 4, 0, 128)
            for g in range(4):
                up = UP[:, g % 2, :]
                upk = ("UP", g % 2)
                bu = ubanks[g]
                nsteps = g + 1
                for si, (c0, L) in enumerate(segs):
                    base = si * (L + 16)
                    S.add("act", lambda e, up=up, bu=bu, base=base, c0=c0, L=L: e.copy(out=up[:, base + 8:base + 8 + L], in_=PS[:, bu, c0:c0 + L]),
                          writes=[upk], excl=[pk(bu)])
                    if sample:
                        S.add("dve", lambda e, up=up, g=g: e.tensor_tensor(out=up[:, 0:8], in0=HB[:, g * 16:g * 16 + 8], in1=MH[:, ti, 0:8], op=ALU.mult),
                              reads=["HBu", "MH"], writes=[upk])
                        S.add("dve", lambda e, up=up, g=g: e.tensor_tensor(out=up[:, 520:528], in0=HB[:, g * 16 + 8:g * 16 + 16], in1=MH[:, ti, 8:16], op=ALU.mult),
                              reads=["HBu", "MH"], writes=[upk])
                    else:
                        S.add("pool", lambda e, up=up, base=base: e.memset(up[:, base:base + 8], 0.0), writes=[upk])
                        S.add("pool", lambda e, up=up, base=base, L=L: e.memset(up[:, base + 8 + L:base + 16 + L], 0.0), writes=[upk])
                bank_state["pinned"].discard(bu)
                for si, (c0, L) in enumerate(segs):
                    base = si * (L + 16)
                    P_ = L + 16
                    cur = up
                    curk = upk
                    curoff = base
                    ln = P_
                    for s_ in range(nsteps):
                        sh = 1 << s_
                        dst = PT[:, s_ % 2, :]
                        dk = ("PT", s_ % 2)
                        nl = ln - sh
                        S.add("dve", lambda e, cur=cur, curoff=curoff, dst=dst, nl=nl, sh=sh: e.tensor_tensor(
                            out=dst[:, 0:nl], in0=cur[:, curoff:curoff + nl], in1=cur[:, curoff + sh:curoff + sh + nl], op=ALU.add),
                            reads=[curk], writes=[dk])
                        cur, curk, curoff, ln = dst, dk, 0, nl
                    w = 2 << g
                    st0 = 8 - w // 2
                    pd = PD[:, g % 2, :]
                    pdk = ("PD", g % 2)
                    S.add("dve", lambda e, cur=cur, st0=st0, L=L, c0=c0, pd=pd, up=up, base=base, w=w: e.scalar_tensor_tensor(
                        out=pd[:, c0:c0 + L], in0=cur[:, st0:st0 + L], scalar=1.0 / w, in1=up[:, base + 8:base + 8 + L], op0=ALU.mult, op1=ALU.subtract),
                        reads=[curk, upk], writes=[pdk])
                    edges = [(0, 0), (L - 8, 8)]
                    if sample:
                        edges = ([(0, 0)] if ti == 0 else []) + ([(L - 8, 8)] if ti == 3 else [])
                    for (e0, r0) in edges:
                        tmp = TMPM[:, 0, 0:8]
                        S.add("dve", lambda e, cur=cur, st0=st0, e0=e0, r0=r0, g=g, si=si, tmp=tmp: e.tensor_tensor(
                            out=tmp, in0=cur[:, st0 + e0:st0 + e0 + 8], in1=RCE[:, ti, g, si, r0:r0 + 8], op=ALU.mult),
                            reads=[curk, "RCE"], writes=[("TMPM", 0)])
                        S.add("dve", lambda e, tmp=tmp, pd=pd, c0=c0, e0=e0, up=up, base=base: e.tensor_tensor(
                            out=pd[:, c0 + e0:c0 + e0 + 8], in0=tmp, in1=up[:, base + 8 + e0:base + 16 + e0], op=ALU.subtract),
                            reads=[("TMPM", 0), upk], writes=[pdk])
                bp = nb()
                mm(PS[:, bp, :], [(pw_[:, g, :], PD[:, g % 2, :])], [pwk, ("PD", g % 2)], bp)
                S.add("act", lambda e, g=g, bp=bp: e.activation(out=YCAT[:, g, :], in_=PS[:, bp, :], func=AF.Identity, scale=prm(P_PSC + g)),
                      reads=["PRM"], writes=[("G", g)], excl=[pk(bp)])
        for j in range(4):
            if j % 2 == 0:
                wunits.clear()
            if j == 0:
                bcg, bv, bbg = win_chunk(8), win_chunk(12), win_chunk(4)
                for bb_ in (bcg, bv, bbg):
                    bank_state["pinned"].add(bb_)
                pool_section()
                for bb_ in (bcg, bv, bbg):
                    bank_state["pinned"].discard(bb_)
            else:
                bcg, bv, bbg = win_chunk(8 + j), win_chunk(12 + j), win_chunk(4 + j)
            zp = ZP[:, j % 2, :]
            zk = ("ZP", j % 2)
            S.add("act", lambda e, bv=bv: e.copy(out=VSB[:, :], in_=PS[:, bv, :]), writes=["VSB"], excl=[pk(bv)])
            for si, (c0, L) in enumerate(segs):
                base = si * (L + 2)
                S.add("dve", lambda e, zp=zp, base=base, L=L, c0=c0, bcg=bcg: e.tensor_tensor(
                    out=zp[:, base + 1:base + 1 + L], in0=PS[:, bcg, c0:c0 + L], in1=VSB[:, c0:c0 + L], op=ALU.mult),
                    reads=["VSB"], writes=[zk], excl=[pk(bcg)])
                if not sample:
                    S.add("pool", lambda e, zp=zp, base=base: e.memset(zp[:, base:base + 1], 0.0), writes=[zk])
                    S.add("pool", lambda e, zp=zp, base=base, L=L: e.memset(zp[:, base + 1 + L:base + 2 + L], 0.0), writes=[zk])
            if sample:
                pass
            if sample:
                hs_c, hs_v = halo_slot[8 + j], halo_slot[12 + j]
                tz = TMPM[:, 1, 0:16]
                S.add("act", lambda e, hs_c=hs_c: e.copy(out=TMPM[:, 0, 16:32], in_=PS[:, bh, hs_c * 16:hs_c * 16 + 16]),
                      writes=[("TMPM", 0)], excl=[pk(bh)])
                S.add("dve", lambda e, hs_v=hs_v, tz=tz: e.tensor_tensor(out=tz, in0=PS[:, bh, hs_v * 16:hs_v * 16 + 16], in1=TMPM[:, 0, 16:32], op=ALU.mult),
                      reads=[("TMPM", 0)], writes=[("TMPM", 1)], excl=[pk(bh)])
                S.add("dve", lambda e, zp=zp, tz=tz: e.tensor_tensor(out=zp[:, 0:1], in0=tz[:, 7:8], in1=MH[:, ti, 7:8], op=ALU.mult),
                      reads=[("TMPM", 1), "MH"], writes=[zk])
                S.add("dve", lambda e, zp=zp, tz=tz: e.tensor_tensor(out=zp[:, 513:514], in0=tz[:, 8:9], in1=MH[:, ti, 8:9], op=ALU.mult),
                      reads=[("TMPM", 1), "MH"], writes=[zk])
            for si, (c0, L) in enumerate(segs):
                base = si * (L + 2)
                acc = ACC[:, c0:c0 + L]
                S.add("act", lambda e, zp=zp, base=base, L=L, acc=acc, j=j: e.activation(
                    out=acc, in_=zp[:, base:base + L], func=AF.Identity, scale=prm(P_CW + 0 * 4 + j), bias=prm(P_CB + j)),
                    reads=[zk, "PRM"], writes=["ACC"])
                for k in (1, 2):
                    S.add("dve", lambda e, zp=zp, base=base, L=L, acc=acc, j=j, k=k: e.scalar_tensor_tensor(
                        out=acc, in0=zp[:, base + k:base + k + L], scalar=prm(P_CW + k * 4 + j), in1=acc, op0=ALU.mult, op1=ALU.add),
                        reads=[zk, "PRM"], writes=["ACC"])
            S.add("dve", lambda e, j=j, bbg=bbg: e.tensor_tensor(out=YCAT[:, 4 + j, :], in0=PS[:, bbg, :], in1=ACC[:, :], op=ALU.mult),
                  reads=["ACC"], writes=[("G", 4 + j)], excl=[pk(bbg)])
        if sample:
            bank_state["pinned"].discard(bh)
        st2 = stats_begin()
        for m in range(8):
            if m % 2 == 0:
                w_, wk = wload(sc_wout, K_wout, 0, 8, (m // 2) * 256, 256)
            b = nb()
            mm(PS[:, b, :], [(w_[:, kc, (m % 2) * 128:(m % 2 + 1) * 128], YCAT[:, kc, :]) for kc in range(8)], [wk] + [("G", kc) for kc in range(8)], b)
            S.add("dve", lambda e, m=m, b=b: e.scalar_tensor_tensor(out=xs(ti, m, 0, 512), in0=PS[:, b, :], scalar=ADA[:, l, 16 + m, v:v + 1],
                                                                    in1=xs(ti, m, 0, 512), op0=ALU.mult, op1=ALU.add),
                  reads=[("ADA", l)], writes=[("X", ti, m)], excl=[pk(b)])
            stats_chunk(st2, xs(ti, m, 0, 512), ("X", ti, m), m)
            stats_flush(st2, keep=2)
        norm_mod(lambda c: xs(ti, c, 0, 512), xk, 512, lambda c: AM[:, l, 1, c, v:v + 1], lambda c: ADA[:, l, 24 + c, v:v + 1], 0, ONESD, "H", pre=st2)
        return ffn(ti, l, v)

    RS2 = sb("RS2", [128, 512], F32, at=OFF["G"])
    KN2 = sb("KN2", [128, 512], F32, at=OFF["G"] + 2048)
    U12 = sb("U12", [128, 512], F32, at=OFF["G"] + 4096)
    U22 = sb("U22", [128, 512], BF16, at=OFF["G"] + 6144)
    HSET = [
        dict(sq=SQR[:, 0, 0:512], sqk=("SQR", 0), rs=RS, rsk="RS", kn=KN, knk="KN", u1=U1, u1k="U1", u2=U2, u2k="U2", extra=[]),
        dict(sq=SQR[:, 1, 0:512], sqk=("SQR", 1), rs=RS2, rsk="RS2", kn=KN2, knk="KN2", u1=U12, u1k="U12", u2=U22, u2k="U22",
             extra=[("G", i) for i in range(7)]),
    ]

    def head_norm_stages(mm_fn, gain_col, out_ap, out_key, rope, par):
        hs = HSET[par]
        st = {}

        def A():
            bq = nb()
            st["bq"] = bq
            bank_state["pinned"].add(bq)
            mm_fn(bq)
            S.add("act", lambda e: e.activation(out=hs["sq"], in_=PS[:, bq, :], func=AF.Square), writes=[hs["sqk"]], excl=[pk(bq)])

        def B():
            bq = st["bq"]
            b2 = nb()
            mm(PS[:, b2, :], [(ONESH[:, :], hs["sq"])], [hs["sqk"], "ONESH"], b2)
            rs = hs["rs"]
            xk = hs["extra"]
            S.add("act", lambda e: e.activation(out=rs[:, :], in_=PS[:, b2, :], func=AF.Ln, bias=EPST[:, 0:1]), writes=[hs["rsk"]] + xk, excl=[pk(b2)])
            S.add("act", lambda e: e.activation(out=rs[:, :], in_=rs[:, :], func=AF.Exp, scale=-0.5), writes=[hs["rsk"]])
            bank_state["pinned"].discard(bq)
            if rope is None:
                S.add("dve", lambda e: e.scalar_tensor_tensor(out=out_ap, in0=PS[:, bq, :], scalar=gain_col, in1=rs[:, :], op0=ALU.mult, op1=ALU.mult),
                      reads=[hs["rsk"], "PRM"], writes=[out_key], excl=[pk(bq)])
                return
            kn, u1, u2 = hs["kn"], hs["u1"], hs["u2"]
            S.add("dve", lambda e: e.scalar_tensor_tensor(out=kn[:, :], in0=PS[:, bq, :], scalar=gain_col, in1=rs[:, :], op0=ALU.mult, op1=ALU.mult),
                  reads=[hs["rsk"], "PRM"], writes=[hs["knk"]] + xk, excl=[pk(bq)])
            S.add("pool", lambda e: e.tensor_tensor(out=u1[:, :], in0=kn[:, :], in1=CS[:, 0, :], op=ALU.mult), reads=[hs["knk"], "CS"], writes=[hs["u1k"]] + xk)
            S.add("dve", lambda e: e.tensor_tensor(out=u2[:, :], in0=kn[:, :], in1=CS[:, 1, :], op=ALU.mult), reads=[hs["knk"], "CS"], writes=[hs["u2k"]] + xk)

        def C():
            if rope is None:
                return
            u1, u2 = hs["u1"], hs["u2"]
            b3 = nb()
            mm(PS[:, b3, :], [(ROTB[:, :], u2[:, :])], [hs["u2k"], "ROTB"], b3)
            S.add("dve", lambda e: e.tensor_tensor(out=out_ap, in0=PS[:, b3, :], in1=u1[:, :], op=ALU.add), reads=[hs["u1k"]], writes=[out_key], excl=[pk(b3)])

        return A, B, C

    def run_head_pipeline(stages):
        n = len(stages)
        stages[0][0]()
        if n > 1:
            stages[1][0]()
        stages[0][1]()
        for h in range(n):
            if h + 2 < n:
                stages[h + 2][0]()
            if h + 1 < n:
                stages[h + 1][1]()
            stages[h][2]()

    def l1_norm1(ti, v, pre=None):
        norm_mod(lambda c: xs(ti, c, 0, 512), lambda c: ("X", ti, c), 512, lambda c: AM[:, 1, 0, c, v:v + 1],
                 lambda c: ADA[:, 1, 0 + c, v:v + 1], 0, ONESD, "H", pre=pre)

    def load_cs(ti):
        S.add("pool", lambda e: e.dma_start(out=CS[:, :, :], in_=cs_d[:, ti, :, :]), writes=["CS"], dma_key="ld_cs")

    def kv_stage(ti, pre=None):
        sample = ti < 4
        v = 1 if sample else 0
        hreads = [("H", kc) for kc in range(8)]
        l1_norm1(ti, v, pre)
        if sample:
            load_cs(ti)
        wk_, wkk = wload(sc_qkv, K_qkv, 0, 8, 1024, 256)
        if sample:
            stages = []
            for kvh in range(2):
                def mmf(bq, kvh=kvh):
                    mm(PS[:, bq, :], [(wk_[:, kc, kvh * 128:(kvh + 1) * 128], Hb[:, kc, 0:512]) for kc in range(8)], [wkk] + hreads, bq)
                stages.append(head_norm_stages(mmf, prm(P_KG), KST[:, kvh, :], ("KST", kvh), True, kvh % 2))
            run_head_pipeline(stages)
        for kvh in range(2):
            if sample:
                break
            b = nb()
            mm(PS[:, b, :], [(wk_[:, kc, kvh * 128:(kvh + 1) * 128], Hb[:, kc, 0:512]) for kc in range(8)], [wkk] + hreads, b)
            if sample:
                pass
            else:
                S.add("act", lambda e, b=b: e.activation(out=SQR[:, 0, 0:512], in_=PS[:, b, :], func=AF.Square), writes=[("SQR", 0)], excl=[pk(b)])
                b2 = nb()
                mm(PS[:, b2, :], [(ONESH[:, :], SQR[:, 0, 0:512])], [("SQR", 0), "ONESH"], b2)
                S.add("act", lambda e, b2=b2: e.activation(out=RS[:, :], in_=PS[:, b2, :], func=AF.Ln, bias=EPST[:, 0:1]), writes=["RS"], excl=[pk(b2)])
                S.add("act", lambda e: e.activation(out=RS[:, :], in_=RS[:, :], func=AF.Exp, scale=-0.5), writes=["RS"])
                S.add("dve", lambda e, b=b: e.scalar_tensor_tensor(out=KN[:, :], in0=PS[:, b, :], scalar=prm(P_KG), in1=RS[:, :], op0=ALU.mult, op1=ALU.mult),
                      reads=["RS", "PRM"], writes=["KN"], excl=[pk(b)])
                S.add("act", lambda e, kvh=kvh: e.copy(out=PKT[:, kvh, :], in_=KN[:, :]), reads=["KN"], writes=[("PKT", kvh)])
                b3 = nb()
                for tt in range(4):
                    S.add("pe", lambda e, tt=tt, b3=b3: e.transpose(PS[:, b3, tt * 128:(tt + 1) * 128], KN[:, tt * 128:(tt + 1) * 128], IDENT[:, :]),
                          reads=["KN", "IDENT"], excl=[pk(b3)])
                S.add("act", lambda e, kvh=kvh, b3=b3: e.copy(out=NKST[:, :, kvh * 128:(kvh + 1) * 128],
                                                              in_=PS[:, b3, :].rearrange("p (t d) -> p t d", t=4)),
                      writes=[("SA", 0), ("SA", 1), "NKST"], excl=[pk(b3)])
        if sample:
            S.add("pool", lambda e: e.dma_start(out=kvx_in[:, 0:4096].rearrange("p (k j t) -> p k j t", k=2, j=4)[:, :, ti, :], in_=KST[:, :, :]),
                  reads=[("KST", 0), ("KST", 1)], writes=["kvx_in"], dma_key="st_k")
        else:
            S.add("pool", lambda e: e.dma_start(out=nk_d.ap().rearrange("(t p) c -> p t c", p=128), in_=NKST[:, :, :]),
                  reads=["NKST", ("SA", 0), ("SA", 1)], dma_key="st_nk")
        wv_, wvk = wload(sc_qkv, K_qkv, 0, 8, 1280, 256)
        for tp in range(2):
            b = nb()
            for t2 in range(2):
                tt = tp * 2 + t2
                mm(PS[:, b, t2 * 256:(t2 + 1) * 256], [(Hb[:, kc, tt * 128:(tt + 1) * 128], wv_[:, kc, :]) for kc in range(8)],
                   [wvk] + hreads, b)
            src = PS[:, b, :].rearrange("p (t c) -> p t c", t=2)
            if sample:
                S.add("act", lambda e, tp=tp, src=src: e.copy(out=VST[:, tp * 2:tp * 2 + 2, :], in_=src), writes=[("VST", tp)], excl=[pk(b)])
            else:
                S.add("act", lambda e, tp=tp, src=src: e.copy(out=NVST[:, tp * 2:tp * 2 + 2, :], in_=src),
                      writes=[("PB", 0), ("PB", 1), ("PB", 2), ("PB", 3), ("NVST", tp)], excl=[pk(b)])
                S.add("dve", lambda e, tp=tp: e.tensor_copy(out=PV[:, tp * 2:tp * 2 + 2, :], in_=NVST[:, tp * 2:tp * 2 + 2, :]),
                      reads=[("NVST", tp), ("PB", 0), ("PB", 1), ("PB", 2), ("PB", 3)], writes=[("PV", tp)])
        if sample:
            S.add("pool", lambda e: e.dma_start(out=kvx_in[:, 4096 + ti * 1024:4096 + (ti + 1) * 1024].rearrange("p (t c) -> p t c", t=4), in_=VST[:, :, :]),
                  reads=[("VST", 0), ("VST", 1)], writes=["kvx_in"], dma_key="st_v")
        else:
            S.add("pool", lambda e: e.dma_start(out=nv_d.ap().rearrange("(t p) c -> p t c", p=128), in_=NVST[:, :, :]),
                  reads=[("NVST", 0), ("NVST", 1), ("PB", 0), ("PB", 1), ("PB", 2), ("PB", 3)], dma_key="st_nv")

    SM_SCALE = float(128 ** -0.5)

    hoisted = set()
    cs_loaded = set()

    def attention(ti, hoist_next=None):
        sample = ti < 4
        v = 1 if sample else 0
        hreads = [("H", kc) for kc in range(8)]
        if ti not in hoisted:
            l1_norm1(ti, v)
        if sample and ti not in cs_loaded:
            load_cs(ti)
        stages = []
        wq_cache = {}

        def get_wq(hp):
            if hp not in wq_cache:
                wq_cache[hp] = wload(sc_qkv, K_qkv, 0, 8, hp * 256, 256)
            return wq_cache[hp]
        for h in range(8):
            def mmf(bq, h=h):
                wq_, wqk = get_wq(h // 2)
                h2 = h % 2
                mm(PS[:, bq, :], [(wq_[:, kc, h2 * 128:(h2 + 1) * 128], Hb[:, kc, 0:512]) for kc in range(8)], [wqk] + hreads, bq)
            stages.append(head_norm_stages(mmf, prm(P_QG), QO[:, h, :], ("QO", h), True if sample else None, h % 2))
        run_head_pipeline(stages)
        if hoist_next is not None:
            load_cs(hoist_next)
            cs_loaded.add(hoist_next)
        segs_ = []
        for h in range(8):
            kvh = h // 4
            bo, bl = (0, 1) if h % 2 == 0 else (2, 3)
            if sample:
                segs_.append(dict(h=h, q0=0, qn=512, bo=bo, bl=bl, last=True,
                                  chunks=[(KT[:, kvh, sc * 128:(sc + 1) * 128], V[:, sc, kvh * 128:(kvh + 1) * 128], "KT", "V") for sc in range(36)]))
            else:
                for s_ in range(2):
                    segs_.append(dict(h=h, q0=s_ * 256, qn=256, bo=bo, bl=bl, last=(s_ == 1),
                                      chunks=[(PKT[:, kvh, s_ * 256 + sc * 128:s_ * 256 + (sc + 1) * 128], PV[:, s_ * 2 + sc, kvh * 128:(kvh + 1) * 128],
                                               ("PKT", kvh), ("PV", s_)) for sc in range(2)]))
        items = [(sg, i) for sg in segs_ for i in range(len(sg["chunks"]))]
        NI = len(items)
        sbank = [None] * NI
        AHEAD = 3

        def issue_s(g):
            sg, i = items[g]
            kt_ap, _, kk, _ = sg["chunks"][i]
            b = 4 + g % 4
            sbank[g] = b
            mm(PS[:, b, 0:sg["qn"]], [(kt_ap, QO[:, sg["h"], sg["q0"]:sg["q0"] + sg["qn"]])], [kk, ("QO", sg["h"])], b)

        def make_finalize(h, bo, bl):
            def finalize():
                S.add("dve", lambda e: e.reciprocal(out=RL[:, :], in_=PS[:, bl, :]), writes=[("RL", kk) for kk in range(8)], excl=[pk(bl)])
                S.add("dve", lambda e: e.tensor_tensor(out=QO[:, h, :], in0=PS[:, bo, :], in1=RL[:, :], op=ALU.mult),
                      reads=[("RL", kk) for kk in range(8)], writes=[("QO", h)], excl=[pk(bo)])
            return finalize

        NPAIR = NI // 2

        def issue_s_pair(p):
            for t in range(2):
                g = 2 * p + t
                sg, i = items[g]
                kt_ap, _, kk, _ = sg["chunks"][i]
                b = 4 + (p % 2) * 2 + t
                mm(PS[:, b, 0:sg["qn"]], [(kt_ap, QO[:, sg["h"], sg["q0"]:sg["q0"] + sg["qn"]])], [kk, ("QO", sg["h"])], b)

        for p in range(min(2, NPAIR)):
            issue_s_pair(p)
        pend = []
        fin_pending = []
        for p in range(NPAIR):
            sg, i0 = items[2 * p]
            n = len(sg["chunks"])
            q0, qn, bo, bl = sg["q0"], sg["qn"], sg["bo"], sg["bl"]
            while fin_pending and fin_pending[0][0] <= p:
                fin_pending.pop(0)[1]()
            b0 = 4 + (p % 2) * 2
            k0 = (p % 2) * 2
            pbp = PB[:, k0:k0 + 2, 0:qn]
            S.add("act", lambda e, b0=b0, pbp=pbp, qn=qn: e.activation(out=pbp, in_=PS[:, b0:b0 + 2, 0:qn], func=AF.Exp, scale=SM_SCALE),
                  writes=[("PB", k0), ("PB", k0 + 1)], excl=[pk(b0), pk(b0 + 1)])
            if p + 2 < NPAIR:
                issue_s_pair(p + 2)
            for t in range(2):
                i = i0 + t
                _, v_ap, _, vk = sg["chunks"][i]
                pb = PB[:, k0 + t, 0:qn]
                S.add("pe", lambda e, v_ap=v_ap, pb=pb, i=i, bo=bo, q0=q0, qn=qn, n=n: e.matmul(PS[:, bo, q0:q0 + qn], lhsT=v_ap, rhs=pb, start=(i == 0), stop=(i == n - 1)),
                      reads=[vk, ("PB", k0 + t)], excl=[pk(bo)])
            pj = p % 4
            psm = PSM[:, pj, 0:qn]
            S.add("dve", lambda e, psm=psm, k0=k0, qn=qn: e.tensor_tensor(out=psm, in0=PB[:, k0, 0:qn], in1=PB[:, k0 + 1, 0:qn], op=ALU.add),
                  reads=[("PB", k0), ("PB", k0 + 1)], writes=[("PSM", pj)])
            ip = i0 // 2
            npairs = n // 2
            if npairs % 2 == 1 or npairs < 2:
                pend.append((psm, ("PSM", pj), p, ip == 0, ip == npairs - 1, bl, q0, qn))
            elif ip % 2 == 1:
                qj = (p // 2) % 3
                psq = PSQ[:, qj, 0:qn]
                psm0 = PSM[:, (p - 1) % 4, 0:qn]
                S.add("dve", lambda e, psq=psq, psm0=psm0, psm=psm: e.tensor_tensor(out=psq, in0=psm0, in1=psm, op=ALU.add),
                      reads=[("PSM", (p - 1) % 4), ("PSM", pj)], writes=[("PSQ", qj)])
                pend.append((psq, ("PSQ", qj), p, ip == 1, ip == npairs - 1, bl, q0, qn))
            while pend and (pend[0][2] <= p - 1 or p == NPAIR - 1):
                src_, key_, p_, first_, last_, bl_, q0_, qn_ = pend.pop(0)
                S.add("pe", lambda e, src_=src_, first_=first_, last_=last_, bl_=bl_, q0_=q0_, qn_=qn_: e.matmul(
                    PS[:, bl_, q0_:q0_ + qn_], lhsT=ONES1[:, :], rhs=src_, start=first_, stop=last_),
                      reads=["ONES1", key_], excl=[pk(bl_)])
            if i0 + 1 == n - 1 and sg["last"]:
                if sample:
                    hh = sg["h"]
                    for k in range(8):
                        def piece(k=k, bl=bl):
                            S.add("dve", lambda e: e.reciprocal(out=RL[:, k * 64:(k + 1) * 64], in_=PS[:, bl, k * 64:(k + 1) * 64]),
                                  writes=[("RL", k)], excl=[pk(bl)])
                        fin_pending.append((p + 4 + k, piece))
                    for k2 in range(2):
                        def mulp(k2=k2, bo=bo, hh=hh):
                            S.add("dve", lambda e: e.tensor_tensor(out=QO[:, hh, k2 * 256:(k2 + 1) * 256], in0=PS[:, bo, k2 * 256:(k2 + 1) * 256],
                                                                   in1=RL[:, k2 * 256:(k2 + 1) * 256], op=ALU.mult),
                                  reads=[("RL", kk) for kk in range(8)], writes=[("QO", hh)], excl=[pk(bo)])
                        fin_pending.append((p + 12 + k2, mulp))
                else:
                    fin_pending.append((p + 2, make_finalize(sg["h"], bo, bl)))
        while fin_pending:
            fin_pending.pop(0)[1]()
        sto = stats_begin()
        for m in range(8):
            if m % 2 == 0:
                w_, wk = wload(sc_wo, KL1["wo"], 0, 8, (m // 2) * 256, 256)
            b = nb()
            mm(PS[:, b, :], [(w_[:, kc, (m % 2) * 128:(m % 2 + 1) * 128], QO[:, kc, :]) for kc in range(8)], [wk] + [("QO", kc) for kc in range(8)], b)
            S.add("dve", lambda e, m=m, b=b: e.scalar_tensor_tensor(out=xs(ti, m, 0, 512), in0=PS[:, b, :], scalar=ADA[:, 1, 16 + m, v:v + 1],
                                                                    in1=xs(ti, m, 0, 512), op0=ALU.mult, op1=ALU.add),
                  reads=[("ADA", 1)], writes=[("X", ti, m)], excl=[pk(b)])
            stats_chunk(sto, xs(ti, m, 0, 512), ("X", ti, m), m)
            stats_flush(sto, keep=2)
        norm_mod(lambda c: xs(ti, c, 0, 512), lambda c: ("X", ti, c), 512, lambda c: AM[:, 1, 1, c, v:v + 1],
                 lambda c: ADA[:, 1, 24 + c, v:v + 1], 0, ONESD, "H", pre=sto)
        stf_ = ffn(ti, 1, v)
        stats_flush(stf_, 0)
        b = stf_["b"]
        bank_state["pinned"].discard(b)
        S.add("act", lambda e, b=b: e.activation(out=RSTD[:, 0:512], in_=PS[:, b, :], func=AF.Ln, bias=EPST[:, 0:1]), writes=["RSTD"], excl=[pk(b)])
        S.add("act", lambda e: e.activation(out=RSTD[:, 0:512], in_=RSTD[:, 0:512], func=AF.Exp, scale=-0.5), writes=["RSTD"])
        for c in range(8):
            S.add("dve", lambda e, c=c: e.scalar_tensor_tensor(out=xs(ti, c, 0, 512), in0=xs(ti, c, 0, 512), scalar=prm(P_FG + c), in1=RSTD[:, 0:512],
                                                               op0=ALU.mult, op1=ALU.mult),
                  reads=["RSTD", "PRM"], writes=[("X", ti, c)])
        nt = hoist_next
        sth = stats_begin() if nt is not None else None
        for tt in range(4):
            if nt is not None:
                for c in (2 * tt, 2 * tt + 1):
                    stats_chunk(sth, xs(nt, c, 0, 512), ("X", nt, c), c)
            late = []
            for half in range(2):
                b = nb()
                for ci in range(4):
                    c = half * 4 + ci
                    S.add("pe", lambda e, b=b, ci=ci, c=c, tt=tt: e.transpose(PS[:, b, ci * 128:(ci + 1) * 128], xs(ti, c, tt * 128, (tt + 1) * 128), IDENT[:, :]),
                          reads=[("X", ti, c), "IDENT"], excl=[pk(b)])
                ev = (lambda half=half, b=b: evac(YST[:, half * 512:(half + 1) * 512], PS[:, b, :], [], [("YST", half)], [pk(b)]))
                if nt is not None and tt == 3:
                    late.append(ev)
                else:
                    ev()
            if nt is not None:
                stats_flush(sth, 0)
                if tt == 3:
                    l1_norm1(nt, 1, sth)
                    hoisted.add(nt)
                    for ev in late:
                        ev()
            r0 = ti * 512 + tt * 128
            S.add("pool", lambda e, r0=r0: e.dma_start(out=y_d[r0:r0 + 128, :], in_=YST[:, :]), reads=[("YST", 0), ("YST", 1)], dma_key="st_y")

    for ti in range(4):
        if ti == 1:
            tick["on"] = True
        stx = layer0(ti)
        kv_stage(ti, stx)
    tick["on"] = False
    while jobs_l1:
        jobs_l1.pop(0)()
    S.add("pool", lambda e: e.collective_compute("AllGather", ALU.bypass, replica_groups=[[0, 1], [2, 3], [4, 5], [6, 7]],
                                                 ins=[kvx_in.ap().opt()], outs=[kvx_out.ap().opt()]),
          reads=["kvx_in"], writes=["kvx_out"], dma_key="cc", inc=1)
    stx = layer0(4)
    kv_stage(4, stx)
    attention(4, hoist_next=0)
    S.barrier()
    for r in range(2):
        for kvh in range(2):
            S.add("sp", lambda e, r=r, kvh=kvh: e.dma_start(out=KT[:, kvh, 512 + r * 2048:512 + (r + 1) * 2048],
                                                          in_=kvx_out[r * 128:(r + 1) * 128, kvh * 2048:(kvh + 1) * 2048]),
                  reads=["kvx_out"], writes=["KT"], dma_key=("ld_kt", r, kvh))
        S.add("sp", lambda e, r=r: e.dma_start(out=V[:, 4 + r * 16:4 + (r + 1) * 16, :],
                                               in_=kvx_out[r * 128:(r + 1) * 128, 4096:8192].rearrange("p (t c) -> p t c", t=16)),
              reads=["kvx_out"], writes=["V"], dma_key=("ld_v", r))
    S.add("pool", lambda e: e.dma_start(out=V[:, 0:4, :], in_=cv_d.ap().rearrange("(t p) c -> p t c", p=128)), writes=["V"], dma_key="ld_cv")
    CK2 = YST
    S.add("sp", lambda e: e.dma_start(out=CK2[:, :].rearrange("p (t c) -> p t c", t=4), in_=ck_d.ap().rearrange("(t p) c -> p t c", p=128)),
          writes=[("YST", 0), ("YST", 1)], dma_key="ld_ck2")
    for kvh in range(2):
        b = nb()
        for tt in range(4):
            S.add("pe", lambda e, b=b, tt=tt, kvh=kvh: e.transpose(PS[:, b, tt * 128:(tt + 1) * 128],
                                                                 CK2[:, tt * 256 + kvh * 128:tt * 256 + (kvh + 1) * 128], IDENT[:, :]),
                  reads=[("YST", 0), ("YST", 1), "IDENT"], excl=[pk(b)])
        S.add("act", lambda e, b=b, kvh=kvh: e.copy(out=KT[:, kvh, 0:512], in_=PS[:, b, :]), writes=["KT"], excl=[pk(b)])
    for ti in range(4):
        attention(ti, hoist_next=(ti + 1 if ti < 3 else None))
    S.emit()
    return nc


_CACHE = {}


def _rope_tables(L, grid_w=64, theta=10000.0, axis_dim=64):
    rows = L // grid_w
    r = np.repeat(np.arange(rows), grid_w).astype(np.float32)
    col = np.tile(np.arange(grid_w), rows).astype(np.float32)
    inv = (np.float32(theta) ** (-np.arange(0, axis_dim, 2, dtype=np.float32) / np.float32(axis_dim))).astype(np.float32)
    ar = r[:, None] * inv
    ac = col[:, None] * inv
    ang = np.concatenate([ar, ar, ac, ac], axis=-1)
    return np.cos(ang).astype(np.float32), np.sin(ang).astype(np.float32)


def kernel(x_prompt, x_sample, cache_k, cache_v, c, c_ctx, w_ada, b_ada, norm_g,
           w_in_even, pool_w, pool_scale, conv_w, conv_b, w_out_even,
           w_qkv, q_gain, k_gain, w_o, w_ffn_in, w_ffn_out, final_g):
    f = lambda a: np.ascontiguousarray(np.asarray(a, dtype=np.float32))
    x_prompt, x_sample, cache_k, cache_v, c, c_ctx = map(f, (x_prompt, x_sample, cache_k, cache_v, c, c_ctx))
    w_ada, b_ada, norm_g, w_in_even, pool_w, pool_scale = map(f, (w_ada, b_ada, norm_g, w_in_even, pool_w, pool_scale))
    conv_w, conv_b, w_out_even, w_qkv, q_gain, k_gain = map(f, (conv_w, conv_b, w_out_even, w_qkv, q_gain, k_gain))
    w_o, w_ffn_in, w_ffn_out, final_g = map(f, (w_o, w_ffn_in, w_ffn_out, final_g))
    if "nc" not in _CACHE:
        _CACHE["nc"] = build_program()
    nc = _CACHE["nc"]

    ident = np.eye(128, dtype=np.float32)
    R = np.zeros((128, 128), np.float32)
    for d in range(128):
        if (d % 64) < 32:
            R[d, d + 32] = -1.0
        else:
            R[d, d - 32] = 1.0
    consts0 = np.concatenate([ident, np.ascontiguousarray(R.T)], axis=1)
    wada_flat = [w_ada[0], w_ada[1]]
    cos, sin = _rope_tables(4096)

    def fm(vec):
        return np.ascontiguousarray(vec.reshape(-1, 128).T)

    in_maps = []
    for core in range(8):
        b = core // 2
        s0 = (core % 2) * 2048
        xt = np.concatenate([x_sample[b, s0:s0 + 2048], x_prompt[2 * core:2 * core + 2].reshape(512, D)], axis=0)
        xh = np.zeros((64, D), np.float32)
        mh = np.zeros((4, 16), np.float32)
        for j in range(4):
            a = s0 + 512 * j
            if a - 8 >= 0:
                xh[j * 16:j * 16 + 8] = x_sample[b, a - 8:a]
                mh[j, 0:8] = 1.0
            if a + 520 <= 4096:
                xh[j * 16 + 8:j * 16 + 16] = x_sample[b, a + 512:a + 520]
                mh[j, 8:16] = 1.0
        rce = np.zeros((5, 4, 2, 16), np.float32)
        for ti in range(5):
            for g in range(4):
                w = 2 << g
                for si in range(2):
                    if ti < 4:
                        L, t0, seglen = 4096, s0 + 512 * ti, 512
                    else:
                        L, t0, seglen = 256, 0, 256
                    cols = np.concatenate([np.arange(0, 8), np.arange(seglen - 8, seglen)]) + t0
                    lo = np.clip(cols - w // 2, 0, L)
                    hi = np.clip(cols + w // 2, 0, L)
                    rce[ti, g, si] = (np.float32(1.0) / (hi - lo).astype(np.float32))
        cvec = np.stack([c_ctx, c[b]], axis=0)
        cvec5 = np.concatenate([c_ctx[None, :], c], axis=0)
        sel = np.zeros((128, 2), np.float32)
        sel[0, 0] = 1.0
        sel[1 + b, 1] = 1.0
        consts = np.ascontiguousarray(np.concatenate([consts0, sel], axis=1))
        wada_sh = wada_flat[core % 2]
        prm = np.zeros((128, NP), np.float32)
        prm[:, P_CVEC:P_CVEC + 16] = cvec.reshape(2, 8, 128).transpose(2, 1, 0).reshape(128, 16)
        prm[:, P_BADA:P_BADA + 96] = b_ada.reshape(2, 48, 128).transpose(2, 0, 1).reshape(128, 96)
        prm[:, P_NG:P_NG + 32] = norm_g.reshape(2, 2, 8, 128).transpose(3, 0, 1, 2).reshape(128, 32)
        prm[:, P_PSC:P_PSC + 4] = fm(pool_scale[0])
        prm[:, P_CW:P_CW + 12] = conv_w[0].reshape(3, 4, 128).transpose(2, 0, 1).reshape(128, 12)
        prm[:, P_CB:P_CB + 4] = fm(conv_b[0])
        prm[:, P_QG] = q_gain[0]
        prm[:, P_KG] = k_gain[0]
        prm[:, P_FG:P_FG + 8] = fm(final_g)
        prm[:, P_CV5:P_CV5 + 40] = cvec5.reshape(5, 8, 128).transpose(2, 1, 0).reshape(128, 40)
        cs = np.zeros((128, 4, 2, 512), np.float32)
        for j in range(4):
            cs[:, j, 0, :] = cos[s0 + 512 * j:s0 + 512 * (j + 1)].T
            cs[:, j, 1, :] = sin[s0 + 512 * j:s0 + 512 * (j + 1)].T
        in_maps.append({
            "xt": np.ascontiguousarray(xt), "xh": xh, "params": prm, "consts": consts,
            "mh": np.ascontiguousarray(np.broadcast_to(mh.reshape(1, 64), (128, 64))),
            "rce": np.ascontiguousarray(np.broadcast_to(rce.reshape(1, 640), (128, 640))),
            "cs": cs,
            "ck": np.ascontiguousarray(cache_k[b, 0].reshape(512, 256)),
            "cv": np.ascontiguousarray(cache_v[b, 0].reshape(512, 256)),
            "wada_sh": wada_sh, "w_in_even": w_in_even[0], "pool_w": pool_w[0], "w_out_even": w_out_even[0],
            "w_qkv": w_qkv[0], "w_o": w_o[0], "w_ffn_in": w_ffn_in, "w_ffn_out": w_ffn_out,
        })
    res = run_bass_kernel_spmd(nc, in_maps, core_ids=list(range(8)))
    y_prompt = np.zeros((16, 256, D), np.float32)
    y_sample = np.zeros((4, 4096, D), np.float32)
    nk = np.zeros((16, 1, 256, 2, 128), np.float32)
    nv = np.zeros((16, 1, 256, 2, 128), np.float32)
    for core in range(8):
        r = res.results[core]
        b = core // 2
        s0 = (core % 2) * 2048
        y = np.asarray(r["y"], dtype=np.float32)
        y_sample[b, s0:s0 + 2048] = y[0:2048]
        y_prompt[2 * core:2 * core + 2] = y[2048:2560].reshape(2, 256, D)
        nk[2 * core:2 * core + 2, 0] = np.asarray(r["nk"], dtype=np.float32).reshape(2, 256, 2, 128)
        nv[2 * core:2 * core + 2, 0] = np.asarray(r["nv"], dtype=np.float32).reshape(2, 256, 2, 128)
    return (y_prompt, y_sample, nk, nv)
```
